# Optimizing a Trainium2 kernel written in Bass

```python
import math
import jax, jax.numpy as jnp
from jax import lax
import numpy as np

D_MODEL = 1024
BATCH = 8
SEQ = 2048
DEPTH = 1
DEC_BATCH = 128
DEC_SEQ = 8
PAST_LEN = 16384
PAGE_SIZE = 128

N_META = 16
SSM_WIDTH = D_MODEL
SSM_GROUP = 16
SSM_GROUPS = SSM_WIDTH // SSM_GROUP
SSM_STATE = 64
CONV_WIDTH = D_MODEL
CONV_K = 31
D_FF = 4 * D_MODEL
IN_COLS = SSM_WIDTH + 2 * CONV_WIDTH + 2 * D_MODEL
ALPHA = (2.0 * DEPTH) ** 0.25
BETA = (8.0 * DEPTH) ** -0.25
LN_EPS = 1e-5

kernel_name = "hybrid_s5_conformer_gated_decoder_step"


def layer_norm(x, g, b):
    xf = x.astype(jnp.float32)
    mu = jnp.mean(xf, axis=-1, keepdims=True)
    xc = xf - mu
    var = jnp.mean(xc * xc, axis=-1, keepdims=True)
    y = xc * lax.rsqrt(var + LN_EPS) * g.astype(jnp.float32) + b.astype(jnp.float32)
    return y.astype(x.dtype)


def _complex_affine_combine(left, right):
    a1r, a1i, b1r, b1i = left
    a2r, a2i, b2r, b2i = right
    ar = a2r * a1r - a2i * a1i
    ai = a2r * a1i + a2i * a1r
    br = a2r * b1r - a2i * b1i + b2r
    bi = a2r * b1i + a2i * b1r + b2i
    return (ar, ai, br, bi)


def s5_scan(u, h0_re, h0_im, a_re, a_im, log_dt, b_re, b_im, c_re, c_im, d_skip):
    n, L, _ = u.shape
    a_re = a_re.astype(jnp.float32); a_im = a_im.astype(jnp.float32)
    b_re = b_re.astype(jnp.float32); b_im = b_im.astype(jnp.float32)
    c_re = c_re.astype(jnp.float32); c_im = c_im.astype(jnp.float32)
    ug = u.astype(jnp.float32).reshape(n, L, SSM_GROUPS, SSM_GROUP).transpose(1, 0, 2, 3)
    dt = jnp.exp(log_dt.astype(jnp.float32))[:, None]
    mag = jnp.exp(a_re * dt)
    abar_re = mag * jnp.cos(a_im * dt)
    abar_im = mag * jnp.sin(a_im * dt)
    den = a_re * a_re + a_im * a_im
    zr = abar_re - 1.0
    zi = abar_im
    f_re = (zr * a_re + zi * a_im) / den
    f_im = (zi * a_re - zr * a_im) / den
    bbar_re = f_re[..., None] * b_re - f_im[..., None] * b_im
    bbar_im = f_re[..., None] * b_im + f_im[..., None] * b_re
    bu_re = jnp.einsum('lngc,gpc->lngp', ug, bbar_re)
    bu_im = jnp.einsum('lngc,gpc->lngp', ug, bbar_im)
    h0r = h0_re.astype(jnp.float32); h0i = h0_im.astype(jnp.float32)
    bu_re = bu_re.at[0].add(abar_re * h0r - abar_im * h0i)
    bu_im = bu_im.at[0].add(abar_re * h0i + abar_im * h0r)
    a_r = jnp.broadcast_to(abar_re, (L, 1, SSM_GROUPS, SSM_STATE))
    a_i = jnp.broadcast_to(abar_im, (L, 1, SSM_GROUPS, SSM_STATE))
    _, _, h_re, h_im = lax.associative_scan(_complex_affine_combine, (a_r, a_i, bu_re, bu_im), axis=0)
    y = (jnp.einsum('lngp,gcp->lngc', h_re, c_re) - jnp.einsum('lngp,gcp->lngc', h_im, c_im)
         + d_skip.astype(jnp.float32).reshape(SSM_GROUPS, SSM_GROUP) * ug)
    y = y.transpose(1, 0, 2, 3).reshape(n, L, SSM_WIDTH).astype(u.dtype)
    return y, h_re[-1], h_im[-1]


def causal_depthwise_conv(hist, w, b):
    out = lax.conv_general_dilated(hist, w[:, None, :].astype(hist.dtype), window_strides=(1,),
                                   padding='VALID', dimension_numbers=('NWC', 'WIO', 'NWC'),
                                   feature_group_count=CONV_WIDTH)
    return out + b


def trunk_layer(x, h0_re, h0_im, conv_buf, p):
    proj = x @ p['w_in'] + p['b_in']
    o1 = SSM_WIDTH
    o2 = o1 + 2 * CONV_WIDTH
    o3 = o2 + D_MODEL
    u = proj[..., :o1]
    cg = proj[..., o1:o2]
    gate_a = jax.nn.sigmoid(proj[..., o2:o3])
    gate_b = jax.nn.sigmoid(proj[..., o3:])
    y, h_re, h_im = s5_scan(u, h0_re, h0_im, p['ssm_a_re'], p['ssm_a_im'], p['ssm_log_dt'],
                            p['ssm_b_re'], p['ssm_b_im'], p['ssm_c_re'], p['ssm_c_im'], p['ssm_d'])
    z = jax.nn.gelu(y)
    za = z * jax.nn.sigmoid(z @ p['w_glu'] + p['b_glu'])
    pa = za @ p['w_a_out']
    glu = cg[..., :CONV_WIDTH] * jax.nn.sigmoid(cg[..., CONV_WIDTH:])
    hist = jnp.concatenate([conv_buf.astype(glu.dtype), glu], axis=1)
    new_buf = hist[:, -(CONV_K - 1):]
    cv = causal_depthwise_conv(hist, p['conv_w'], p['conv_b'])
    cv = jax.nn.silu(layer_norm(cv, p['conv_ln_g'], p['conv_ln_b']))
    pb = cv @ p['w_b_out'] + p['b_b_out']
    mix = (gate_a * pa + gate_b * pb) @ p['w_o'] + p['b_o']
    x = layer_norm(ALPHA * x + mix, p['ln1_g'], p['ln1_b'])
    hdn = jnp.square(jax.nn.relu(x @ p['w_ff1'] + p['b_ff1']))
    ff = hdn @ p['w_ff2'] + p['b_ff2']
    x = layer_norm(ALPHA * x + ff, p['ln2_g'], p['ln2_b'])
    return x, h_re, h_im, new_buf


def setup_inputs(seed: int = 0) -> dict:
    key = jax.random.key(seed)
    ks = iter(jax.random.split(key, 48))
    f32 = jnp.float32

    def nrm(shape, scale):
        return jax.random.normal(next(ks), shape, f32) * scale

    def gain(shape):
        return 1.0 + nrm(shape, 0.02)

    L = DEPTH
    n_idx = jnp.arange(SSM_STATE, dtype=f32)
    inputs = {}
    inputs['x_prompt'] = nrm((BATCH, SEQ, D_MODEL), 1.0)
    inputs['x_sample'] = nrm((DEC_BATCH, DEC_SEQ, D_MODEL), 1.0)
    inputs['state_ssm_re'] = nrm((L, DEC_BATCH, SSM_GROUPS, SSM_STATE), 0.5)
    inputs['state_ssm_im'] = nrm((L, DEC_BATCH, SSM_GROUPS, SSM_STATE), 0.5)
    inputs['state_conv'] = nrm((L, DEC_BATCH, CONV_K - 1, CONV_WIDTH), 0.5)
    inputs['meta_tokens'] = nrm((N_META, D_MODEL), 1.0)
    inputs['ln_in_g'] = gain((D_MODEL,))
    inputs['ln_in_b'] = nrm((D_MODEL,), 0.02)
    inputs['w_in'] = nrm((L, D_MODEL, IN_COLS), D_MODEL ** -0.5)
    inputs['b_in'] = nrm((L, IN_COLS), 0.02)
    inputs['ssm_a_re'] = -0.5 * (1.0 + nrm((L, SSM_GROUPS, SSM_STATE), 0.01))
    inputs['ssm_a_im'] = jnp.broadcast_to(math.pi * n_idx, (L, SSM_GROUPS, SSM_STATE)) + nrm((L, SSM_GROUPS, SSM_STATE), 0.01)
    inputs['ssm_log_dt'] = jax.random.uniform(next(ks), (L, SSM_GROUPS), f32,
                                              minval=math.log(1e-3), maxval=math.log(1e-1))
    inputs['ssm_b_re'] = nrm((L, SSM_GROUPS, SSM_STATE, SSM_GROUP), (2.0 * SSM_GROUP) ** -0.5)
    inputs['ssm_b_im'] = nrm((L, SSM_GROUPS, SSM_STATE, SSM_GROUP), (2.0 * SSM_GROUP) ** -0.5)
    inputs['ssm_c_re'] = nrm((L, SSM_GROUPS, SSM_GROUP, SSM_STATE), (2.0 * SSM_STATE) ** -0.5)
    inputs['ssm_c_im'] = nrm((L, SSM_GROUPS, SSM_GROUP, SSM_STATE), (2.0 * SSM_STATE) ** -0.5)
    inputs['ssm_d'] = nrm((L, SSM_WIDTH), 1.0)
    inputs['w_glu'] = nrm((L, SSM_WIDTH, SSM_WIDTH), SSM_WIDTH ** -0.5)
    inputs['b_glu'] = nrm((L, SSM_WIDTH), 0.02)
    inputs['w_a_out'] = nrm((L, SSM_WIDTH, D_MODEL), SSM_WIDTH ** -0.5)
    inputs['conv_w'] = nrm((L, CONV_K, CONV_WIDTH), CONV_K ** -0.5)
    inputs['conv_b'] = nrm((L, CONV_WIDTH), 0.02)
    inputs['conv_ln_g'] = gain((L, CONV_WIDTH))
    inputs['conv_ln_b'] = nrm((L, CONV_WIDTH), 0.02)
    inputs['w_b_out'] = nrm((L, CONV_WIDTH, D_MODEL), CONV_WIDTH ** -0.5)
    inputs['b_b_out'] = nrm((L, D_MODEL), 0.02)
    inputs['w_o'] = nrm((L, D_MODEL, D_MODEL), BETA * D_MODEL ** -0.5)
    inputs['b_o'] = nrm((L, D_MODEL), 0.02)
    inputs['ln1_g'] = gain((L, D_MODEL))
    inputs['ln1_b'] = nrm((L, D_MODEL), 0.02)
    inputs['w_ff1'] = nrm((L, D_MODEL, D_FF), D_MODEL ** -0.5)
    inputs['b_ff1'] = nrm((L, D_FF), 0.02)
    inputs['w_ff2'] = nrm((L, D_FF, D_MODEL), BETA * D_FF ** -0.5)
    inputs['b_ff2'] = nrm((L, D_MODEL), 0.02)
    inputs['ln2_g'] = gain((L, D_MODEL))
    inputs['ln2_b'] = nrm((L, D_MODEL), 0.02)
    return inputs


def reference(x_prompt, x_sample, state_ssm_re, state_ssm_im, state_conv, meta_tokens, ln_in_g, ln_in_b,
              w_in, b_in, ssm_a_re, ssm_a_im, ssm_log_dt, ssm_b_re, ssm_b_im, ssm_c_re, ssm_c_im, ssm_d,
              w_glu, b_glu, w_a_out, conv_w, conv_b, conv_ln_g, conv_ln_b, w_b_out, b_b_out, w_o, b_o,
              ln1_g, ln1_b, w_ff1, b_ff1, w_ff2, b_ff2, ln2_g, ln2_b):
    nb = x_prompt.shape[0]
    meta = jnp.broadcast_to(meta_tokens.astype(x_prompt.dtype)[None], (nb, N_META, D_MODEL))
    xp = layer_norm(jnp.concatenate([meta, x_prompt], axis=1), ln_in_g, ln_in_b)
    xs = layer_norm(x_sample, ln_in_g, ln_in_b)
    zero_h = jnp.zeros((nb, SSM_GROUPS, SSM_STATE), jnp.float32)
    zero_buf = jnp.zeros((nb, CONV_K - 1, CONV_WIDTH), xp.dtype)
    p_re, p_im, p_cv, s_re, s_im, s_cv = [], [], [], [], [], []
    for l in range(DEPTH):
        p = dict(w_in=w_in[l], b_in=b_in[l], ssm_a_re=ssm_a_re[l], ssm_a_im=ssm_a_im[l],
                 ssm_log_dt=ssm_log_dt[l], ssm_b_re=ssm_b_re[l], ssm_b_im=ssm_b_im[l],
                 ssm_c_re=ssm_c_re[l], ssm_c_im=ssm_c_im[l], ssm_d=ssm_d[l], w_glu=w_glu[l],
                 b_glu=b_glu[l], w_a_out=w_a_out[l], conv_w=conv_w[l], conv_b=conv_b[l],
                 conv_ln_g=conv_ln_g[l], conv_ln_b=conv_ln_b[l], w_b_out=w_b_out[l],
                 b_b_out=b_b_out[l], w_o=w_o[l], b_o=b_o[l], ln1_g=ln1_g[l], ln1_b=ln1_b[l],
                 w_ff1=w_ff1[l], b_ff1=b_ff1[l], w_ff2=w_ff2[l], b_ff2=b_ff2[l],
                 ln2_g=ln2_g[l], ln2_b=ln2_b[l])
        xp, hpr, hpi, bp = trunk_layer(xp, zero_h, zero_h, zero_buf, p)
        xs, hsr, hsi, bs = trunk_layer(xs, state_ssm_re[l], state_ssm_im[l], state_conv[l], p)
        p_re.append(hpr); p_im.append(hpi); p_cv.append(bp)
        s_re.append(hsr); s_im.append(hsi); s_cv.append(bs)
    y_prompt = xp[:, N_META:]
    y_sample = xs
    new_ssm_re_prompt = jnp.stack(p_re)
    new_ssm_im_prompt = jnp.stack(p_im)
    new_conv_prompt = jnp.stack(p_cv)
    new_ssm_re_sample = jnp.stack(s_re)
    new_ssm_im_sample = jnp.stack(s_im)
    new_conv_sample = jnp.stack(s_cv)
    return (y_prompt, y_sample, new_ssm_re_prompt, new_ssm_im_prompt, new_conv_prompt,
            new_ssm_re_sample, new_ssm_im_sample, new_conv_sample)
```

```python
import math
import numpy as np
import concourse.bass as bass
import concourse.mybir as mybir
from concourse.bass_utils import run_bass_kernel_spmd

F32 = mybir.dt.float32
BF16 = mybir.dt.bfloat16
I32 = mybir.dt.int32
AF = mybir.ActivationFunctionType
ALU = mybir.AluOpType

D = 1024
NCH = 8
DFF = 4096
HP = 30
TC = 8
ALPHA = 2.0 ** 0.25
EPS = 1e-5
NTMAX = 512
NKMAX = NTMAX // TC
SEQ = 2048
NMETA = 16
NSAMP_TOK = 128
NSEQ = 16

VEC_SPECS = [("b_in", 40), ("b_glu", 8), ("b_b_out", 8), ("b_o", 8), ("b_ff1", 32), ("b_ff2", 8),
             ("ln_in_g", 8), ("ln_in_b", 8), ("conv_b", 8), ("conv_ln_g", 8), ("conv_ln_b", 8),
             ("ln1_g", 8), ("ln1_b", 8), ("ln2_g", 8), ("ln2_b", 8), ("ssm_d", 8), ("conv_w", 31 * 8)]
VOFF = {}
_o = 0
for _n, _c in VEC_SPECS:
    VOFF[_n] = _o
    _o += _c
NV = _o


class Ev:
    __slots__ = ("sem", "val")

    def __init__(self, sem, val):
        self.sem = sem
        self.val = val


class Eng:
    def __init__(self, K, raw, name):
        self.K = K
        self.raw = raw
        self.name = name
        self.nsem = 0
        self.seen = {}
        self._new_sem()
        K.engs.append(self)

    def _new_sem(self):
        self.sem = self.K.nc.alloc_semaphore(f"s_{self.name}_{self.nsem}")
        self.nsem += 1
        self.cnt = 0

    def wait(self, *evs):
        for ev in evs:
            if ev is None:
                continue
            if isinstance(ev, (list, tuple)):
                self.wait(*ev)
                continue
            k = ev.sem.num if hasattr(ev.sem, "num") else id(ev.sem)
            if self.seen.get(k, 0) >= ev.val:
                continue
            self.raw.wait_ge(ev.sem, ev.val)
            self.seen[k] = ev.val

    def mark(self, ins):
        if self.cnt >= 6000:
            self._new_sem()
        self.cnt += 1
        ins.then_inc(self.sem, 1)
        return Ev(self.sem, self.cnt)


class StopBuild(Exception):
    pass


class DmaSem:
    def __init__(self, K, name):
        self.sem = K.nc.alloc_semaphore(name)
        self.cnt = 0
        K.dmasems.append(self)

    def add(self, ins):
        self.cnt += 16
        ins.then_inc(self.sem, 16)
        return Ev(self.sem, self.cnt)


class Kern:
    def __init__(self, debug=None):
        self.debug = debug or {}
        self.nc = bass.Bass("TRN2", target_bir_lowering=False)
        self.ctx = []
        self.dbg_outs = {}
        self.dmasems = []
        self.engs = []
        self.pe_n = 0
        self.phase_log = []

    def MM(self, *a, **kw):
        self.pe_n += 1
        return self.nc.tensor.matmul(*a, **kw)

    def TR(self, *a, **kw):
        self.pe_n += 1
        return self.nc.tensor.transpose(*a, **kw)

    def sb(self, name, shape, dt):
        g = self.nc.sbuf_tensor(name, list(shape), dt)
        t = g.__enter__()
        self.ctx.append(g)
        return t

    def push_scope(self):
        self.ctx.append("SCOPE")

    def pop_scope(self):
        while True:
            g = self.ctx.pop()
            if g == "SCOPE":
                break
            g.__exit__(None, None, None)

    def close(self):
        for g in reversed(self.ctx):
            if g != "SCOPE":
                g.__exit__(None, None, None)
        self.ctx = []

    def din(self, name, shape, dt=F32):
        return self.nc.dram_tensor(name, list(shape), dt, kind="ExternalInput").ap()

    def dout(self, name, shape, dt=F32):
        return self.nc.dram_tensor(name, list(shape), dt, kind="ExternalOutput").ap()

    def dscr(self, name, shape, dt):
        return self.nc.dram_tensor(name, list(shape), dt).ap()


def view(t, off, dims):
    full = t[:]
    pstep = full.ap[0][0]
    npart = full.ap[0][1]
    return bass.AP(full.tensor, off, [[pstep, npart]] + [[s, c] for s, c in dims])


def build(debug=None):
    K = Kern(debug)
    nc = K.nc
    dbg = K.debug

    xp = K.din("xp", [SEQ, D])
    xs = K.din("xs", [NSAMP_TOK, D])
    meta = K.din("meta", [NMETA, D])
    vecs_d = K.din("vecs", [128, NV])
    ssm_small = K.din("ssm_small", [128, 96])
    bre_d = K.din("bre", [128, 512])
    bim_d = K.din("bim", [128, 512])
    cre_d = K.din("cre", [128, 512])
    cim_d = K.din("cim", [128, 512])
    h0re_d = K.din("h0re", [128, 512])
    h0im_d = K.din("h0im", [128, 512])
    sconv_fm = K.din("sconv_fm", [128, NCH * NSEQ * HP])
    sconv_nat = K.din("sconv_nat", [NSEQ * HP, D])
    w_in_d = K.din("w_in", [D, 5 * D])
    w_glu_d = K.din("w_glu", [D, D])
    w_aout_d = K.din("w_a_out", [D, D])
    w_bout_d = K.din("w_b_out", [D, D])
    w_o_d = K.din("w_o", [D, D])
    w_ff1_d = K.din("w_ff1", [D, DFF])
    w_ff2_d = K.din("w_ff2", [DFF, D])

    yp = K.dout("yp", [SEQ, D])
    ys = K.dout("ys", [NSAMP_TOK, D])
    nre_p = K.dout("nre_p", [32, 128])
    nim_p = K.dout("nim_p", [32, 128])
    ncv_p = K.dout("ncv_p", [HP, D])
    nre_s = K.dout("nre_s", [NSEQ * 32, 128])
    nim_s = K.dout("nim_s", [NSEQ * 32, 128])
    ncv_s = K.dout("ncv_s", [NSEQ * HP, D])

    wsc = {
        "in": K.dscr("wsc_in", [D, 5 * D], BF16),
        "glu": K.dscr("wsc_glu", [D, D], BF16),
        "aout": K.dscr("wsc_aout", [D, D], BF16),
        "bout": K.dscr("wsc_bout", [D, D], BF16),
        "o": K.dscr("wsc_o", [D, D], BF16),
        "ff1": K.dscr("wsc_ff1", [D, DFF], BF16),
        "ff2": K.dscr("wsc_ff2", [DFF, D], BF16),
    }
    wsrc = {"in": w_in_d, "glu": w_glu_d, "aout": w_aout_d, "bout": w_bout_d, "o": w_o_d,
            "ff1": w_ff1_d, "ff2": w_ff2_d}
    dsc = K.dscr("dsc", [NCH, 128, 31 * 128], BF16)
    winsc = K.dscr("winsc", [128, 8 * 8 * 2 * 128], BF16)
    cssc = K.dscr("cssc", [128, 32 * 8 * 2 * 32], BF16)

    PE = Eng(K, nc.tensor, "pe")
    ACT = Eng(K, nc.scalar, "act")
    DVE = Eng(K, nc.vector, "dve")
    POOL = Eng(K, nc.gpsimd, "pool")
    SP = Eng(K, nc.sync, "sp")

    IDF = K.sb("IDF", [128, 128], F32)
    IDB = K.sb("IDB", [128, 128], BF16)
    ONESB = K.sb("ONESB", [128, 128], BF16)
    NHALF = K.sb("NHALF", [128, 1], F32)
    VECS = K.sb("VECS", [128, NV], F32)
    DERV = K.sb("DERV", [128, 4, 8], F32)
    ARt = K.sb("ARt", [128, 32, 2], F32)
    AI2t = K.sb("AI2t", [128, 32, 2], F32)
    HCAR = K.sb("HCAR", [128, 32, 2], F32)
    KB = K.sb("KB", [128, 8, 8, 128], BF16)
    WT = K.sb("WT", [128, 2, 8, 2, 128], BF16)
    PSUM_g = nc.psum_tensor("PS", [128, 8, 512], F32)
    PS = PSUM_g.__enter__()
    K.ctx.append(PSUM_g)

    def V(name, k=None):
        o = VOFF[name]
        if k is None:
            return o
        return VECS[:, o + k:o + k + 1]

    class Banks:
        def __init__(self):
            self.nxt = 0
            self.free = [[] for _ in range(8)]
            self.held = set()

        def alloc(self, hold=False):
            b = self.nxt
            while b in self.held:
                b = (b + 1) % 8
            self.nxt = (b + 1) % 8
            PE.wait(self.free[b])
            self.free[b] = []
            if hold:
                self.held.add(b)
            return b

        def release(self, b, *evs):
            self.free[b].extend(evs)
            self.held.discard(b)

    BK = Banks()

    pl = DmaSem(K, "pl")
    outs = DmaSem(K, "outs")
    scr = DmaSem(K, "scr")

    conv_ev = {}

    def conv_dma(key, name, rows, cols):
        s = DmaSem(K, f"cv_{key}")
        ins = nc.gpsimd.dma_start(out=wsc[name][rows[0]:rows[1], cols[0]:cols[1]],
                                  in_=wsrc[name][rows[0]:rows[1], cols[0]:cols[1]])
        conv_ev[key] = s.add(ins)

    ld = []
    ld.append(pl.add(nc.sync.dma_start(out=VECS[:], in_=vecs_d)))
    K.push_scope()
    SSMP = K.sb("SSMP", [128, 96], F32)
    BRE = K.sb("BRE", [128, 32, 16], F32)
    BIM = K.sb("BIM", [128, 32, 16], F32)
    CRE = K.sb("CRE", [128, 32, 16], F32)
    CIM = K.sb("CIM", [128, 32, 16], F32)
    pl.add(nc.sync.dma_start(out=SSMP[:], in_=ssm_small))
    pl.add(nc.sync.dma_start(out=BRE[:], in_=bre_d.rearrange("p (q c) -> p q c", c=16)))
    pl.add(nc.sync.dma_start(out=BIM[:], in_=bim_d.rearrange("p (q c) -> p q c", c=16)))
    pl.add(nc.sync.dma_start(out=CRE[:], in_=cre_d.rearrange("p (q c) -> p q c", c=16)))
    ev_pl = pl.add(nc.sync.dma_start(out=CIM[:], in_=cim_d.rearrange("p (q c) -> p q c", c=16)))

    conv_dma("in0", "in", (0, D), (0, 1024))
    conv_dma("in1", "in", (0, D), (1024, 2048))
    conv_dma("in2", "in", (0, D), (2048, 3072))
    conv_dma("in4", "in", (0, D), (4096, 5120))
    conv_dma("in3", "in", (0, D), (3072, 4096))
    conv_dma("bout", "bout", (0, D), (0, D))
    conv_dma("glu", "glu", (0, D), (0, D))
    conv_dma("aout", "aout", (0, D), (0, D))
    conv_dma("o", "o", (0, D), (0, D))

    IDX = K.sb("IDX", [128, 128], I32)
    POOL.raw.iota(IDX[:], pattern=[[1, 128]], base=0, channel_multiplier=-1)
    POOL.raw.memset(NHALF[:], -0.5)
    e_pc = POOL.mark(POOL.raw.memset(ONESB[:], 1.0 / 1024.0))
    DVE.wait(e_pc)
    DVE.raw.tensor_scalar(out=IDF[:], in0=IDX[:], scalar1=0.0, scalar2=None, op0=ALU.is_equal)
    e_id = DVE.mark(DVE.raw.tensor_scalar(out=IDB[:], in0=IDX[:], scalar1=0.0, scalar2=None, op0=ALU.is_equal))

    DVE.wait(ev_pl)
    g_in = VECS[:, V("ln_in_g"):V("ln_in_g") + 8]
    b_in_ln = VECS[:, V("ln_in_b"):V("ln_in_b") + 8]
    DVE.raw.tensor_scalar(out=DERV[:, 0, :], in0=g_in, scalar1=ALPHA, scalar2=None, op0=ALU.mult)
    DVE.raw.scalar_tensor_tensor(out=DERV[:, 1, :], in0=b_in_ln, scalar=ALPHA,
                                 in1=VECS[:, V("b_o"):V("b_o") + 8], op0=ALU.mult, op1=ALU.add)
    DVE.raw.tensor_scalar(out=DERV[:, 2, :], in0=VECS[:, V("ln1_g"):V("ln1_g") + 8], scalar1=ALPHA,
                          scalar2=None, op0=ALU.mult)
    e_derv = DVE.mark(DVE.raw.scalar_tensor_tensor(
        out=DERV[:, 3, :], in0=VECS[:, V("ln1_b"):V("ln1_b") + 8], scalar=ALPHA,
        in1=VECS[:, V("b_ff2"):V("b_ff2") + 8], op0=ALU.mult, op1=ALU.add))

    SM = K.sb("SM", [128, 24, 32], F32)
    SMI = K.sb("SMI", [128, 2, 32], I32)
    PW = K.sb("PW", [128, 9, 2, 32], F32)
    are = SSMP[:, 0:32]
    aim = SSMP[:, 32:64]
    ldt = SSMP[:, 64:96]
    (DT, ZR, TH, MAG, GS, GC, FS, FCc, SINT, COST, ABR, ABI, DEN, RDEN, ZR1, FR, FI, T1, T2, T3) = \
        [SM[:, i, :] for i in range(20)]

    wpend, rpend = {}, {}

    def _k(ap):
        return (ap.tensor.name, int(ap.offset))

    def dvl(make, out, reads):
        need = []
        for ap in reads:
            if _k(ap) in wpend:
                need.append(wpend[_k(ap)])
        ko = _k(out)
        if ko in wpend:
            need.append(wpend[ko])
        if ko in rpend:
            need.append(rpend[ko])
        DVE.wait(*need)
        e = DVE.mark(make())
        wpend[ko] = e
        for ap in reads:
            rpend[_k(ap)] = e
        return e

    def dv(ins):
        e = DVE.mark(ins)
        DVE.wait(e)
        return e

    def tt(out, a, b, op):
        return dvl(lambda: DVE.raw.tensor_tensor(out=out, in0=a, in1=b, op=op), out, [a, b])

    def ts(out, a, s1, op0, s2=None, op1=None):
        if op1 is None:
            return dvl(lambda: DVE.raw.tensor_scalar(out=out, in0=a, scalar1=s1, scalar2=None, op0=op0), out, [a])
        return dvl(lambda: DVE.raw.tensor_scalar(out=out, in0=a, scalar1=s1, scalar2=s2, op0=op0, op1=op1), out, [a])

    def cp(out, a):
        return dvl(lambda: DVE.raw.tensor_copy(out=out, in_=a), out, [a])

    POOL.wait(ev_pl)
    e = POOL.mark(POOL.raw.memset(T3, math.e))
    POOL.wait(e)
    e = POOL.mark(POOL.raw.tensor_tensor(out=DT, in0=T3, in1=ldt, op=ALU.pow))
    DVE.wait(e)
    tt(ZR, are, DT, ALU.mult)
    e_zr = tt(TH, aim, DT, ALU.mult)
    POOL.wait(e_zr)
    e_mag = POOL.mark(POOL.raw.tensor_tensor(out=MAG, in0=T3, in1=ZR, op=ALU.pow))
    INV2PI = 1.0 / (2.0 * math.pi)
    ts(GS, TH, INV2PI, ALU.mult)
    ts(GC, TH, INV2PI, ALU.mult, 0.25, ALU.add)

    def frac_center(dst, src, ii):
        cp(SMI[:, ii, :], src)
        cp(T1, SMI[:, ii, :])
        tt(dst, src, T1, ALU.subtract)
        ts(T2, dst, 0.5, ALU.is_gt)
        tt(dst, dst, T2, ALU.subtract)
        ts(T2, dst, -0.5, ALU.is_lt)
        return tt(dst, dst, T2, ALU.add)

    frac_center(FS, GS, 0)
    e_f = frac_center(FCc, GC, 1)
    ACT.wait(e_f)
    TWO_PI_SAFE = 6.283185
    ACT.raw.activation(out=SINT, in_=FS, func=AF.Sin, scale=TWO_PI_SAFE)
    e_sc = ACT.mark(ACT.raw.activation(out=COST, in_=FCc, func=AF.Sin, scale=TWO_PI_SAFE))
    DVE.wait(e_sc, e_mag)
    tt(ABR, MAG, COST, ALU.mult)
    tt(ABI, MAG, SINT, ALU.mult)
    tt(DEN, are, are, ALU.mult)
    tt(T1, aim, aim, ALU.mult)
    tt(DEN, DEN, T1, ALU.add)
    dvl(lambda: DVE.raw.reciprocal(out=RDEN, in_=DEN), RDEN, [DEN])
    ts(ZR1, ABR, -1.0, ALU.add)
    tt(T1, ZR1, are, ALU.mult)
    tt(T2, ABI, aim, ALU.mult)
    tt(T1, T1, T2, ALU.add)
    tt(FR, T1, RDEN, ALU.mult)
    tt(T1, ABI, are, ALU.mult)
    tt(T2, ZR1, aim, ALU.mult)
    tt(T1, T1, T2, ALU.subtract)
    tt(FI, T1, RDEN, ALU.mult)
    dvl(lambda: DVE.raw.memset(PW[:, 0, 0, :], 1.0), PW[:, 0, 0, :], [])
    dvl(lambda: DVE.raw.memset(PW[:, 0, 1, :], 0.0), PW[:, 0, 1, :], [])
    cp(PW[:, 1, 0, :], ABR)
    cp(PW[:, 1, 1, :], ABI)
    for s in range(1, 8):
        pr, pi = PW[:, s, 0, :], PW[:, s, 1, :]
        tt(T1, pr, ABR, ALU.mult)
        tt(T2, pi, ABI, ALU.mult)
        tt(PW[:, s + 1, 0, :], T1, T2, ALU.subtract)
        tt(T1, pr, ABI, ALU.mult)
        tt(T2, pi, ABR, ALU.mult)
        tt(PW[:, s + 1, 1, :], T1, T2, ALU.add)
    cp(ARt[:, :, 0], PW[:, 8, 0, :])
    cp(ARt[:, :, 1], PW[:, 8, 0, :])
    cp(AI2t[:, :, 0], PW[:, 8, 1, :])
    ts(AI2t[:, :, 1], PW[:, 8, 1, :], -1.0, ALU.mult)
    e_ar = dv(DVE.raw.memset(T3, 0.0))

    DG = K.sb("DG", [128, 2, 31, 128], BF16)
    dg_free = [None, None]
    dg_sem = [DmaSem(K, "dg0"), DmaSem(K, "dg1")]

    def emit_diag(c):
        slot = c % 2
        ACT.wait(dg_free[slot], e_id, ev_pl)
        for jt in range(31):
            col = V("conv_w") + jt * 8 + c
            ins = ACT.raw.activation(out=DG[:, slot, jt, :], in_=IDB[:], func=AF.Identity,
                                     scale=VECS[:, col:col + 1])
        e_dg = ACT.mark(ins)
        SP.wait(e_dg)
        dg_free[slot] = dg_sem[slot].add(nc.sync.dma_start(out=dsc[c], in_=DG[:, slot, :, :].rearrange("p t m -> p (t m)")))

    def bq(ap2d):
        return ap2d.unsqueeze(2).broadcast_to([128, 32, 16])

    BBR = K.sb("BBR", [128, 32, 16], F32)
    BBI = K.sb("BBI", [128, 32, 16], F32)
    TA = K.sb("TA", [128, 32, 16], F32)
    TB = K.sb("TB", [128, 32, 16], F32)
    tt(TA[:], BRE[:], bq(FR), ALU.mult)
    tt(TB[:], BIM[:], bq(FI), ALU.mult)
    tt(BBR[:], TA[:], TB[:], ALU.subtract)
    tt(TA[:], BIM[:], bq(FR), ALU.mult)
    tt(TB[:], BRE[:], bq(FI), ALU.mult)
    tt(BBI[:], TA[:], TB[:], ALU.add)

    WP = K.sb("WP", [128, 2, 32, 2, 16], BF16)
    BBP = K.sb("BBP", [128, 32, 2, 2, 16], BF16)
    CP0 = K.sb("CP0", [128, 32, 2, 2, 16], BF16)
    CSF = K.sb("CSF", [128, 8, 32, 2, 2, 16], BF16)
    POOL.raw.memset(BBP[:], 0.0)
    e_zb = POOL.mark(POOL.raw.memset(WP[:], 0.0))
    POOL.raw.memset(CP0[:], 0.0)
    POOL.raw.memset(HCAR[:], 0.0)
    POOL.raw.memset(KB[:], 0.0)
    e_z = POOL.mark(POOL.raw.memset(CSF[:], 0.0))
    DVE.wait(e_zb)

    def put_pad(dst_fn, src, negate=False):
        ks = _k(src[:])
        if ks in wpend:
            DVE.wait(wpend[ks])
        last = None
        for gm in range(2):
            sl = slice(64 * gm, 64 * gm + 64)
            if negate:
                last = DVE.raw.tensor_scalar(out=dst_fn(sl, gm), in0=src[sl], scalar1=-1.0, scalar2=None,
                                             op0=ALU.mult)
            else:
                last = DVE.raw.tensor_copy(out=dst_fn(sl, gm), in_=src[sl])
        e = DVE.mark(last)
        rpend[ks] = e
        return e

    put_pad(lambda sl, gm: BBP[sl, :, 0, gm, :], BBR)
    put_pad(lambda sl, gm: BBP[sl, :, 1, gm, :], BBI)

    cs_sem = DmaSem(K, "cssem")
    DVE.wait(e_z)
    put_pad(lambda sl, gm: CP0[sl, :, 0, gm, :], CRE)
    put_pad(lambda sl, gm: CP0[sl, :, 1, gm, :], CIM, negate=True)
    for tp in range(8):
        pr, pi = PW[:, tp + 1, 0, :], PW[:, tp + 1, 1, :]
        tt(TA[:], CRE[:], bq(pr), ALU.mult)
        tt(TB[:], CIM[:], bq(pi), ALU.mult)
        tt(TA[:], TA[:], TB[:], ALU.subtract)
        put_pad(lambda sl, gm: CSF[sl, tp, :, 0, gm, :], TA)
        tt(TA[:], CRE[:], bq(pi), ALU.mult)
        tt(TB[:], CIM[:], bq(pr), ALU.mult)
        tt(TA[:], TA[:], TB[:], ALU.add)
        e_cs = put_pad(lambda sl, gm: CSF[sl, tp, :, 1, gm, :], TA, negate=True)
        SP.wait(e_cs)
        ev_cssc = cs_sem.add(nc.sync.dma_start(
            out=cssc.rearrange("p (t x) -> p t x", t=8)[:, tp, :],
            in_=CSF[:, tp, :, :, :, :].rearrange("p q r g c -> p (q r g c)")))

    wt_free = [None, None]
    wt_sem = [DmaSem(K, "wt0"), DmaSem(K, "wt1")]
    for s in range(8):
        e_ = 7 - s
        pr, pi = PW[:, e_, 0, :], PW[:, e_, 1, :]
        PE_done_prev = None
        tt(TA[:], BBR[:], bq(pr), ALU.mult)
        tt(TB[:], BBI[:], bq(pi), ALU.mult)
        tt(TA[:], TA[:], TB[:], ALU.subtract)
        if s > 0:
            DVE.wait(e_wp_read)
        put_pad(lambda sl, gm: WP[sl, 0, :, gm, :], TA)
        tt(TA[:], BBR[:], bq(pi), ALU.mult)
        tt(TB[:], BBI[:], bq(pr), ALU.mult)
        tt(TA[:], TA[:], TB[:], ALU.add)
        e_wp = put_pad(lambda sl, gm: WP[sl, 1, :, gm, :], TA)
        PE.wait(e_wp)
        slot = s % 2
        bA = BK.alloc()
        bB = BK.alloc()
        for ri in range(2):
            bb = bA if ri == 0 else bB
            pb16 = PS[:, bb, :].bitcast(BF16)
            for j in range(8):
                ins = K.TR(
                    pb16[:, j * 128:(j + 1) * 128],
                    WP[:, ri, 4 * j:4 * j + 4, :, :].rearrange("p q g c -> p (q g c)"), IDB[:])
        e_wp_read = PE.mark(ins)
        ACT.wait(e_wp_read, wt_free[slot])
        for ri in range(2):
            bb = bA if ri == 0 else bB
            pb16 = PS[:, bb, :].bitcast(BF16)
            ins = ACT.raw.activation(out=WT[:, slot, :, ri, :],
                                     in_=pb16.rearrange("p (j m) -> p j m", m=128), func=AF.Copy)
        e_wt = ACT.mark(ins)
        BK.release(bA, e_wt)
        BK.release(bB, e_wt)
        SP.wait(e_wt)
        dst = winsc.rearrange("p (s j r m) -> p s j r m", j=8, s=8, r=2)[:, s, :, :, :]
        wt_free[slot] = wt_sem[slot].add(nc.sync.dma_start(out=dst, in_=WT[:, slot, :, :, :]))
        emit_diag(s)
    ev_winsc = wt_free[1]
    ev_winsc0 = wt_free[0]
    ev_dsc = [dg_free[0], dg_free[1]]

    PE.wait(e_cs, e_id)
    kb_evs = []
    for j in range(8):
        b = BK.alloc() if j % 2 == 0 else b
        base = (j % 2) * 256
        for qq in range(4):
            q = 4 * j + qq
            osl = PS[32 * qq:32 * qq + 32, b, base:base + 256]
            first = (j % 2 == 0)
            for ri in range(2):
                K.MM(osl[:, 0:32], lhsT=BBP[:, q, ri, :, :].rearrange("p g c -> p (g c)"),
                                 rhs=CP0[:, q, ri, :, :].rearrange("p g c -> p (g c)"),
                                 start=(first and ri == 0), stop=False, skip_group_check=True,
                                 tile_position=(0, 32 * qq))
            for ri in range(2):
                ins = K.MM(osl[:, 32:256].rearrange("p (t m) -> p t m", m=32),
                                       lhsT=BBP[:, q, ri, :, :].rearrange("p g c -> p (g c)"),
                                       rhs=CSF[:, 0:7, q, ri, :, :].rearrange("p t g c -> p t (g c)"),
                                       start=False, stop=(ri == 1), skip_group_check=True,
                                       tile_position=(0, 32 * qq))
        if j % 2 == 1:
            e_mm = PE.mark(ins)
            DVE.wait(e_mm)
            for jj in (j - 1, j):
                bs = (jj % 2) * 256
                for qq in range(4):
                    last = DVE.raw.tensor_copy(
                        out=KB[32 * qq:32 * qq + 32, jj, :, 32 * qq:32 * qq + 32],
                        in_=PS[32 * qq:32 * qq + 32, b, bs:bs + 256].rearrange("p (t m) -> p t m", m=32))
            e_kb = dv(last)
            BK.release(b, e_kb)
    for j in range(8):
        last = DVE.raw.scalar_tensor_tensor(out=KB[:, j, 0, :], in0=IDF[:], scalar=V("ssm_d", j),
                                            in1=KB[:, j, 0, :], op0=ALU.mult, op1=ALU.add)
    e_kbd = dv(last)

    if "prologue" in dbg:
        d_pw = K.dout("d_pw", [128, 9 * 2 * 32])
        d_kb = K.dout("d_kb", [128, 8 * 8 * 128], BF16)
        d_f = K.dout("d_f", [128, 2, 32])
        SP.wait(e_kbd, e_ar)
        outs.add(nc.sync.dma_start(out=d_pw, in_=PW[:].rearrange("p s r q -> p (s r q)")))
        outs.add(nc.sync.dma_start(out=d_kb, in_=KB[:].rearrange("p j o m -> p (j o m)")))
        outs.add(nc.sync.dma_start(out=d_f[:, 0, :], in_=FR))
        outs.add(nc.sync.dma_start(out=d_f[:, 1, :], in_=FI))

    conv_dma("ff1a", "ff1", (0, D), (0, 2048))
    conv_dma("ff2a", "ff2", (0, 2048), (0, D))
    conv_dma("ff1b", "ff1", (0, D), (2048, 4096))
    conv_dma("ff2b", "ff2", (2048, 4096), (0, D))

    e_end_dve = DVE.mark(DVE.raw.memset(T3, 0.0))
    for E_ in (PE, ACT, POOL, SP):
        E_.wait(e_end_dve, e_kbd, ev_cssc, ev_dsc, e_wt)
    DVE.wait(ev_cssc, ev_dsc)
    K.pop_scope()

    K.prologue_events = dict(cssc=ev_cssc, winsc=[ev_winsc0, ev_winsc], dsc=ev_dsc, conv=conv_ev,
                             derv=e_derv, ar=e_ar, kb=e_kbd, ident=e_id)
    K.handles = dict(PE=PE, ACT=ACT, DVE=DVE, POOL=POOL, SP=SP, BK=BK, PS=PS, outs=outs, V=V,
                     VECS=VECS, DERV=DERV, ARt=ARt, AI2t=AI2t, HCAR=HCAR, KB=KB, IDF=IDF, IDB=IDB,
                     ONESB=ONESB, NHALF=NHALF,
                     dram=dict(xp=xp, xs=xs, meta=meta, h0re=h0re_d, h0im=h0im_d, sconv_fm=sconv_fm,
                               sconv_nat=sconv_nat, yp=yp, ys=ys, nre_p=nre_p, nim_p=nim_p, ncv_p=ncv_p,
                               nre_s=nre_s, nim_s=nim_s, ncv_s=ncv_s, wsc=wsc, dsc=dsc, winsc=winsc,
                               cssc=cssc))
    return K


def main_loop(K, ntiles=5, stop_after=None):
    nc = K.nc
    dbg = K.debug

    def chk(name, ti):
        K.phase_log.append((name, ti, K.pe_n))
        if stop_after is not None and stop_after == (name, ti):
            raise StopBuild()
    h = K.handles
    PE, ACT, DVE, POOL, SP, BK, PS, outs, V = (h[k] for k in ("PE", "ACT", "DVE", "POOL", "SP", "BK", "PS", "outs", "V"))
    VECS, DERV, ARt, AI2t, HCAR, KB, IDF, IDB, ONESB, NHALF = (h[k] for k in (
        "VECS", "DERV", "ARt", "AI2t", "HCAR", "KB", "IDF", "IDB", "ONESB", "NHALF"))
    dr = h["dram"]
    pev = K.prologue_events

    XS = K.sb("XS", [128, 2, D], F32)
    YS = K.sb("YS", [128, 2, D], F32)
    R32 = K.sb("R32", [128, NCH, NTMAX], F32)
    B0 = K.sb("B0", [128, NCH, NTMAX], BF16)
    B1 = K.sb("B1", [128, NCH, NTMAX], BF16)
    B2 = K.sb("B2", [128, NCH, NTMAX], BF16)
    B5 = K.sb("B5", [128, NCH, NTMAX], BF16)
    GLU = K.sb("GLU", [128, NCH, HP + NTMAX], BF16)
    G16 = K.sb("G16", [128, 2 * NCH * NTMAX], BF16)
    TMPA = K.sb("TMPA", [128, NTMAX], F32)
    TMPB = K.sb("TMPB", [128, NTMAX], F32)
    SG = K.sb("SG", [128, 2, NTMAX], BF16)
    XH = K.sb("XH", [128, 32, 2, NKMAX], F32)
    HB = K.sb("HB", [128, 32, 2, NKMAX], BF16)
    SCR = K.sb("SCR", [128, 3, 32, 2], F32)
    WSL = K.sb("WSL", [128, 3, 4096], BF16)
    SSMW = K.sb("SSMW", [128, 16384], BF16)
    ST6 = K.sb("ST6", [128, 2, 12], F32)
    MV = K.sb("MV", [128, 2, 2], F32)
    RS = K.sb("RS", [128, 2, 2], F32)
    OST = K.sb("OST", [128, 2, 128], F32)
    H0S = view(YS, D, [(2 * NSEQ, 32), (NSEQ, 2), (1, NSEQ)])

    WINv = SSMW[:, :].rearrange("p (s j r m) -> p s j r m", j=8, s=8, r=2)
    CSv = SSMW[:, :].rearrange("p (t q r m) -> p t q r m", q=32, t=8, r=2)

    zero_ev = POOL.mark(POOL.raw.memset(GLU[:, :, 0:HP], 0.0))

    def tile_units():
        u = [("mat", "in", 0, 0), ("mat", "in", 0, 1),
             ("mat", "in", 0, 2), ("mat", "in", 0, 4), ("mat", "in", 0, 3), ("mat", "in", 0, 5)]
        u += [("diag", c) for c in range(8)]
        u += [("mat", "in", 0, 8), ("mat", "in", 0, 9), ("mat", "in", 0, 6), ("mat", "in", 0, 7)]
        u += [("mat", "bout", 0, 0), ("mat", "bout", 0, 1), ("mat", "glu", 0, 0), ("mat", "glu", 0, 1),
              ("mat", "aout", 0, 0), ("mat", "aout", 0, 1), ("mat", "o", 0, 0), ("mat", "o", 0, 1)]
        for hh in range(2):
            u += [("mat", "ff1", 0, 4 * hh + cb) for cb in range(4)]
            for cbo in range(2):
                u += [("mat", "ff2", 2 * hh + kgl, cbo) for kgl in range(2)]
        return u

    class WStream:
        def __init__(self, ntile):
            self.specs = []
            for t in range(ntile):
                self.specs += [(sp, t == 0) for sp in tile_units()]
            self.issued = 0
            self.taken = 0
            self.released = {}
            self.handles = {}
            self.sem = [DmaSem(K, f"w{i}") for i in range(3)]

        def _issue(self, j):
            spec, first_tile = self.specs[j]
            sl = j % 3
            if j >= 3:
                SP.wait(self.released[j - 3])
            if spec[0] == "mat":
                _, name, kg, cb = spec
                if first_tile:
                    if name == "in":
                        key = f"in{cb // 2}"
                    elif name == "ff1":
                        key = "ff1a" if cb < 4 else "ff1b"
                    elif name == "ff2":
                        key = "ff2a" if kg < 2 else "ff2b"
                    else:
                        key = name
                    SP.wait(pev["conv"][key])
                src = dr["wsc"][name][kg * 1024:(kg + 1) * 1024, cb * 512:(cb + 1) * 512].rearrange(
                    "(kc p) n -> p kc n", p=128)
                dst = WSL[:, sl, :].rearrange("p (kc n) -> p kc n", n=512)
            else:
                _, c = spec
                if first_tile:
                    SP.wait(pev["dsc"])
                src = dr["dsc"][c]
                dst = WSL[:, sl, 0:31 * 128]
            ev = self.sem[sl].add(nc.sync.dma_start(out=dst, in_=src))
            self.handles[j] = (j, ev)

        def pump(self, upto):
            while self.issued <= min(upto, len(self.specs) - 1):
                j = self.issued
                if j >= 3 and (j - 3) not in self.released:
                    break
                self._issue(j)
                self.issued += 1

        def load(self, spec, first_tile):
            idx = self.taken
            self.taken += 1
            assert self.specs[idx][0] == spec, (self.specs[idx], spec)
            self.pump(idx + 2)
            assert idx in self.handles
            j, ev = self.handles[idx]
            return j, ev

        def release(self, j, ev):
            self.released[j] = ev
            self.pump(self.taken + 1)

    WS = None

    def wmat(j):
        return WSL[:, j % 3, :].rearrange("p (kc n) -> p kc n", n=512)

    xs_free = [None, None]
    xs_sem = [DmaSem(K, "xs0"), DmaSem(K, "xs1")]
    ys_free = [None, None]
    ssmw_sem = DmaSem(K, "ssmw")
    h0_sem = DmaSem(K, "h0")
    hist_sem = DmaSem(K, "hist")
    ys_sem = [DmaSem(K, "ys0"), DmaSem(K, "ys1")]
    ost_sem = [DmaSem(K, "ost0"), DmaSem(K, "ost1")]
    cvo_sem = DmaSem(K, "cvo")
    ssmw_free = None
    glu_hist_ev = zero_ev
    hcar_ev = None
    state = dict(xs_i=0, ys_i=0)

    tiles = [dict(NT=512, kind="P", tok0=512 * i) for i in range(4)] + [dict(NT=144, kind="S")]
    tiles = tiles[:ntiles] if ntiles < 5 else tiles
    if dbg.get("only_last"):
        tiles = [tiles[-1]]
    WS = WStream(len(tiles))

    def make_subt(T):
        if T["kind"] == "S":
            return [dict(c0=0, R=128, src=[(dr["xs"][0:128, :], 0, 128)], dst=[(dr["ys"][0:128, :], 0, 128)]),
                    dict(c0=128, R=16, src=[(dr["xp"][2032:2048, :], 0, 16)], dst=[(dr["yp"][2032:2048, :], 0, 16)])]
        subt = []
        for m in range(4):
            t0 = T["tok0"] + 128 * m
            if t0 == 0:
                src = [(dr["meta"][0:16, :], 0, 16), (dr["xp"][0:112, :], 16, 128)]
                dst = [(dr["yp"][0:112, :], 16, 128)]
            else:
                src = [(dr["xp"][t0 - 16:t0 + 112, :], 0, 128)]
                dst = [(dr["yp"][t0 - 16:t0 + 112, :], 0, 128)]
            subt.append(dict(c0=128 * m, R=128, src=src, dst=dst))
        return subt

    entry_evs_by_tile = {}
    subts = [make_subt(T) for T in tiles]

    def entry_A(ti_, st):
        c0, R = st["c0"], st["R"]
        sl = state["xs_i"] % 2
        state["xs_i"] += 1
        SP.wait(xs_free[sl])
        for (src, r0, r1) in st["src"]:
            ev_ld = xs_sem[sl].add(nc.sync.dma_start(out=XS[r0:r1, sl, :], in_=src))
        DVE.wait(ev_ld)
        DVE.raw.bn_stats(ST6[0:R, sl, 0:6], XS[0:R, sl, 0:512])
        e = DVE.mark(DVE.raw.bn_stats(ST6[0:R, sl, 6:12], XS[0:R, sl, 512:1024]))
        DVE.wait(e)
        e = DVE.mark(DVE.raw.bn_aggr(MV[0:R, sl, :], ST6[0:R, sl, :]))
        DVE.wait(e)
        e = DVE.mark(DVE.raw.tensor_scalar(out=RS[0:R, sl, 0:1], in0=MV[0:R, sl, 1:2], scalar1=EPS,
                                           scalar2=None, op0=ALU.add))
        ACT.wait(e)
        e = ACT.mark(ACT.raw.activation(out=RS[0:R, sl, 1:2], in_=RS[0:R, sl, 0:1], func=AF.Ln))
        ACT.wait(e)
        e = ACT.mark(ACT.raw.activation(out=RS[0:R, sl, 1:2], in_=RS[0:R, sl, 1:2], func=AF.Exp, scale=-0.5))
        DVE.wait(e)
        e_n = DVE.mark(DVE.raw.tensor_scalar(out=XS[0:R, sl, :], in0=XS[0:R, sl, :], scalar1=MV[0:R, sl, 0:1],
                                             scalar2=RS[0:R, sl, 1:2], op0=ALU.subtract, op1=ALU.mult))
        return dict(sl=sl, e_n=e_n)

    def entry_B(ti_, st, A_, guard_ev):
        XB = B0
        entry_evs = entry_evs_by_tile.setdefault(ti_, [])
        c0, R = st["c0"], st["R"]
        sl, e_n = A_["sl"], A_["e_n"]
        PE.wait(e_n, pev["ident"])
        for half in range(2):
            b = BK.alloc()
            for k4 in range(4):
                kc = half * 4 + k4
                ins = K.TR(PS[:, b, k4 * 128:k4 * 128 + R], XS[0:R, sl, kc * 128:(kc + 1) * 128],
                                          IDF[0:R, 0:R])
            e_t = PE.mark(ins)
            if half == 1:
                xs_free[sl] = e_t
            eng = ACT if half == 0 else DVE
            eng.wait(e_t, pev["derv"], guard_ev)
            for k4 in range(4):
                kc = half * 4 + k4
                src = PS[:, b, k4 * 128:k4 * 128 + R]
                if eng is ACT:
                    ACT.raw.activation(out=XB[:, kc, c0:c0 + R], in_=src, func=AF.Identity,
                                       scale=V("ln_in_g", kc), bias=V("ln_in_b", kc))
                    ins = ACT.raw.activation(out=R32[:, kc, c0:c0 + R], in_=src, func=AF.Identity,
                                             scale=DERV[:, 0, kc:kc + 1], bias=DERV[:, 1, kc:kc + 1])
                else:
                    DVE.raw.tensor_scalar(out=XB[:, kc, c0:c0 + R], in0=src, scalar1=V("ln_in_g", kc),
                                          scalar2=V("ln_in_b", kc), op0=ALU.mult, op1=ALU.add)
                    ins = DVE.raw.tensor_scalar(out=R32[:, kc, c0:c0 + R], in0=src,
                                                scalar1=DERV[:, 0, kc:kc + 1], scalar2=DERV[:, 1, kc:kc + 1],
                                                op0=ALU.mult, op1=ALU.add)
            ee = eng.mark(ins)
            BK.release(b, ee)
            entry_evs.append(ee)

    def issue_win(free_ev, first_):
        SP.wait(free_ev)
        if first_:
            SP.wait(pev["winsc"])
        state["ev_win"] = ssmw_sem.add(nc.sync.dma_start(out=SSMW[:, :], in_=dr["winsc"]))

    issue_win(None, True)
    _A = {}
    for m0, st0 in enumerate(subts[0]):
        if m0 < 2:
            _A[m0] = entry_A(0, st0)
    for m0, st0 in enumerate(subts[0]):
        entry_B(0, st0, _A[m0], None)
        if m0 + 2 < len(subts[0]):
            _A[m0 + 2] = entry_A(0, subts[0][m0 + 2])
    chk("start", 0)
    for ti, T in enumerate(tiles):
        NT = T["NT"]
        NK = NT // TC
        first = (ti == 0)
        isS = T["kind"] == "S"
        if isS:
            GA = view(G16, 0, [(NT, 8), (1, NT)])
            GB = view(G16, 8 * NT, [(NT, 8), (1, NT)])
            HDN = view(G16, 0, [(NT, 16), (1, NT)])
            NSQ2 = NSEQ + 2
            GLUS = view(G16, 16 * NT, [(NSQ2 * 38, 8), (38, NSQ2), (1, 38)])
            GNEW = XS[:, 1, 0:512].bitcast(BF16).rearrange("p (k n) -> p k n", n=128)
        else:
            GA = view(G16, 0, [(NTMAX, 8), (1, NT)])
            GB = view(G16, 8 * NTMAX, [(NTMAX, 8), (1, NT)])
            HDN = view(G16, 0, [(NTMAX, 16), (1, NT)])
            GLUS = None
        XB = B0
        MT = B0
        X1B = B0
        UT = B1
        SBF = B1
        CV = B2
        ZT = B2
        SQ = B2
        CVSQ = B5
        CVN = B5
        ZSG = B5
        subt = subts[ti]

        if isS:
            SP.wait(ys_free[1])
            h0_sem.add(nc.sync.dma_start(out=H0S[:, :, 0, :], in_=dr["h0re"].rearrange("p (q s) -> p q s", s=NSEQ)))
            ev_h0 = h0_sem.add(nc.sync.dma_start(out=H0S[:, :, 1, :], in_=dr["h0im"].rearrange("p (q s) -> p q s", s=NSEQ)))
            POOL.wait(state.get("g16_free"))
            scf = dr["sconv_fm"].rearrange("p (k s r) -> p k s r", k=NCH, s=NSEQ)
            for k2 in range(NCH):
                ev_hist = hist_sem.add(nc.gpsimd.dma_start(out=GLUS[:, k2, 0:NSEQ, 0:HP], in_=scf[:, k2, :, :]))
            outs.add(nc.sync.dma_start(
                out=dr["ncv_s"].rearrange("(s r) d -> s r d", r=HP)[:, 0:HP - TC, :],
                in_=dr["sconv_nat"].rearrange("(s r) d -> s r d", r=HP)[:, TC:HP, :]))

        entry_evs = entry_evs_by_tile[ti]
        xb_ready = list(entry_evs)
        chk("entry", ti)

        def proj_unit(spec, rhs_fn, nk, evac_fn, rhs_wait, oc0, kc0=0, bank_of=None, start0=True, stop_last=True,
                      release=True):
            sl, ev_w = WS.load(spec, first)
            PE.wait(ev_w, rhs_wait)
            wm = wmat(sl)
            last = None
            for o4 in range(4):
                if bank_of is None:
                    b = BK.alloc()
                else:
                    b = bank_of(o4)
                for kc in range(nk):
                    last = K.MM(PS[:, b, 0:NT], lhsT=wm[:, kc, o4 * 128:(o4 + 1) * 128],
                                            rhs=rhs_fn(kc0 + kc), start=(start0 and kc == 0),
                                            stop=(stop_last and kc == nk - 1))
                if evac_fn is not None:
                    e_mm = PE.mark(last)
                    evac_fn(oc0 + o4, b, e_mm)
            e_rel = PE.mark(last) if evac_fn is None else e_mm
            WS.release(sl, e_rel)
            return e_rel

        ut_evs = []

        def evac_u(oc, b, e_mm):
            ACT.wait(e_mm)
            e = ACT.mark(ACT.raw.activation(out=UT[:, oc, 0:NT], in_=PS[:, b, 0:NT], func=AF.Identity,
                                            bias=V("b_in", oc), scale=1.0))
            BK.release(b, e)
            ut_evs.append(e)

        ACT.wait(state.get("b1_free"))
        for cb in range(2):
            proj_unit(("mat", "in", 0, cb), lambda kc: XB[:, kc, 0:NT], 8, evac_u, xb_ready, oc0=4 * cb)

        chk("inproj_u", ti)
        ev_win = state["ev_win"]
        PE.wait(ev_win, ut_evs)
        x_evs = []
        for jr in range(2):
            banks = [BK.alloc() for _ in range(4)]
            fst = [True] * 4
            for j4 in range(4):
                j = jr * 4 + j4
                for ri in range(2):
                    col = (j4 * 2 + ri) * NK
                    for s_ in range(TC):
                        for qq in range(4):
                            rows = slice(32 * qq, 32 * qq + 32)
                            last = K.MM(
                                PS[:, banks[qq], col:col + NK], lhsT=WINv[rows, s_, j, ri, :],
                                rhs=UT[rows, j, s_:NT:TC], start=fst[qq], stop=(s_ == TC - 1),
                                skip_group_check=True, tile_position=(32 * qq, 0))
                            fst[qq] = False
            e_mm = PE.mark(last)
            XHv = XH[:, :, :, :].rearrange("p (j q) r k -> p j q r k", q=4)
            for qq in range(4):
                eng = ACT if qq % 2 == 0 else DVE
                eng.wait(e_mm, state.get("xh_free"))
                src = PS[:, banks[qq], 0:8 * NK].rearrange("p (j r k) -> p j r k", j=4, r=2)
                dst = XHv[:, jr * 4:jr * 4 + 4, qq, :, 0:NK]
                if eng is ACT:
                    e = ACT.mark(ACT.raw.activation(out=dst, in_=src, func=AF.Copy))
                else:
                    e = DVE.mark(DVE.raw.tensor_copy(out=dst, in_=src))
                BK.release(banks[qq], e)
                x_evs.append(e)
        ssmw_free = e_mm
        SP.wait(ssmw_free)
        if first:
            SP.wait(pev["cssc"])
        ev_cs = ssmw_sem.add(nc.sync.dma_start(out=SSMW[:, :], in_=dr["cssc"]))

        chk("ssm_in", ti)
        POOL.wait(x_evs, hcar_ev, pev["ar"], state.get("hb_free"))

        def pw(ins):
            e = POOL.mark(ins)
            POOL.wait(e)
            return e

        def scan_step(hprev, xk_view):
            m1, m2, t_ = SCR[:, 0], SCR[:, 1], SCR[:, 2]
            POOL.raw.tensor_tensor(out=m1, in0=hprev, in1=ARt[:], op=ALU.mult)
            pw(POOL.raw.tensor_tensor(out=m2, in0=hprev, in1=AI2t[:], op=ALU.mult))
            pw(POOL.raw.tensor_tensor(out=t_, in0=m1, in1=xk_view, op=ALU.add))
            return pw(POOL.raw.tensor_tensor(out=xk_view, in0=t_, in1=m2[:, :, ::-1], op=ALU.add))

        if isS:
            POOL.wait(ev_h0)
            pw(POOL.raw.tensor_copy(out=HB[:, :, :, 0:NSEQ], in_=H0S))
            for s_ in range(NSEQ):
                scan_step(H0S[:, :, :, s_], XH[:, :, :, s_])
            pk0 = NSEQ
        else:
            pk0 = 0
        pw(POOL.raw.tensor_copy(out=HB[:, :, :, pk0], in_=HCAR[:]))
        for k in range(pk0, NK):
            hprev = HCAR[:] if k == pk0 else XH[:, :, :, k - 1]
            e_sc = scan_step(hprev, XH[:, :, :, k])
        hcar_ev = pw(POOL.raw.tensor_copy(out=HCAR[:], in_=XH[:, :, :, NK - 1]))
        scan_done = hcar_ev

        chk("scan", ti)
        glu_evs = [None] * 8
        for hh in range(2):
            slA, evA = WS.load(("mat", "in", 0, 2 + hh), first)
            slB, evB = WS.load(("mat", "in", 0, 4 + hh), first)
            PE.wait(evA, evB)
            for c4 in range(4):
                c = hh * 4 + c4
                bA = BK.alloc()
                for kc in range(8):
                    la = K.MM(PS[:, bA, 0:NT], lhsT=wmat(slA)[:, kc, c4 * 128:(c4 + 1) * 128],
                                          rhs=XB[:, kc, 0:NT], start=(kc == 0), stop=(kc == 7))
                bB = BK.alloc()
                for kc in range(8):
                    lb = K.MM(PS[:, bB, 0:NT], lhsT=wmat(slB)[:, kc, c4 * 128:(c4 + 1) * 128],
                                          rhs=XB[:, kc, 0:NT], start=(kc == 0), stop=(kc == 7))
                e_mm = PE.mark(lb)
                sgs = SG[:, c % 2, 0:NT]
                ACT.wait(e_mm, state.get(f"sg_free{c % 2}"))
                e_sg = ACT.mark(ACT.raw.activation(out=sgs, in_=PS[:, bB, 0:NT], func=AF.Sigmoid,
                                                   bias=V("b_in", 16 + c), scale=1.0))
                DVE.wait(e_sg, glu_hist_ev)
                if isS:
                    DVE.wait(ev_hist, xs_free[1])
                    e_gn = DVE.mark(DVE.raw.scalar_tensor_tensor(
                        out=GNEW[:, c, :], in0=PS[:, bA, 0:128], scalar=V("b_in", 8 + c), in1=SG[:, c % 2, 0:128],
                        op0=ALU.add, op1=ALU.mult))
                    e_gp = DVE.mark(DVE.raw.scalar_tensor_tensor(
                        out=GLU[:, c, HP:HP + 16], in0=PS[:, bA, 128:144], scalar=V("b_in", 8 + c),
                        in1=SG[:, c % 2, 128:144], op0=ALU.add, op1=ALU.mult))
                    DVE.wait(e_gn, e_gp)
                    DVE.raw.tensor_copy(out=GLUS[:, c, 0:NSEQ, HP:HP + TC],
                                        in_=GNEW[:, c, :].rearrange("p (s t) -> p s t", t=TC))
                    ins = DVE.raw.tensor_copy(out=GLUS[:, c, NSEQ:NSEQ + 2, :],
                                              in_=view(GLU, c * (HP + NTMAX), [(TC, 2), (1, 38)]))
                else:
                    ins = DVE.raw.scalar_tensor_tensor(
                        out=GLU[:, c, HP:HP + NT], in0=PS[:, bA, 0:NT], scalar=V("b_in", 8 + c), in1=sgs,
                        op0=ALU.add, op1=ALU.mult)
                e_g = DVE.mark(ins)
                state[f"sg_free{c % 2}"] = e_g
                BK.release(bA, e_g)
                BK.release(bB, e_sg)
                glu_evs[c] = e_g
            WS.release(slA, e_mm)
            WS.release(slB, e_mm)

        chk("glu", ti)
        cv_evs = []
        conv_mm = None
        ACT.wait(state.get("b2_free"), state.get("b5_free"))
        for c in range(8):
            sl, ev_w = WS.load(("diag", c), first)
            PE.wait(ev_w, glu_evs[c])
            dg = WSL[:, sl % 3, 0:31 * 128].rearrange("p (t m) -> p t m", m=128)
            b = BK.alloc()
            for jt in range(31):
                if isS:
                    last = K.MM(PS[:, b, 0:NT], lhsT=dg[:, jt, :], rhs=GLUS[:, c, :, jt:jt + TC],
                                start=(jt == 0), stop=(jt == 30))
                else:
                    last = K.MM(PS[:, b, 0:NT], lhsT=dg[:, jt, :], rhs=GLU[:, c, jt:jt + NT],
                                            start=(jt == 0), stop=(jt == 30))
            e_mm = PE.mark(last)
            conv_mm = e_mm
            WS.release(sl, e_mm)
            ACT.wait(e_mm)
            ACT.raw.activation(out=CV[:, c, 0:NT], in_=PS[:, b, 0:NT], func=AF.Identity, bias=V("conv_b", c),
                               scale=1.0)
            e = ACT.mark(ACT.raw.activation(out=CVSQ[:, c, 0:NT], in_=PS[:, b, 0:NT], func=AF.Square,
                                            bias=V("conv_b", c), scale=1.0))
            BK.release(b, e)
            cv_evs.append(e)

        chk("conv", ti)
        if isS:
            tr_evs = []
            for half in range(2):
                b = BK.alloc()
                pb16 = PS[:, b, :].bitcast(BF16)
                for k4 in range(4):
                    kc = half * 4 + k4
                    K.TR(pb16[0:128, k4 * 128:(k4 + 1) * 128], GNEW[:, kc, :], IDB[:])
                    ins = K.TR(pb16[0:HP, 512 + k4 * 128:512 + (k4 + 1) * 128],
                                              GLU[:, kc, 16:16 + HP], IDB[:])
                e_t = PE.mark(ins)
                ACT.wait(e_t, ys_free[0])
                ACT.raw.activation(out=YS[0:128, 0, half * 512:(half + 1) * 512], in_=pb16[0:128, 0:512],
                                   func=AF.Copy)
                e = ACT.mark(ACT.raw.activation(out=XS[0:HP, 0, half * 512:(half + 1) * 512],
                                                in_=pb16[0:HP, 512:1024], func=AF.Copy))
                BK.release(b, e)
                tr_evs.append(e)
            SP.wait(tr_evs)
            ncs = dr["ncv_s"].rearrange("(s r) d -> s r d", r=HP)
            for s_ in range(NSEQ):
                ev_o = ys_sem[0].add(nc.sync.dma_start(out=ncs[s_, HP - TC:HP, :], in_=YS[s_ * TC:(s_ + 1) * TC, 0, :]))
            ys_free[0] = ev_o
            xs_free[0] = cvo_sem.add(nc.sync.dma_start(out=dr["ncv_p"], in_=XS[0:HP, 0, :]))
        else:
            POOL.wait(conv_mm)
            glu_hist_ev = POOL.mark(POOL.raw.tensor_copy(out=GLU[:, :, 0:HP], in_=GLU[:, :, NT:NT + HP]))

        def ln_stats(src_bf, sq_bf, ready_evs, banks=None):
            if banks is None:
                bM = BK.alloc(hold=True)
                bQ = BK.alloc(hold=True)
            else:
                bM, bQ = banks
            for kc in range(8):
                PE.wait(ready_evs[kc])
                K.MM(PS[:, bM, 0:NT], lhsT=ONESB[:], rhs=src_bf[:, kc, 0:NT], start=(kc == 0),
                     stop=(kc == 7))
                last = K.MM(PS[:, bQ, 0:NT], lhsT=ONESB[:], rhs=sq_bf[:, kc, 0:NT], start=(kc == 0),
                            stop=(kc == 7))
            e_mm = PE.mark(last)
            ACT.wait(e_mm, state.get("tmp_free"))
            e = ACT.mark(ACT.raw.activation(out=TMPA[:, 0:NT], in_=PS[:, bM, 0:NT], func=AF.Square))
            DVE.wait(e)
            e = DVE.mark(DVE.raw.scalar_tensor_tensor(out=TMPB[:, 0:NT], in0=PS[:, bQ, 0:NT], scalar=EPS,
                                                      in1=TMPA[:, 0:NT], op0=ALU.add, op1=ALU.subtract))
            ACT.wait(e)
            e = ACT.mark(ACT.raw.activation(out=TMPB[:, 0:NT], in_=TMPB[:, 0:NT], func=AF.Ln))
            ACT.wait(e)
            e = ACT.mark(ACT.raw.activation(out=TMPA[:, 0:NT], in_=TMPB[:, 0:NT], func=AF.Exp, scale=-0.5))
            DVE.wait(e)
            e = DVE.mark(DVE.raw.scalar_tensor_tensor(out=PS[:, bQ, 0:NT], in0=PS[:, bM, 0:NT], scalar=-1.0,
                                                      in1=TMPA[:, 0:NT], op0=ALU.mult, op1=ALU.mult))
            DVE.wait(e)
            e = DVE.mark(DVE.raw.tensor_copy(out=PS[:, bM, 0:NT], in_=TMPA[:, 0:NT]))
            DVE.wait(e)
            return bM, bQ, e

        bR, bN, e_st = ln_stats(CV, CVSQ, cv_evs)
        chk("convln", ti)
        ga_evs, gb_evs = [], []

        def evac_gate(dst, lst, boff):
            def f(oc, b, e_mm):
                ACT.wait(e_mm)
                e = ACT.mark(ACT.raw.activation(out=dst[:, oc, :], in_=PS[:, b, 0:NT], func=AF.Sigmoid,
                                                bias=V("b_in", boff + oc), scale=1.0))
                BK.release(b, e)
                lst.append(e)
            return f

        ACT.wait(state.get("g16_free"))
        for cb in range(2):
            proj_unit(("mat", "in", 0, 8 + cb), lambda kc: XB[:, kc, 0:NT], 8, evac_gate(GB, gb_evs, 32), None,
                      oc0=4 * cb)
        for cb in range(2):
            e_xb_done = proj_unit(("mat", "in", 0, 6 + cb), lambda kc: XB[:, kc, 0:NT], 8,
                                  evac_gate(GA, ga_evs, 24), None, oc0=4 * cb)

        cvn_evs = []
        for c in range(8):
            tmp = TMPA if c % 2 == 0 else TMPB
            DVE.wait(state.get(f"tmpn_free{c % 2}"))
            e = DVE.mark(DVE.raw.tensor_tensor(out=tmp[:, 0:NT], in0=CV[:, c, 0:NT], in1=PS[:, bR, 0:NT], op=ALU.mult))
            DVE.wait(e)
            e = DVE.mark(DVE.raw.tensor_tensor(out=tmp[:, 0:NT], in0=tmp[:, 0:NT], in1=PS[:, bN, 0:NT], op=ALU.add))
            ACT.wait(e)
            e = ACT.mark(ACT.raw.activation(out=CVN[:, c, 0:NT], in_=tmp[:, 0:NT], func=AF.Silu,
                                            scale=V("conv_ln_g", c), bias=V("conv_ln_b", c)))
            state[f"tmpn_free{c % 2}"] = e
            cvn_evs.append(e)
        BK.release(bR, e)
        BK.release(bN, e)
        state["tmp_free"] = e

        chk("gates", ti)
        mt_evs = [None] * 8

        def evac_pb(oc, b, e_mm):
            DVE.wait(e_mm, gb_evs, e_xb_done)
            e = DVE.mark(DVE.raw.scalar_tensor_tensor(out=MT[:, oc, 0:NT], in0=PS[:, b, 0:NT],
                                                      scalar=V("b_b_out", oc), in1=GB[:, oc, :],
                                                      op0=ALU.add, op1=ALU.mult))
            BK.release(b, e)
            mt_evs[oc] = e

        for cb in range(2):
            e_cvn_done = proj_unit(("mat", "bout", 0, cb), lambda kc: CVN[:, kc, 0:NT], 8, evac_pb, cvn_evs, oc0=4 * cb)

        chk("wbout", ti)
        hb_ev = scan_done
        if NK - pk0 > 1:
            DVE.wait(scan_done)
            hb_ev = DVE.mark(DVE.raw.tensor_copy(out=HB[:, :, :, pk0 + 1:NK], in_=XH[:, :, :, pk0:NK - 1]))
        PE.wait(ev_cs, scan_done, hb_ev, pev["kb"])
        zt_evs = []
        ACT.wait(cvn_evs)
        for j in range(8):
            b = BK.alloc()
            fst = True
            utv = UT[:, j, 0:NT].rearrange("p (k s) -> p s k", s=TC)
            for off in range(TC):
                ns = TC - off
                K.MM(PS[:, b, off * NK:TC * NK], lhsT=KB[:, j, off, :], rhs=utv[:, 0:ns, :],
                     start=fst, stop=False, skip_group_check=True)
                fst = False
            for tp in range(TC):
                for qq in range(4):
                    q = 4 * j + qq
                    for ri in range(2):
                        last = K.MM(PS[32 * qq:32 * qq + 32, b, tp * NK:(tp + 1) * NK],
                                                lhsT=CSv[:, tp, q, ri, :], rhs=HB[:, q, ri, 0:NK], start=False,
                                                stop=(tp == TC - 1 and qq == 3 and ri == 1),
                                                skip_group_check=True, tile_position=(0, 32 * qq))
            e_mm = PE.mark(last)
            ACT.wait(e_mm)
            e = ACT.mark(ACT.raw.activation(
                out=ZT[:, j, 0:NT].rearrange("p (k t) -> p k t", t=TC),
                in_=PS[:, b, 0:NT].rearrange("p (t k) -> p k t", t=TC), func=AF.Gelu_apprx_tanh))
            BK.release(b, e)
            zt_evs.append(e)
        ssmw_free = e_mm
        if ti + 1 < len(tiles):
            issue_win(ssmw_free, False)
        state["hb_free"] = e_mm
        state["xh_free"] = [scan_done, hb_ev]
        ut_done = e_mm

        chk("ssm_out", ti)
        if isS:
            DVE.wait(scan_done, state.get("tmpn_free0"), state.get("tmpn_free1"))
            for ri in range(2):
                dst_s = dr["nre_s"] if ri == 0 else dr["nim_s"]
                dst_p = dr["nre_p"] if ri == 0 else dr["nim_p"]
                for g4 in range(5):
                    b = BK.alloc()
                    if g4 < 4:
                        e = DVE.mark(DVE.raw.tensor_copy(
                            out=TMPA[:, 0:128].rearrange("p (s q) -> p s q", q=32),
                            in_=XH[:, :, ri, 4 * g4:4 * g4 + 4].rearrange("p q s -> p s q")))
                        ncol = 128
                    else:
                        e = DVE.mark(DVE.raw.tensor_copy(out=TMPA[:, 0:32], in_=XH[:, :, ri, NK - 1]))
                        ncol = 32
                    PE.wait(e)
                    e_t = PE.mark(K.TR(PS[0:ncol, b, 0:128], TMPA[:, 0:ncol], IDF[:]))
                    osl = state.get("ost_i", 0) % 2
                    state["ost_i"] = state.get("ost_i", 0) + 1
                    DVE.wait(e_t, state.get(f"ost_free{osl}"))
                    e = DVE.mark(DVE.raw.tensor_copy(out=OST[0:ncol, osl, :], in_=PS[0:ncol, b, 0:128]))
                    BK.release(b, e)
                    SP.wait(e)
                    if g4 < 4:
                        ev_o = ost_sem[osl].add(nc.sync.dma_start(out=dst_s[128 * g4:128 * (g4 + 1), :], in_=OST[:, osl, :]))
                    else:
                        ev_o = ost_sem[osl].add(nc.sync.dma_start(out=dst_p, in_=OST[0:32, osl, :]))
                    state[f"ost_free{osl}"] = ev_o
            state["tmp_free2"] = e_t

        chk("stateout", ti)
        za_evs = [None] * 8

        def evac_glu(oc, b, e_mm):
            ACT.wait(e_mm, e_cvn_done)
            e = ACT.mark(ACT.raw.activation(out=ZSG[:, oc, 0:NT], in_=PS[:, b, 0:NT], func=AF.Sigmoid,
                                            bias=V("b_glu", oc), scale=1.0))
            BK.release(b, e)
            DVE.wait(e)
            e2 = DVE.mark(DVE.raw.tensor_tensor(out=ZSG[:, oc, 0:NT], in0=ZSG[:, oc, 0:NT], in1=ZT[:, oc, 0:NT],
                                                op=ALU.mult))
            za_evs[oc] = e2

        for cb in range(2):
            e_zt_done = proj_unit(("mat", "glu", 0, cb), lambda kc: ZT[:, kc, 0:NT], 8, evac_glu, zt_evs, oc0=4 * cb)

        chk("wglu", ti)
        mt2_evs = [None] * 8

        def evac_pa(oc, b, e_mm):
            tmp = TMPA if oc % 2 == 0 else TMPB
            DVE.wait(e_mm, ga_evs, mt_evs[oc], state.get("tmp_free2"))
            e = DVE.mark(DVE.raw.tensor_tensor(out=tmp[:, 0:NT], in0=PS[:, b, 0:NT], in1=GA[:, oc, :], op=ALU.mult))
            BK.release(b, e)
            DVE.wait(e)
            e2 = DVE.mark(DVE.raw.tensor_tensor(out=MT[:, oc, 0:NT], in0=tmp[:, 0:NT], in1=MT[:, oc, 0:NT],
                                                op=ALU.add))
            DVE.wait(e2)
            mt2_evs[oc] = e2

        for cb in range(2):
            e_za_done = proj_unit(("mat", "aout", 0, cb), lambda kc: ZSG[:, kc, 0:NT], 8, evac_pa, za_evs, oc0=4 * cb)
        state["b5_free"] = e_za_done

        chk("waout", ti)
        s1_evs = []

        def evac_o(oc, b, e_mm):
            DVE.wait(e_mm)
            e = DVE.mark(DVE.raw.tensor_tensor(out=R32[:, oc, 0:NT], in0=PS[:, b, 0:NT], in1=R32[:, oc, 0:NT],
                                               op=ALU.add))
            BK.release(b, e)
            s1_evs.append(e)

        ln1_banks = None
        for cb in range(2):
            if cb == 1:
                ln1_banks = (BK.alloc(hold=True), BK.alloc(hold=True))
            e_mt_done = proj_unit(("mat", "o", 0, cb), lambda kc: MT[:, kc, 0:NT], 8, evac_o, mt2_evs, oc0=4 * cb)

        chk("wo", ti)
        def layer_norm_r32(ready, pre_free, banks=None):
            ACT.wait(pre_free)
            evs = []
            for kc in range(8):
                ACT.wait(ready[kc])
                ACT.raw.activation(out=SBF[:, kc, 0:NT], in_=R32[:, kc, 0:NT], func=AF.Copy)
                evs.append(ACT.mark(ACT.raw.activation(out=SQ[:, kc, 0:NT], in_=R32[:, kc, 0:NT], func=AF.Square)))
            bR_, bN_, e_st_ = ln_stats(SBF, SQ, evs, banks)
            nevs = []
            for kc in range(8):
                e = DVE.mark(DVE.raw.tensor_tensor(out=R32[:, kc, 0:NT], in0=R32[:, kc, 0:NT], in1=PS[:, bR_, 0:NT],
                                                   op=ALU.mult))
                DVE.wait(e)
                e = DVE.mark(DVE.raw.tensor_tensor(out=R32[:, kc, 0:NT], in0=R32[:, kc, 0:NT], in1=PS[:, bN_, 0:NT],
                                                   op=ALU.add))
                DVE.wait(e)
                nevs.append(e)
            BK.release(bR_, e)
            BK.release(bN_, e)
            state["tmp_free"] = e
            return nevs

        nevs = layer_norm_r32(s1_evs, [ut_done, e_zt_done], ln1_banks)
        x1_evs = []
        for kc in range(8):
            ACT.wait(nevs[kc], e_mt_done)
            e = ACT.mark(ACT.raw.activation(out=X1B[:, kc, 0:NT], in_=R32[:, kc, 0:NT], func=AF.Identity,
                                            scale=V("ln1_g", kc), bias=V("ln1_b", kc)))
            ACT.wait(e)
            e = ACT.mark(ACT.raw.activation(out=R32[:, kc, 0:NT], in_=R32[:, kc, 0:NT], func=AF.Identity,
                                            scale=DERV[:, 2, kc:kc + 1], bias=DERV[:, 3, kc:kc + 1]))
            x1_evs.append(e)

        chk("ln1", ti)
        s2_evs = {}
        for hh in range(2):
            hd_evs = []

            def evac_ff1(oc, b, e_mm):
                ol = oc - 16 * hh
                ACT.wait(e_mm, state.get("hdn_free"), ga_evs, gb_evs)
                e = ACT.mark(ACT.raw.activation(out=HDN[:, ol, :], in_=PS[:, b, 0:NT], func=AF.Relu,
                                                bias=V("b_ff1", oc), scale=1.0))
                BK.release(b, e)
                POOL.wait(e)
                e2 = POOL.mark(POOL.raw.tensor_tensor(out=HDN[:, ol, :], in0=HDN[:, ol, :], in1=HDN[:, ol, :],
                                                      op=ALU.mult))
                hd_evs.append(e2)

            for cb in range(4):
                proj_unit(("mat", "ff1", 0, 4 * hh + cb), lambda kc: X1B[:, kc, 0:NT], 8, evac_ff1,
                          x1_evs + [e_za_done] if (hh == 0 and cb == 0) else None, oc0=16 * hh + 4 * cb)
            for cbo in range(2):
                if hh == 1 and cbo == 1:
                    ln2_banks = (BK.alloc(hold=True), BK.alloc(hold=True))
                banks = [BK.alloc() for _ in range(4)]
                for kgl in range(2):
                    def evac_ff2(oc, b, e_mm):
                        DVE.wait(e_mm, x1_evs)
                        e = DVE.mark(DVE.raw.tensor_tensor(out=R32[:, oc, 0:NT], in0=PS[:, b, 0:NT],
                                                           in1=R32[:, oc, 0:NT], op=ALU.add))
                        BK.release(b, e)
                        DVE.wait(e)
                        s2_evs[(hh, oc)] = e
                    e_h = proj_unit(("mat", "ff2", 2 * hh + kgl, cbo), lambda kc: HDN[:, kc, :], 8,
                                    evac_ff2 if kgl == 1 else None, hd_evs, oc0=4 * cbo, kc0=8 * kgl,
                                    bank_of=lambda o4: banks[o4], start0=(kgl == 0), stop_last=(kgl == 1))
            state["hdn_free"] = e_h
        state["g16_free"] = e_h

        chk("ffn", ti)
        nevs = layer_norm_r32([s2_evs[(1, kc)] for kc in range(8)], None, ln2_banks)
        y_evs = []
        for kc in range(8):
            ACT.wait(nevs[kc])
            e = ACT.mark(ACT.raw.activation(out=R32[:, kc, 0:NT], in_=R32[:, kc, 0:NT], func=AF.Identity,
                                            scale=V("ln2_g", kc), bias=V("ln2_b", kc)))
            y_evs.append(e)
        state["b1_free"] = nevs[-1]
        state["b2_free"] = nevs[-1]
        nxt_i = 0
        nA = {}
        if ti + 1 < len(tiles):
            for m_ in range(min(2, len(subts[ti + 1]))):
                nA[m_] = entry_A(ti + 1, subts[ti + 1][m_])
        for st in subt:
            c0, R = st["c0"], st["R"]
            sl = state["ys_i"] % 2
            state["ys_i"] += 1
            PE.wait(y_evs)
            for half in range(2):
                b = BK.alloc()
                for k4 in range(4):
                    kc = half * 4 + k4
                    ins = K.TR(PS[0:R, b, k4 * 128:(k4 + 1) * 128], R32[:, kc, c0:c0 + R], IDF[:])
                e_t = PE.mark(ins)
                eng = ACT if half == 0 else DVE
                eng.wait(e_t, ys_free[sl])
                if eng is ACT:
                    e = ACT.mark(ACT.raw.activation(out=YS[0:R, sl, half * 512:(half + 1) * 512],
                                                    in_=PS[0:R, b, :], func=AF.Copy))
                else:
                    e = DVE.mark(DVE.raw.tensor_copy(out=YS[0:R, sl, half * 512:(half + 1) * 512],
                                                     in_=PS[0:R, b, :]))
                BK.release(b, e)
                SP.wait(e)
            for (dst, r0, r1) in st["dst"]:
                ev_o = ys_sem[sl].add(nc.sync.dma_start(out=dst, in_=YS[r0:r1, sl, :]))
            ys_free[sl] = ev_o
            if ti + 1 < len(tiles):
                done_cols = c0 + R
                while nxt_i < len(subts[ti + 1]) and subts[ti + 1][nxt_i]["c0"] + subts[ti + 1][nxt_i]["R"] <= done_cols:
                    entry_B(ti + 1, subts[ti + 1][nxt_i], nA[nxt_i], e_t)
                    if nxt_i + 2 < len(subts[ti + 1]):
                        nA[nxt_i + 2] = entry_A(ti + 1, subts[ti + 1][nxt_i + 2])
                    nxt_i += 1
        if ti + 1 < len(tiles):
            while nxt_i < len(subts[ti + 1]):
                entry_B(ti + 1, subts[ti + 1][nxt_i], nA[nxt_i], e_t)
                if nxt_i + 2 < len(subts[ti + 1]):
                    nA[nxt_i + 2] = entry_A(ti + 1, subts[ti + 1][nxt_i + 2])
                nxt_i += 1

def finish(K):
    h = K.handles
    SP = h["SP"]
    outs = h["outs"]
    for E_ in K.engs:
        if E_ is not SP and E_.cnt > 0:
            SP.raw.wait_ge(E_.sem, E_.cnt)
    for ds in K.dmasems:
        if ds.cnt > 0:
            SP.raw.wait_ge(ds.sem, ds.cnt)
    SP.raw.wait_ge(outs.sem, outs.cnt)
    K.close()
    return K.nc


def _pack_vecs(inp):
    cols = []
    for name, n in VEC_SPECS:
        a = np.asarray(inp[name], np.float32)
        if name == "conv_w":
            a = a.reshape(31, 8, 128).transpose(2, 0, 1).reshape(128, 31 * 8)
        else:
            a = a.reshape(n, 128).T
        cols.append(a)
    return np.ascontiguousarray(np.concatenate(cols, axis=1), dtype=np.float32)


def _sl(a):
    return np.asarray(a, np.float32).reshape(32, 2, 64).transpose(1, 2, 0).reshape(128, 32)


def host_prep(inp):
    f32 = np.float32
    sh = {}
    sh["meta"] = np.ascontiguousarray(inp["meta_tokens"], f32)
    sh["vecs"] = _pack_vecs(inp)
    ldt = np.asarray(inp["ssm_log_dt"], f32).reshape(32, 2)
    ldt_sl = np.broadcast_to(ldt.T[:, None, :], (2, 64, 32)).reshape(128, 32)
    sh["ssm_small"] = np.ascontiguousarray(
        np.concatenate([_sl(inp["ssm_a_re"][0]), _sl(inp["ssm_a_im"][0]), ldt_sl], axis=1), f32)
    for nm, key in (("bre", "ssm_b_re"), ("bim", "ssm_b_im")):
        a = np.asarray(inp[key][0], f32).reshape(32, 2, 64, 16).transpose(1, 2, 0, 3).reshape(128, 512)
        sh[nm] = np.ascontiguousarray(a)
    for nm, key in (("cre", "ssm_c_re"), ("cim", "ssm_c_im")):
        a = np.asarray(inp[key][0], f32).reshape(32, 2, 16, 64).transpose(1, 3, 0, 2).reshape(128, 512)
        sh[nm] = np.ascontiguousarray(a)
    for nm, key in (("w_in", "w_in"), ("w_glu", "w_glu"), ("w_a_out", "w_a_out"), ("w_b_out", "w_b_out"),
                    ("w_o", "w_o"), ("w_ff1", "w_ff1"), ("w_ff2", "w_ff2")):
        sh[nm] = np.ascontiguousarray(inp[key][0], f32)
    per = []
    for b in range(8):
        d = dict(sh)
        d["xp"] = np.ascontiguousarray(inp["x_prompt"][b], f32)
        d["xs"] = np.ascontiguousarray(inp["x_sample"][16 * b:16 * b + 16], f32).reshape(128, 1024)
        for nm, key in (("h0re", "state_ssm_re"), ("h0im", "state_ssm_im")):
            a = np.asarray(inp[key][0, 16 * b:16 * b + 16], f32)
            d[nm] = np.ascontiguousarray(a.reshape(16, 32, 2, 64).transpose(2, 3, 1, 0).reshape(128, 512))
        sc = np.asarray(inp["state_conv"][0, 16 * b:16 * b + 16], f32)
        d["sconv_nat"] = np.ascontiguousarray(sc.reshape(16 * 30, 1024))
        d["sconv_fm"] = np.ascontiguousarray(sc.reshape(16, 30, 8, 128).transpose(3, 2, 0, 1).reshape(128, 8 * 16 * 30))
        per.append(d)
    return per


_CACHE = {}


def kernel(**inputs):
    if "nc" not in _CACHE:
        K = build()
        main_loop(K)
        _CACHE["nc"] = finish(K)
    nc = _CACHE["nc"]
    per = host_prep(inputs)
    res = run_bass_kernel_spmd(nc, per, core_ids=list(range(8)))
    R = res.results
    f32 = np.float32
    y_prompt = np.stack([np.asarray(R[b]["yp"], f32) for b in range(8)], 0)
    y_sample = np.concatenate([np.asarray(R[b]["ys"], f32).reshape(16, 8, 1024) for b in range(8)], 0)
    nrp = np.stack([np.asarray(R[b]["nre_p"], f32).reshape(64, 64) for b in range(8)], 0)[None]
    nip = np.stack([np.asarray(R[b]["nim_p"], f32).reshape(64, 64) for b in range(8)], 0)[None]
    ncp = np.stack([np.asarray(R[b]["ncv_p"], f32) for b in range(8)], 0)[None]
    nrs = np.concatenate([np.asarray(R[b]["nre_s"], f32).reshape(16, 64, 64) for b in range(8)], 0)[None]
    nis = np.concatenate([np.asarray(R[b]["nim_s"], f32).reshape(16, 64, 64) for b in range(8)], 0)[None]
    ncs = np.concatenate([np.asarray(R[b]["ncv_s"], f32).reshape(16, 30, 1024) for b in range(8)], 0)[None]
    return (y_prompt, y_sample, nrp, nip, ncp, nrs, nis, ncs)
```

```python
import math
import numpy as np
import concourse.bass as bass
import concourse.mybir as mybir
from concourse.bass_utils import run_bass_kernel_spmd

F32 = mybir.dt.float32
BF16 = mybir.dt.bfloat16
I32 = mybir.dt.int32
AF = mybir.ActivationFunctionType
ALU = mybir.AluOpType

D = 1024
NCH = 8
DFF = 4096
HP = 30
TC = 8
ALPHA = 2.0 ** 0.25
EPS = 1e-5
NTMAX = 512
NKMAX = NTMAX // TC
SEQ = 2048
NMETA = 16
NSAMP_TOK = 128
NSEQ = 16

VEC_SPECS = [("b_in", 40), ("b_glu", 8), ("b_b_out", 8), ("b_o", 8), ("b_ff1", 32), ("b_ff2", 8),
             ("ln_in_g", 8), ("ln_in_b", 8), ("conv_b", 8), ("conv_ln_g", 8), ("conv_ln_b", 8),
             ("ln1_g", 8), ("ln1_b", 8), ("ln2_g", 8), ("ln2_b", 8), ("ssm_d", 8), ("conv_w", 31 * 8)]
VOFF = {}
_o = 0
for _n, _c in VEC_SPECS:
    VOFF[_n] = _o
    _o += _c
NV = _o


class Ev:
    __slots__ = ("sem", "val")

    def __init__(self, sem, val):
        self.sem = sem
        self.val = val


class Eng:
    def __init__(self, K, raw, name):
        self.K = K
        self.raw = raw
        self.name = name
        self.nsem = 0
        self.seen = {}
        self._new_sem()
        K.engs.append(self)

    def _new_sem(self):
        self.sem = self.K.nc.alloc_semaphore(f"s_{self.name}_{self.nsem}")
        self.nsem += 1
        self.cnt = 0

    def wait(self, *evs):
        for ev in evs:
            if ev is None:
                continue
            if isinstance(ev, (list, tuple)):
                self.wait(*ev)
                continue
            k = ev.sem.num if hasattr(ev.sem, "num") else id(ev.sem)
            if self.seen.get(k, 0) >= ev.val:
                continue
            self.raw.wait_ge(ev.sem, ev.val)
            self.seen[k] = ev.val

    def mark(self, ins):
        if self.cnt >= 6000:
            self._new_sem()
        self.cnt += 1
        ins.then_inc(self.sem, 1)
        return Ev(self.sem, self.cnt)


class StopBuild(Exception):
    pass


class DmaSem:
    def __init__(self, K, name):
        self.sem = K.nc.alloc_semaphore(name)
        self.cnt = 0
        K.dmasems.append(self)

    def add(self, ins):
        self.cnt += 16
        ins.then_inc(self.sem, 16)
        return Ev(self.sem, self.cnt)


class Kern:
    def __init__(self, debug=None):
        self.debug = debug or {}
        self.nc = bass.Bass("TRN2", target_bir_lowering=False)
        self.ctx = []
        self.dbg_outs = {}
        self.dmasems = []
        self.engs = []
        self.pe_n = 0
        self.phase_log = []

    def MM(self, *a, **kw):
        self.pe_n += 1
        return self.nc.tensor.matmul(*a, **kw)

    def TR(self, *a, **kw):
        self.pe_n += 1
        return self.nc.tensor.transpose(*a, **kw)

    def sb(self, name, shape, dt):
        g = self.nc.sbuf_tensor(name, list(shape), dt)
        t = g.__enter__()
        self.ctx.append(g)
        return t

    def push_scope(self):
        self.ctx.append("SCOPE")

    def pop_scope(self):
        while True:
            g = self.ctx.pop()
            if g == "SCOPE":
                break
            g.__exit__(None, None, None)

    def close(self):
        for g in reversed(self.ctx):
            if g != "SCOPE":
                g.__exit__(None, None, None)
        self.ctx = []

    def din(self, name, shape, dt=F32):
        return self.nc.dram_tensor(name, list(shape), dt, kind="ExternalInput").ap()

    def dout(self, name, shape, dt=F32):
        return self.nc.dram_tensor(name, list(shape), dt, kind="ExternalOutput").ap()

    def dscr(self, name, shape, dt):
        return self.nc.dram_tensor(name, list(shape), dt).ap()


def view(t, off, dims):
    full = t[:]
    pstep = full.ap[0][0]
    npart = full.ap[0][1]
    return bass.AP(full.tensor, off, [[pstep, npart]] + [[s, c] for s, c in dims])


def build(debug=None):
    K = Kern(debug)
    nc = K.nc
    dbg = K.debug

    xp = K.din("xp", [SEQ, D])
    xs = K.din("xs", [NSAMP_TOK, D])
    meta = K.din("meta", [NMETA, D])
    vecs_d = K.din("vecs", [128, NV])
    ssm_small = K.din("ssm_small", [128, 96])
    bre_d = K.din("bre", [128, 512])
    bim_d = K.din("bim", [128, 512])
    cre_d = K.din("cre", [128, 512])
    cim_d = K.din("cim", [128, 512])
    h0re_d = K.din("h0re", [128, 512])
    h0im_d = K.din("h0im", [128, 512])
    sconv_fm = K.din("sconv_fm", [128, NCH * NSEQ * HP])
    sconv_nat = K.din("sconv_nat", [NSEQ * HP, D])
    w_in_d = K.din("w_in", [D, 5 * D])
    w_glu_d = K.din("w_glu", [D, D])
    w_aout_d = K.din("w_a_out", [D, D])
    w_bout_d = K.din("w_b_out", [D, D])
    w_o_d = K.din("w_o", [D, D])
    w_ff1_d = K.din("w_ff1", [D, DFF])
    w_ff2_d = K.din("w_ff2", [DFF, D])

    yp = K.dout("yp", [SEQ, D])
    ys = K.dout("ys", [NSAMP_TOK, D])
    nre_p = K.dout("nre_p", [32, 128])
    nim_p = K.dout("nim_p", [32, 128])
    ncv_p = K.dout("ncv_p", [HP, D])
    nre_s = K.dout("nre_s", [NSEQ * 32, 128])
    nim_s = K.dout("nim_s", [NSEQ * 32, 128])
    ncv_s = K.dout("ncv_s", [NSEQ * HP, D])

    wsc = {
        "in": K.dscr("wsc_in", [D, 5 * D], BF16),
        "glu": K.dscr("wsc_glu", [D, D], BF16),
        "aout": K.dscr("wsc_aout", [D, D], BF16),
        "bout": K.dscr("wsc_bout", [D, D], BF16),
        "o": K.dscr("wsc_o", [D, D], BF16),
        "ff1": K.dscr("wsc_ff1", [D, DFF], BF16),
        "ff2": K.dscr("wsc_ff2", [DFF, D], BF16),
    }
    wsrc = {"in": w_in_d, "glu": w_glu_d, "aout": w_aout_d, "bout": w_bout_d, "o": w_o_d,
            "ff1": w_ff1_d, "ff2": w_ff2_d}
    dsc = K.dscr("dsc", [NCH, 128, 31 * 128], BF16)
    winsc = K.dscr("winsc", [128, 8 * 8 * 2 * 128], BF16)
    cssc = K.dscr("cssc", [128, 32 * 8 * 2 * 32], BF16)

    PE = Eng(K, nc.tensor, "pe")
    ACT = Eng(K, nc.scalar, "act")
    DVE = Eng(K, nc.vector, "dve")
    POOL = Eng(K, nc.gpsimd, "pool")
    SP = Eng(K, nc.sync, "sp")

    IDF = K.sb("IDF", [128, 128], F32)
    IDB = K.sb("IDB", [128, 128], BF16)
    ONESB = K.sb("ONESB", [128, 128], BF16)
    NHALF = K.sb("NHALF", [128, 1], F32)
    VECS = K.sb("VECS", [128, NV], F32)
    DERV = K.sb("DERV", [128, 4, 8], F32)
    ARt = K.sb("ARt", [128, 32, 2], F32)
    AI2t = K.sb("AI2t", [128, 32, 2], F32)
    HCAR = K.sb("HCAR", [128, 32, 2], F32)
    KB = K.sb("KB", [128, 8, 8, 128], BF16)
    WT = K.sb("WT", [128, 2, 8, 2, 128], BF16)
    PSUM_g = nc.psum_tensor("PS", [128, 8, 512], F32)
    PS = PSUM_g.__enter__()
    K.ctx.append(PSUM_g)

    def V(name, k=None):
        o = VOFF[name]
        if k is None:
            return o
        return VECS[:, o + k:o + k + 1]

    class Banks:
        def __init__(self):
            self.nxt = 0
            self.free = [[] for _ in range(8)]
            self.held = set()

        def alloc(self, hold=False):
            b = self.nxt
            while b in self.held:
                b = (b + 1) % 8
            self.nxt = (b + 1) % 8
            PE.wait(self.free[b])
            self.free[b] = []
            if hold:
                self.held.add(b)
            return b

        def release(self, b, *evs):
            self.free[b].extend(evs)
            self.held.discard(b)

    BK = Banks()

    pl = DmaSem(K, "pl")
    outs = DmaSem(K, "outs")
    scr = DmaSem(K, "scr")

    conv_ev = {}

    def conv_dma(key, name, rows, cols):
        s = DmaSem(K, f"cv_{key}")
        ins = nc.gpsimd.dma_start(out=wsc[name][rows[0]:rows[1], cols[0]:cols[1]],
                                  in_=wsrc[name][rows[0]:rows[1], cols[0]:cols[1]])
        conv_ev[key] = s.add(ins)

    ld = []
    ld.append(pl.add(nc.sync.dma_start(out=VECS[:], in_=vecs_d)))
    K.push_scope()
    DG = K.sb("DG", [128, 2, 31, 128], BF16)
    CSF = K.sb("CSF", [128, 8, 32, 2, 2, 16], BF16)
    SSMP = K.sb("SSMP", [128, 96], F32)
    BRE = K.sb("BRE", [128, 32, 16], F32)
    BIM = K.sb("BIM", [128, 32, 16], F32)
    CRE = K.sb("CRE", [128, 32, 16], F32)
    CIM = K.sb("CIM", [128, 32, 16], F32)
    pl.add(nc.sync.dma_start(out=SSMP[:], in_=ssm_small))
    pl.add(nc.sync.dma_start(out=BRE[:], in_=bre_d.rearrange("p (q c) -> p q c", c=16)))
    pl.add(nc.sync.dma_start(out=BIM[:], in_=bim_d.rearrange("p (q c) -> p q c", c=16)))
    pl.add(nc.sync.dma_start(out=CRE[:], in_=cre_d.rearrange("p (q c) -> p q c", c=16)))
    ev_pl = pl.add(nc.sync.dma_start(out=CIM[:], in_=cim_d.rearrange("p (q c) -> p q c", c=16)))

    conv_dma("in0", "in", (0, D), (0, 1024))
    conv_dma("in1", "in", (0, D), (1024, 2048))
    conv_dma("in2", "in", (0, D), (2048, 3072))
    conv_dma("in4", "in", (0, D), (4096, 5120))
    conv_dma("in3", "in", (0, D), (3072, 4096))
    conv_dma("bout", "bout", (0, D), (0, D))
    conv_dma("glu", "glu", (0, D), (0, D))
    conv_dma("aout", "aout", (0, D), (0, D))
    conv_dma("o", "o", (0, D), (0, D))

    IDX = K.sb("IDX", [128, 128], I32)
    POOL.raw.iota(IDX[:], pattern=[[1, 128]], base=0, channel_multiplier=-1)
    POOL.raw.memset(NHALF[:], -0.5)
    e_pc = POOL.mark(POOL.raw.memset(ONESB[:], 1.0 / 1024.0))
    DVE.wait(e_pc)
    DVE.raw.tensor_scalar(out=IDF[:], in0=IDX[:], scalar1=0.0, scalar2=None, op0=ALU.is_equal)
    e_id = DVE.mark(DVE.raw.tensor_scalar(out=IDB[:], in0=IDX[:], scalar1=0.0, scalar2=None, op0=ALU.is_equal))

    DVE.wait(ev_pl)
    g_in = VECS[:, V("ln_in_g"):V("ln_in_g") + 8]
    b_in_ln = VECS[:, V("ln_in_b"):V("ln_in_b") + 8]
    DVE.raw.tensor_scalar(out=DERV[:, 0, :], in0=g_in, scalar1=ALPHA, scalar2=None, op0=ALU.mult)
    DVE.raw.scalar_tensor_tensor(out=DERV[:, 1, :], in0=b_in_ln, scalar=ALPHA,
                                 in1=VECS[:, V("b_o"):V("b_o") + 8], op0=ALU.mult, op1=ALU.add)
    DVE.raw.tensor_scalar(out=DERV[:, 2, :], in0=VECS[:, V("ln1_g"):V("ln1_g") + 8], scalar1=ALPHA,
                          scalar2=None, op0=ALU.mult)
    e_derv = DVE.mark(DVE.raw.scalar_tensor_tensor(
        out=DERV[:, 3, :], in0=VECS[:, V("ln1_b"):V("ln1_b") + 8], scalar=ALPHA,
        in1=VECS[:, V("b_ff2"):V("b_ff2") + 8], op0=ALU.mult, op1=ALU.add))

    SM = K.sb("SM", [128, 24, 32], F32)
    SMI = K.sb("SMI", [128, 2, 32], I32)
    PW = K.sb("PW", [128, 9, 2, 32], F32)
    are = SSMP[:, 0:32]
    aim = SSMP[:, 32:64]
    ldt = SSMP[:, 64:96]
    (DT, ZR, TH, MAG, GS, GC, FS, FCc, SINT, COST, ABR, ABI, DEN, RDEN, ZR1, FR, FI, T1, T2, T3) = \
        [SM[:, i, :] for i in range(20)]

    wpend, rpend = {}, {}

    def _k(ap):
        return (ap.tensor.name, int(ap.offset))

    def dvl(make, out, reads):
        need = []
        for ap in reads:
            if _k(ap) in wpend:
                need.append(wpend[_k(ap)])
        ko = _k(out)
        if ko in wpend:
            need.append(wpend[ko])
        if ko in rpend:
            need.append(rpend[ko])
        DVE.wait(*need)
        e = DVE.mark(make())
        wpend[ko] = e
        for ap in reads:
            rpend[_k(ap)] = e
        return e

    def dv(ins):
        e = DVE.mark(ins)
        DVE.wait(e)
        return e

    def tt(out, a, b, op):
        return dvl(lambda: DVE.raw.tensor_tensor(out=out, in0=a, in1=b, op=op), out, [a, b])

    def ts(out, a, s1, op0, s2=None, op1=None):
        if op1 is None:
            return dvl(lambda: DVE.raw.tensor_scalar(out=out, in0=a, scalar1=s1, scalar2=None, op0=op0), out, [a])
        return dvl(lambda: DVE.raw.tensor_scalar(out=out, in0=a, scalar1=s1, scalar2=s2, op0=op0, op1=op1), out, [a])

    def cp(out, a):
        return dvl(lambda: DVE.raw.tensor_copy(out=out, in_=a), out, [a])

    POOL.wait(ev_pl)
    e = POOL.mark(POOL.raw.memset(T3, math.e))
    POOL.wait(e)
    e = POOL.mark(POOL.raw.tensor_tensor(out=DT, in0=T3, in1=ldt, op=ALU.pow))
    DVE.wait(e)
    tt(ZR, are, DT, ALU.mult)
    e_zr = tt(TH, aim, DT, ALU.mult)
    POOL.wait(e_zr)
    e_mag = POOL.mark(POOL.raw.tensor_tensor(out=MAG, in0=T3, in1=ZR, op=ALU.pow))
    INV2PI = 1.0 / (2.0 * math.pi)
    ts(GS, TH, INV2PI, ALU.mult)
    ts(GC, TH, INV2PI, ALU.mult, 0.25, ALU.add)

    def frac_center(dst, src, ii):
        cp(SMI[:, ii, :], src)
        cp(T1, SMI[:, ii, :])
        tt(dst, src, T1, ALU.subtract)
        ts(T2, dst, 0.5, ALU.is_gt)
        tt(dst, dst, T2, ALU.subtract)
        ts(T2, dst, -0.5, ALU.is_lt)
        return tt(dst, dst, T2, ALU.add)

    frac_center(FS, GS, 0)
    e_f = frac_center(FCc, GC, 1)
    ACT.wait(e_f)
    TWO_PI_SAFE = 6.283185
    ACT.raw.activation(out=SINT, in_=FS, func=AF.Sin, scale=TWO_PI_SAFE)
    e_sc = ACT.mark(ACT.raw.activation(out=COST, in_=FCc, func=AF.Sin, scale=TWO_PI_SAFE))
    DVE.wait(e_sc, e_mag)
    tt(ABR, MAG, COST, ALU.mult)
    tt(ABI, MAG, SINT, ALU.mult)
    tt(DEN, are, are, ALU.mult)
    tt(T1, aim, aim, ALU.mult)
    tt(DEN, DEN, T1, ALU.add)
    dvl(lambda: DVE.raw.reciprocal(out=RDEN, in_=DEN), RDEN, [DEN])
    ts(ZR1, ABR, -1.0, ALU.add)
    tt(T1, ZR1, are, ALU.mult)
    tt(T2, ABI, aim, ALU.mult)
    tt(T1, T1, T2, ALU.add)
    tt(FR, T1, RDEN, ALU.mult)
    tt(T1, ABI, are, ALU.mult)
    tt(T2, ZR1, aim, ALU.mult)
    tt(T1, T1, T2, ALU.subtract)
    tt(FI, T1, RDEN, ALU.mult)
    dvl(lambda: DVE.raw.memset(PW[:, 0, 0, :], 1.0), PW[:, 0, 0, :], [])
    dvl(lambda: DVE.raw.memset(PW[:, 0, 1, :], 0.0), PW[:, 0, 1, :], [])
    cp(PW[:, 1, 0, :], ABR)
    cp(PW[:, 1, 1, :], ABI)
    for s in range(1, 8):
        pr, pi = PW[:, s, 0, :], PW[:, s, 1, :]
        tt(T1, pr, ABR, ALU.mult)
        tt(T2, pi, ABI, ALU.mult)
        tt(PW[:, s + 1, 0, :], T1, T2, ALU.subtract)
        tt(T1, pr, ABI, ALU.mult)
        tt(T2, pi, ABR, ALU.mult)
        tt(PW[:, s + 1, 1, :], T1, T2, ALU.add)
    cp(ARt[:, :, 0], PW[:, 8, 0, :])
    cp(ARt[:, :, 1], PW[:, 8, 0, :])
    cp(AI2t[:, :, 0], PW[:, 8, 1, :])
    ts(AI2t[:, :, 1], PW[:, 8, 1, :], -1.0, ALU.mult)
    e_ar = dv(DVE.raw.memset(T3, 0.0))

    dg_free = [None, None]
    dg_sem = [DmaSem(K, "dg0"), DmaSem(K, "dg1")]

    def emit_diag(c):
        slot = c % 2
        ACT.wait(dg_free[slot], e_id, ev_pl)
        for jt in range(31):
            col = V("conv_w") + jt * 8 + c
            ins = ACT.raw.activation(out=DG[:, slot, jt, :], in_=IDB[:], func=AF.Identity,
                                     scale=VECS[:, col:col + 1])
        e_dg = ACT.mark(ins)
        SP.wait(e_dg)
        dg_free[slot] = dg_sem[slot].add(nc.sync.dma_start(out=dsc[c], in_=DG[:, slot, :, :].rearrange("p t m -> p (t m)")))

    def bq(ap2d):
        return ap2d.unsqueeze(2).broadcast_to([128, 32, 16])

    BBR = K.sb("BBR", [128, 32, 16], F32)
    BBI = K.sb("BBI", [128, 32, 16], F32)
    TA = K.sb("TA", [128, 32, 16], F32)
    TB = K.sb("TB", [128, 32, 16], F32)
    tt(TA[:], BRE[:], bq(FR), ALU.mult)
    tt(TB[:], BIM[:], bq(FI), ALU.mult)
    tt(BBR[:], TA[:], TB[:], ALU.subtract)
    tt(TA[:], BIM[:], bq(FR), ALU.mult)
    tt(TB[:], BRE[:], bq(FI), ALU.mult)
    tt(BBI[:], TA[:], TB[:], ALU.add)

    WP = K.sb("WP", [128, 2, 32, 2, 16], BF16)
    BBP = K.sb("BBP", [128, 32, 2, 2, 16], BF16)
    CP0 = K.sb("CP0", [128, 32, 2, 2, 16], BF16)
    POOL.raw.memset(BBP[:], 0.0)
    e_zb = POOL.mark(POOL.raw.memset(WP[:], 0.0))
    POOL.raw.memset(CP0[:], 0.0)
    POOL.raw.memset(HCAR[:], 0.0)
    POOL.raw.memset(KB[:], 0.0)
    e_z = POOL.mark(POOL.raw.memset(CSF[:], 0.0))
    DVE.wait(e_zb)

    def put_pad(dst_fn, src, negate=False):
        ks = _k(src[:])
        if ks in wpend:
            DVE.wait(wpend[ks])
        last = None
        for gm in range(2):
            sl = slice(64 * gm, 64 * gm + 64)
            if negate:
                last = DVE.raw.tensor_scalar(out=dst_fn(sl, gm), in0=src[sl], scalar1=-1.0, scalar2=None,
                                             op0=ALU.mult)
            else:
                last = DVE.raw.tensor_copy(out=dst_fn(sl, gm), in_=src[sl])
        e = DVE.mark(last)
        rpend[ks] = e
        return e

    put_pad(lambda sl, gm: BBP[sl, :, 0, gm, :], BBR)
    put_pad(lambda sl, gm: BBP[sl, :, 1, gm, :], BBI)

    cs_sem = DmaSem(K, "cssem")
    DVE.wait(e_z)
    put_pad(lambda sl, gm: CP0[sl, :, 0, gm, :], CRE)
    put_pad(lambda sl, gm: CP0[sl, :, 1, gm, :], CIM, negate=True)
    for tp in range(8):
        pr, pi = PW[:, tp + 1, 0, :], PW[:, tp + 1, 1, :]
        tt(TA[:], CRE[:], bq(pr), ALU.mult)
        tt(TB[:], CIM[:], bq(pi), ALU.mult)
        tt(TA[:], TA[:], TB[:], ALU.subtract)
        put_pad(lambda sl, gm: CSF[sl, tp, :, 0, gm, :], TA)
        tt(TA[:], CRE[:], bq(pi), ALU.mult)
        tt(TB[:], CIM[:], bq(pr), ALU.mult)
        tt(TA[:], TA[:], TB[:], ALU.add)
        e_cs = put_pad(lambda sl, gm: CSF[sl, tp, :, 1, gm, :], TA, negate=True)
        SP.wait(e_cs)
        ev_cssc = cs_sem.add(nc.sync.dma_start(
            out=cssc.rearrange("p (t x) -> p t x", t=8)[:, tp, :],
            in_=CSF[:, tp, :, :, :, :].rearrange("p q r g c -> p (q r g c)")))

    wt_free = [None, None]
    wt_sem = [DmaSem(K, "wt0"), DmaSem(K, "wt1")]
    for s in range(8):
        e_ = 7 - s
        pr, pi = PW[:, e_, 0, :], PW[:, e_, 1, :]
        PE_done_prev = None
        tt(TA[:], BBR[:], bq(pr), ALU.mult)
        tt(TB[:], BBI[:], bq(pi), ALU.mult)
        tt(TA[:], TA[:], TB[:], ALU.subtract)
        if s > 0:
            DVE.wait(e_wp_read)
        put_pad(lambda sl, gm: WP[sl, 0, :, gm, :], TA)
        tt(TA[:], BBR[:], bq(pi), ALU.mult)
        tt(TB[:], BBI[:], bq(pr), ALU.mult)
        tt(TA[:], TA[:], TB[:], ALU.add)
        e_wp = put_pad(lambda sl, gm: WP[sl, 1, :, gm, :], TA)
        PE.wait(e_wp)
        slot = s % 2
        bA = BK.alloc()
        bB = BK.alloc()
        for ri in range(2):
            bb = bA if ri == 0 else bB
            pb16 = PS[:, bb, :].bitcast(BF16)
            for j in range(8):
                ins = K.TR(
                    pb16[:, j * 128:(j + 1) * 128],
                    WP[:, ri, 4 * j:4 * j + 4, :, :].rearrange("p q g c -> p (q g c)"), IDB[:])
        e_wp_read = PE.mark(ins)
        ACT.wait(e_wp_read, wt_free[slot])
        for ri in range(2):
            bb = bA if ri == 0 else bB
            pb16 = PS[:, bb, :].bitcast(BF16)
            ins = ACT.raw.activation(out=WT[:, slot, :, ri, :],
                                     in_=pb16.rearrange("p (j m) -> p j m", m=128), func=AF.Copy)
        e_wt = ACT.mark(ins)
        BK.release(bA, e_wt)
        BK.release(bB, e_wt)
        SP.wait(e_wt)
        dst = winsc.rearrange("p (s j r m) -> p s j r m", j=8, s=8, r=2)[:, s, :, :, :]
        wt_free[slot] = wt_sem[slot].add(nc.sync.dma_start(out=dst, in_=WT[:, slot, :, :, :]))
        emit_diag(s)
    ev_winsc = wt_free[1]
    ev_winsc0 = wt_free[0]
    ev_dsc = [dg_free[0], dg_free[1]]

    PE.wait(e_cs, e_id)
    kb_evs = []
    for j in range(8):
        b = BK.alloc() if j % 2 == 0 else b
        base = (j % 2) * 256
        for qq in range(4):
            q = 4 * j + qq
            osl = PS[32 * qq:32 * qq + 32, b, base:base + 256]
            first = (j % 2 == 0)
            for ri in range(2):
                K.MM(osl[:, 0:32], lhsT=BBP[:, q, ri, :, :].rearrange("p g c -> p (g c)"),
                                 rhs=CP0[:, q, ri, :, :].rearrange("p g c -> p (g c)"),
                                 start=(first and ri == 0), stop=False, skip_group_check=True,
                                 tile_position=(0, 32 * qq))
            for ri in range(2):
                ins = K.MM(osl[:, 32:256].rearrange("p (t m) -> p t m", m=32),
                                       lhsT=BBP[:, q, ri, :, :].rearrange("p g c -> p (g c)"),
                                       rhs=CSF[:, 0:7, q, ri, :, :].rearrange("p t g c -> p t (g c)"),
                                       start=False, stop=(ri == 1), skip_group_check=True,
                                       tile_position=(0, 32 * qq))
        if j % 2 == 1:
            e_mm = PE.mark(ins)
            DVE.wait(e_mm)
            for jj in (j - 1, j):
                bs = (jj % 2) * 256
                for qq in range(4):
                    last = DVE.raw.tensor_copy(
                        out=KB[32 * qq:32 * qq + 32, jj, :, 32 * qq:32 * qq + 32],
                        in_=PS[32 * qq:32 * qq + 32, b, bs:bs + 256].rearrange("p (t m) -> p t m", m=32))
            e_kb = dv(last)
            BK.release(b, e_kb)
    for j in range(8):
        last = DVE.raw.scalar_tensor_tensor(out=KB[:, j, 0, :], in0=IDF[:], scalar=V("ssm_d", j),
                                            in1=KB[:, j, 0, :], op0=ALU.mult, op1=ALU.add)
    e_kbd = dv(last)

    if "prologue" in dbg:
        d_pw = K.dout("d_pw", [128, 9 * 2 * 32])
        d_kb = K.dout("d_kb", [128, 8 * 8 * 128], BF16)
        d_f = K.dout("d_f", [128, 2, 32])
        SP.wait(e_kbd, e_ar)
        outs.add(nc.sync.dma_start(out=d_pw, in_=PW[:].rearrange("p s r q -> p (s r q)")))
        outs.add(nc.sync.dma_start(out=d_kb, in_=KB[:].rearrange("p j o m -> p (j o m)")))
        outs.add(nc.sync.dma_start(out=d_f[:, 0, :], in_=FR))
        outs.add(nc.sync.dma_start(out=d_f[:, 1, :], in_=FI))

    conv_dma("ff1a", "ff1", (0, D), (0, 2048))
    conv_dma("ff2a", "ff2", (0, 2048), (0, D))
    conv_dma("ff1b", "ff1", (0, D), (2048, 4096))
    conv_dma("ff2b", "ff2", (2048, 4096), (0, D))

    e_end_dve = DVE.mark(DVE.raw.memset(T3, 0.0))
    for E_ in (PE, ACT, POOL, SP):
        E_.wait(e_end_dve, e_kbd, e_wt)
    K.pop_scope()

    K.prologue_events = dict(cssc=ev_cssc, winsc=[ev_winsc0, ev_winsc], dsc=ev_dsc, conv=conv_ev,
                             derv=e_derv, ar=e_ar, kb=e_kbd, ident=e_id)
    K.handles = dict(PE=PE, ACT=ACT, DVE=DVE, POOL=POOL, SP=SP, BK=BK, PS=PS, outs=outs, V=V,
                     VECS=VECS, DERV=DERV, ARt=ARt, AI2t=AI2t, HCAR=HCAR, KB=KB, IDF=IDF, IDB=IDB,
                     ONESB=ONESB, NHALF=NHALF,
                     dram=dict(xp=xp, xs=xs, meta=meta, h0re=h0re_d, h0im=h0im_d, sconv_fm=sconv_fm,
                               sconv_nat=sconv_nat, yp=yp, ys=ys, nre_p=nre_p, nim_p=nim_p, ncv_p=ncv_p,
                               nre_s=nre_s, nim_s=nim_s, ncv_s=ncv_s, wsc=wsc, dsc=dsc, winsc=winsc,
                               cssc=cssc))
    return K


def main_loop(K, ntiles=5, stop_after=None):
    nc = K.nc
    dbg = K.debug

    def chk(name, ti):
        K.phase_log.append((name, ti, K.pe_n))
        if stop_after is not None and stop_after == (name, ti):
            raise StopBuild()
    h = K.handles
    PE, ACT, DVE, POOL, SP, BK, PS, outs, V = (h[k] for k in ("PE", "ACT", "DVE", "POOL", "SP", "BK", "PS", "outs", "V"))
    VECS, DERV, ARt, AI2t, HCAR, KB, IDF, IDB, ONESB, NHALF = (h[k] for k in (
        "VECS", "DERV", "ARt", "AI2t", "HCAR", "KB", "IDF", "IDB", "ONESB", "NHALF"))
    dr = h["dram"]
    pev = K.prologue_events

    YS = K.sb("YS", [128, 2, D], F32)
    B2 = K.sb("B2", [128, NCH, NTMAX], BF16)
    B5 = K.sb("B5", [128, NCH, NTMAX], BF16)
    G16 = K.sb("G16", [128, 2 * NCH * NTMAX], BF16)
    GLU = K.sb("GLU", [128, NCH, HP + NTMAX], BF16)
    TMPA = K.sb("TMPA", [128, NTMAX], F32)
    TMPB = K.sb("TMPB", [128, NTMAX], F32)
    XS = K.sb("XS", [128, 2, D], F32)
    R32 = K.sb("R32", [128, NCH, NTMAX], F32)
    B0 = K.sb("B0", [128, NCH, NTMAX], BF16)
    B1 = K.sb("B1", [128, NCH, NTMAX], BF16)
    SG = K.sb("SG", [128, 2, NTMAX], BF16)
    XH = K.sb("XH", [128, 32, 2, NKMAX], F32)
    HB = K.sb("HB", [128, 32, 2, NKMAX], BF16)
    SCR = K.sb("SCR", [128, 3, 32, 2], F32)
    WSL = K.sb("WSL", [128, 3, 4096], BF16)
    SSMW = K.sb("SSMW", [128, 16384], BF16)
    ST6 = K.sb("ST6", [128, 2, 12], F32)
    MV = K.sb("MV", [128, 2, 2], F32)
    RS = K.sb("RS", [128, 2, 2], F32)
    OST = K.sb("OST", [128, 2, 128], F32)
    H0S = view(YS, D, [(2 * NSEQ, 32), (NSEQ, 2), (1, NSEQ)])

    WINv = SSMW[:, :].rearrange("p (s j r m) -> p s j r m", j=8, s=8, r=2)
    CSv = SSMW[:, :].rearrange("p (t q r m) -> p t q r m", q=32, t=8, r=2)

    late_guard = [pev["cssc"], pev["dsc"], pev["kb"]]
    POOL.wait(late_guard)
    zero_ev = POOL.mark(POOL.raw.memset(GLU[:, :, 0:HP], 0.0))

    def tile_units():
        u = [("mat", "in", 0, 0), ("mat", "in", 0, 1),
             ("mat", "in", 0, 2), ("mat", "in", 0, 4), ("mat", "in", 0, 3), ("mat", "in", 0, 5)]
        u += [("diag", c) for c in range(8)]
        u += [("mat", "in", 0, 8), ("mat", "in", 0, 9), ("mat", "in", 0, 6), ("mat", "in", 0, 7)]
        u += [("mat", "bout", 0, 0), ("mat", "bout", 0, 1), ("mat", "glu", 0, 0), ("mat", "glu", 0, 1),
              ("mat", "aout", 0, 0), ("mat", "aout", 0, 1), ("mat", "o", 0, 0), ("mat", "o", 0, 1)]
        for hh in range(2):
            u += [("mat", "ff1", 0, 4 * hh + cb) for cb in range(4)]
            for cbo in range(2):
                u += [("mat", "ff2", 2 * hh + kgl, cbo) for kgl in range(2)]
        return u

    class WStream:
        def __init__(self, ntile):
            self.specs = []
            for t in range(ntile):
                self.specs += [(sp, t == 0) for sp in tile_units()]
            self.issued = 0
            self.taken = 0
            self.released = {}
            self.handles = {}
            self.sem = [DmaSem(K, f"w{i}") for i in range(3)]

        def _issue(self, j):
            spec, first_tile = self.specs[j]
            sl = j % 3
            if j >= 3:
                SP.wait(self.released[j - 3])
            if spec[0] == "mat":
                _, name, kg, cb = spec
                if first_tile:
                    if name == "in":
                        key = f"in{cb // 2}"
                    elif name == "ff1":
                        key = "ff1a" if cb < 4 else "ff1b"
                    elif name == "ff2":
                        key = "ff2a" if kg < 2 else "ff2b"
                    else:
                        key = name
                    SP.wait(pev["conv"][key])
                src = dr["wsc"][name][kg * 1024:(kg + 1) * 1024, cb * 512:(cb + 1) * 512].rearrange(
                    "(kc p) n -> p kc n", p=128)
                dst = WSL[:, sl, :].rearrange("p (kc n) -> p kc n", n=512)
            else:
                _, c = spec
                if first_tile:
                    SP.wait(pev["dsc"])
                src = dr["dsc"][c]
                dst = WSL[:, sl, 0:31 * 128]
            ev = self.sem[sl].add(nc.sync.dma_start(out=dst, in_=src))
            self.handles[j] = (j, ev)

        def pump(self, upto):
            while self.issued <= min(upto, len(self.specs) - 1):
                j = self.issued
                if j >= 3 and (j - 3) not in self.released:
                    break
                self._issue(j)
                self.issued += 1

        def load(self, spec, first_tile):
            idx = self.taken
            self.taken += 1
            assert self.specs[idx][0] == spec, (self.specs[idx], spec)
            self.pump(idx + 2)
            assert idx in self.handles
            j, ev = self.handles[idx]
            return j, ev

        def release(self, j, ev):
            self.released[j] = ev
            self.pump(self.taken + 1)

    WS = None

    def wmat(j):
        return WSL[:, j % 3, :].rearrange("p (kc n) -> p kc n", n=512)

    xs_free = [None, None]
    xs_sem = [DmaSem(K, "xs0"), DmaSem(K, "xs1")]
    ys_free = [None, None]
    ssmw_sem = DmaSem(K, "ssmw")
    h0_sem = DmaSem(K, "h0")
    hist_sem = DmaSem(K, "hist")
    ys_sem = [DmaSem(K, "ys0"), DmaSem(K, "ys1")]
    ost_sem = [DmaSem(K, "ost0"), DmaSem(K, "ost1")]
    cvo_sem = DmaSem(K, "cvo")
    ssmw_free = None
    glu_hist_ev = zero_ev
    hcar_ev = None
    state = dict(xs_i=0, ys_i=0)

    tiles = [dict(NT=512, kind="P", tok0=512 * i) for i in range(4)] + [dict(NT=144, kind="S")]
    tiles = tiles[:ntiles] if ntiles < 5 else tiles
    if dbg.get("only_last"):
        tiles = [tiles[-1]]
    WS = WStream(len(tiles))

    def make_subt(T):
        if T["kind"] == "S":
            return [dict(c0=0, R=128, src=[(dr["xs"][0:128, :], 0, 128)], dst=[(dr["ys"][0:128, :], 0, 128)]),
                    dict(c0=128, R=16, src=[(dr["xp"][2032:2048, :], 0, 16)], dst=[(dr["yp"][2032:2048, :], 0, 16)])]
        subt = []
        for m in range(4):
            t0 = T["tok0"] + 128 * m
            if t0 == 0:
                src = [(dr["meta"][0:16, :], 0, 16), (dr["xp"][0:112, :], 16, 128)]
                dst = [(dr["yp"][0:112, :], 16, 128)]
            else:
                src = [(dr["xp"][t0 - 16:t0 + 112, :], 0, 128)]
                dst = [(dr["yp"][t0 - 16:t0 + 112, :], 0, 128)]
            subt.append(dict(c0=128 * m, R=128, src=src, dst=dst))
        return subt

    entry_evs_by_tile = {}
    subts = [make_subt(T) for T in tiles]

    def entry_A(ti_, st):
        c0, R = st["c0"], st["R"]
        sl = state["xs_i"] % 2
        state["xs_i"] += 1
        SP.wait(xs_free[sl])
        for (src, r0, r1) in st["src"]:
            ev_ld = xs_sem[sl].add(nc.sync.dma_start(out=XS[r0:r1, sl, :], in_=src))
        DVE.wait(ev_ld)
        DVE.raw.bn_stats(ST6[0:R, sl, 0:6], XS[0:R, sl, 0:512])
        e = DVE.mark(DVE.raw.bn_stats(ST6[0:R, sl, 6:12], XS[0:R, sl, 512:1024]))
        DVE.wait(e)
        e = DVE.mark(DVE.raw.bn_aggr(MV[0:R, sl, :], ST6[0:R, sl, :]))
        DVE.wait(e)
        e = DVE.mark(DVE.raw.tensor_scalar(out=RS[0:R, sl, 0:1], in0=MV[0:R, sl, 1:2], scalar1=EPS,
                                           scalar2=None, op0=ALU.add))
        ACT.wait(e)
        e = ACT.mark(ACT.raw.activation(out=RS[0:R, sl, 1:2], in_=RS[0:R, sl, 0:1], func=AF.Ln))
        ACT.wait(e)
        e = ACT.mark(ACT.raw.activation(out=RS[0:R, sl, 1:2], in_=RS[0:R, sl, 1:2], func=AF.Exp, scale=-0.5))
        DVE.wait(e)
        e_n = DVE.mark(DVE.raw.tensor_scalar(out=XS[0:R, sl, :], in0=XS[0:R, sl, :], scalar1=MV[0:R, sl, 0:1],
                                             scalar2=RS[0:R, sl, 1:2], op0=ALU.subtract, op1=ALU.mult))
        return dict(sl=sl, e_n=e_n)

    def entry_B(ti_, st, A_, guard_ev):
        XB = B0
        entry_evs = entry_evs_by_tile.setdefault(ti_, [])
        c0, R = st["c0"], st["R"]
        sl, e_n = A_["sl"], A_["e_n"]
        PE.wait(e_n, pev["ident"])
        for half in range(2):
            b = BK.alloc()
            for k4 in range(4):
                kc = half * 4 + k4
                ins = K.TR(PS[:, b, k4 * 128:k4 * 128 + R], XS[0:R, sl, kc * 128:(kc + 1) * 128],
                                          IDF[0:R, 0:R])
            e_t = PE.mark(ins)
            if half == 1:
                xs_free[sl] = e_t
            eng = ACT if half == 0 else DVE
            eng.wait(e_t, pev["derv"], guard_ev)
            for k4 in range(4):
                kc = half * 4 + k4
                src = PS[:, b, k4 * 128:k4 * 128 + R]
                if eng is ACT:
                    ACT.raw.activation(out=XB[:, kc, c0:c0 + R], in_=src, func=AF.Identity,
                                       scale=V("ln_in_g", kc), bias=V("ln_in_b", kc))
                    ins = ACT.raw.activation(out=R32[:, kc, c0:c0 + R], in_=src, func=AF.Identity,
                                             scale=DERV[:, 0, kc:kc + 1], bias=DERV[:, 1, kc:kc + 1])
                else:
                    DVE.raw.tensor_scalar(out=XB[:, kc, c0:c0 + R], in0=src, scalar1=V("ln_in_g", kc),
                                          scalar2=V("ln_in_b", kc), op0=ALU.mult, op1=ALU.add)
                    ins = DVE.raw.tensor_scalar(out=R32[:, kc, c0:c0 + R], in0=src,
                                                scalar1=DERV[:, 0, kc:kc + 1], scalar2=DERV[:, 1, kc:kc + 1],
                                                op0=ALU.mult, op1=ALU.add)
            ee = eng.mark(ins)
            BK.release(b, ee)
            entry_evs.append(ee)

    def issue_win(free_ev, first_):
        SP.wait(free_ev)
        if first_:
            SP.wait(pev["winsc"])
        state["ev_win"] = ssmw_sem.add(nc.sync.dma_start(out=SSMW[:, :], in_=dr["winsc"]))

    issue_win(None, True)
    _A = {}
    for m0, st0 in enumerate(subts[0]):
        if m0 < 2:
            _A[m0] = entry_A(0, st0)
    for m0, st0 in enumerate(subts[0]):
        entry_B(0, st0, _A[m0], None)
        if m0 + 2 < len(subts[0]):
            _A[m0 + 2] = entry_A(0, subts[0][m0 + 2])
    chk("start", 0)
    for ti, T in enumerate(tiles):
        NT = T["NT"]
        NK = NT // TC
        first = (ti == 0)
        isS = T["kind"] == "S"
        if isS:
            GA = view(G16, 0, [(NT, 8), (1, NT)])
            GB = view(G16, 8 * NT, [(NT, 8), (1, NT)])
            HDN = view(G16, 0, [(NT, 16), (1, NT)])
            NSQ2 = NSEQ + 2
            GLUS = view(G16, 16 * NT, [(NSQ2 * 38, 8), (38, NSQ2), (1, 38)])
            GNEW = XS[:, 1, 0:512].bitcast(BF16).rearrange("p (k n) -> p k n", n=128)
        else:
            GA = view(G16, 0, [(NTMAX, 8), (1, NT)])
            GB = view(G16, 8 * NTMAX, [(NTMAX, 8), (1, NT)])
            HDN = view(G16, 0, [(NTMAX, 16), (1, NT)])
            GLUS = None
        XB = B0
        MT = B0
        X1B = B0
        UT = B1
        SBF = B1
        CV = B2
        ZT = B2
        SQ = B2
        CVSQ = B5
        CVN = B5
        ZSG = B5
        subt = subts[ti]

        if isS:
            SP.wait(ys_free[1])
            h0_sem.add(nc.sync.dma_start(out=H0S[:, :, 0, :], in_=dr["h0re"].rearrange("p (q s) -> p q s", s=NSEQ)))
            ev_h0 = h0_sem.add(nc.sync.dma_start(out=H0S[:, :, 1, :], in_=dr["h0im"].rearrange("p (q s) -> p q s", s=NSEQ)))
            POOL.wait(state.get("g16_free"))
            scf = dr["sconv_fm"].rearrange("p (k s r) -> p k s r", k=NCH, s=NSEQ)
            for k2 in range(NCH):
                ev_hist = hist_sem.add(nc.gpsimd.dma_start(out=GLUS[:, k2, 0:NSEQ, 0:HP], in_=scf[:, k2, :, :]))
            outs.add(nc.sync.dma_start(
                out=dr["ncv_s"].rearrange("(s r) d -> s r d", r=HP)[:, 0:HP - TC, :],
                in_=dr["sconv_nat"].rearrange("(s r) d -> s r d", r=HP)[:, TC:HP, :]))

        entry_evs = entry_evs_by_tile[ti]
        xb_ready = list(entry_evs)
        chk("entry", ti)

        def proj_unit(spec, rhs_fn, nk, evac_fn, rhs_wait, oc0, kc0=0, bank_of=None, start0=True, stop_last=True,
                      release=True):
            sl, ev_w = WS.load(spec, first)
            PE.wait(ev_w, rhs_wait)
            wm = wmat(sl)
            last = None
            for o4 in range(4):
                if bank_of is None:
                    b = BK.alloc()
                else:
                    b = bank_of(o4)
                for kc in range(nk):
                    last = K.MM(PS[:, b, 0:NT], lhsT=wm[:, kc, o4 * 128:(o4 + 1) * 128],
                                            rhs=rhs_fn(kc0 + kc), start=(start0 and kc == 0),
                                            stop=(stop_last and kc == nk - 1))
                if evac_fn is not None:
                    e_mm = PE.mark(last)
                    evac_fn(oc0 + o4, b, e_mm)
            e_rel = PE.mark(last) if evac_fn is None else e_mm
            WS.release(sl, e_rel)
            return e_rel

        ut_evs = []

        def evac_u(oc, b, e_mm):
            ACT.wait(e_mm)
            e = ACT.mark(ACT.raw.activation(out=UT[:, oc, 0:NT], in_=PS[:, b, 0:NT], func=AF.Identity,
                                            bias=V("b_in", oc), scale=1.0))
            BK.release(b, e)
            ut_evs.append(e)

        ACT.wait(state.get("b1_free"))
        for cb in range(2):
            proj_unit(("mat", "in", 0, cb), lambda kc: XB[:, kc, 0:NT], 8, evac_u, xb_ready, oc0=4 * cb)

        chk("inproj_u", ti)
        ev_win = state["ev_win"]
        PE.wait(ev_win, ut_evs)
        x_evs = []
        for jr in range(2):
            banks = [BK.alloc() for _ in range(4)]
            fst = [True] * 4
            for j4 in range(4):
                j = jr * 4 + j4
                for ri in range(2):
                    col = (j4 * 2 + ri) * NK
                    for s_ in range(TC):
                        for qq in range(4):
                            rows = slice(32 * qq, 32 * qq + 32)
                            last = K.MM(
                                PS[:, banks[qq], col:col + NK], lhsT=WINv[rows, s_, j, ri, :],
                                rhs=UT[rows, j, s_:NT:TC], start=fst[qq], stop=(s_ == TC - 1),
                                skip_group_check=True, tile_position=(32 * qq, 0))
                            fst[qq] = False
            e_mm = PE.mark(last)
            XHv = XH[:, :, :, :].rearrange("p (j q) r k -> p j q r k", q=4)
            for qq in range(4):
                eng = ACT if qq % 2 == 0 else DVE
                eng.wait(e_mm, state.get("xh_free"))
                src = PS[:, banks[qq], 0:8 * NK].rearrange("p (j r k) -> p j r k", j=4, r=2)
                dst = XHv[:, jr * 4:jr * 4 + 4, qq, :, 0:NK]
                if eng is ACT:
                    e = ACT.mark(ACT.raw.activation(out=dst, in_=src, func=AF.Copy))
                else:
                    e = DVE.mark(DVE.raw.tensor_copy(out=dst, in_=src))
                BK.release(banks[qq], e)
                x_evs.append(e)
        ssmw_free = e_mm
        SP.wait(ssmw_free)
        if first:
            SP.wait(pev["cssc"])
        ev_cs = ssmw_sem.add(nc.sync.dma_start(out=SSMW[:, :], in_=dr["cssc"]))

        chk("ssm_in", ti)
        POOL.wait(x_evs, hcar_ev, pev["ar"], state.get("hb_free"))

        def pw(ins):
            e = POOL.mark(ins)
            POOL.wait(e)
            return e

        def scan_step(hprev, xk_view):
            m1, m2, t_ = SCR[:, 0], SCR[:, 1], SCR[:, 2]
            POOL.raw.tensor_tensor(out=m1, in0=hprev, in1=ARt[:], op=ALU.mult)
            pw(POOL.raw.tensor_tensor(out=m2, in0=hprev, in1=AI2t[:], op=ALU.mult))
            pw(POOL.raw.tensor_tensor(out=t_, in0=m1, in1=xk_view, op=ALU.add))
            return pw(POOL.raw.tensor_tensor(out=xk_view, in0=t_, in1=m2[:, :, ::-1], op=ALU.add))

        if isS:
            POOL.wait(ev_h0)
            pw(POOL.raw.tensor_copy(out=HB[:, :, :, 0:NSEQ], in_=H0S))
            for s_ in range(NSEQ):
                scan_step(H0S[:, :, :, s_], XH[:, :, :, s_])
            pk0 = NSEQ
        else:
            pk0 = 0
        pw(POOL.raw.tensor_copy(out=HB[:, :, :, pk0], in_=HCAR[:]))
        for k in range(pk0, NK):
            hprev = HCAR[:] if k == pk0 else XH[:, :, :, k - 1]
            e_sc = scan_step(hprev, XH[:, :, :, k])
        hcar_ev = pw(POOL.raw.tensor_copy(out=HCAR[:], in_=XH[:, :, :, NK - 1]))
        scan_done = hcar_ev

        chk("scan", ti)
        if first:
            ACT.wait(late_guard)
            DVE.wait(late_guard)
            SP.wait(late_guard)
        glu_evs = [None] * 8
        for hh in range(2):
            slA, evA = WS.load(("mat", "in", 0, 2 + hh), first)
            slB, evB = WS.load(("mat", "in", 0, 4 + hh), first)
            PE.wait(evA, evB)
            for c4 in range(4):
                c = hh * 4 + c4
                bA = BK.alloc()
                for kc in range(8):
                    la = K.MM(PS[:, bA, 0:NT], lhsT=wmat(slA)[:, kc, c4 * 128:(c4 + 1) * 128],
                                          rhs=XB[:, kc, 0:NT], start=(kc == 0), stop=(kc == 7))
                bB = BK.alloc()
                for kc in range(8):
                    lb = K.MM(PS[:, bB, 0:NT], lhsT=wmat(slB)[:, kc, c4 * 128:(c4 + 1) * 128],
                                          rhs=XB[:, kc, 0:NT], start=(kc == 0), stop=(kc == 7))
                e_mm = PE.mark(lb)
                sgs = SG[:, c % 2, 0:NT]
                ACT.wait(e_mm, state.get(f"sg_free{c % 2}"))
                e_sg = ACT.mark(ACT.raw.activation(out=sgs, in_=PS[:, bB, 0:NT], func=AF.Sigmoid,
                                                   bias=V("b_in", 16 + c), scale=1.0))
                DVE.wait(e_sg, glu_hist_ev)
                if isS:
                    DVE.wait(ev_hist, xs_free[1])
                    e_gn = DVE.mark(DVE.raw.scalar_tensor_tensor(
                        out=GNEW[:, c, :], in0=PS[:, bA, 0:128], scalar=V("b_in", 8 + c), in1=SG[:, c % 2, 0:128],
                        op0=ALU.add, op1=ALU.mult))
                    e_gp = DVE.mark(DVE.raw.scalar_tensor_tensor(
                        out=GLU[:, c, HP:HP + 16], in0=PS[:, bA, 128:144], scalar=V("b_in", 8 + c),
                        in1=SG[:, c % 2, 128:144], op0=ALU.add, op1=ALU.mult))
                    DVE.wait(e_gn, e_gp)
                    DVE.raw.tensor_copy(out=GLUS[:, c, 0:NSEQ, HP:HP + TC],
                                        in_=GNEW[:, c, :].rearrange("p (s t) -> p s t", t=TC))
                    ins = DVE.raw.tensor_copy(out=GLUS[:, c, NSEQ:NSEQ + 2, :],
                                              in_=view(GLU, c * (HP + NTMAX), [(TC, 2), (1, 38)]))
                else:
                    ins = DVE.raw.scalar_tensor_tensor(
                        out=GLU[:, c, HP:HP + NT], in0=PS[:, bA, 0:NT], scalar=V("b_in", 8 + c), in1=sgs,
                        op0=ALU.add, op1=ALU.mult)
                e_g = DVE.mark(ins)
                state[f"sg_free{c % 2}"] = e_g
                BK.release(bA, e_g)
                BK.release(bB, e_sg)
                glu_evs[c] = e_g
            WS.release(slA, e_mm)
            WS.release(slB, e_mm)

        chk("glu", ti)
        cv_evs = []
        conv_mm = None
        ACT.wait(state.get("b2_free"), state.get("b5_free"))
        for c in range(8):
            sl, ev_w = WS.load(("diag", c), first)
            PE.wait(ev_w, glu_evs[c])
            dg = WSL[:, sl % 3, 0:31 * 128].rearrange("p (t m) -> p t m", m=128)
            b = BK.alloc()
            for jt in range(31):
                if isS:
                    last = K.MM(PS[:, b, 0:NT], lhsT=dg[:, jt, :], rhs=GLUS[:, c, :, jt:jt + TC],
                                start=(jt == 0), stop=(jt == 30))
                else:
                    last = K.MM(PS[:, b, 0:NT], lhsT=dg[:, jt, :], rhs=GLU[:, c, jt:jt + NT],
                                            start=(jt == 0), stop=(jt == 30))
            e_mm = PE.mark(last)
            conv_mm = e_mm
            WS.release(sl, e_mm)
            ACT.wait(e_mm)
            ACT.raw.activation(out=CV[:, c, 0:NT], in_=PS[:, b, 0:NT], func=AF.Identity, bias=V("conv_b", c),
                               scale=1.0)
            e = ACT.mark(ACT.raw.activation(out=CVSQ[:, c, 0:NT], in_=PS[:, b, 0:NT], func=AF.Square,
                                            bias=V("conv_b", c), scale=1.0))
            BK.release(b, e)
            cv_evs.append(e)

        chk("conv", ti)
        if isS:
            tr_evs = []
            for half in range(2):
                b = BK.alloc()
                pb16 = PS[:, b, :].bitcast(BF16)
                for k4 in range(4):
                    kc = half * 4 + k4
                    K.TR(pb16[0:128, k4 * 128:(k4 + 1) * 128], GNEW[:, kc, :], IDB[:])
                    ins = K.TR(pb16[0:HP, 512 + k4 * 128:512 + (k4 + 1) * 128],
                                              GLU[:, kc, 16:16 + HP], IDB[:])
                e_t = PE.mark(ins)
                ACT.wait(e_t, ys_free[0])
                ACT.raw.activation(out=YS[0:128, 0, half * 512:(half + 1) * 512], in_=pb16[0:128, 0:512],
                                   func=AF.Copy)
                e = ACT.mark(ACT.raw.activation(out=XS[0:HP, 0, half * 512:(half + 1) * 512],
                                                in_=pb16[0:HP, 512:1024], func=AF.Copy))
                BK.release(b, e)
                tr_evs.append(e)
            SP.wait(tr_evs)
            ncs = dr["ncv_s"].rearrange("(s r) d -> s r d", r=HP)
            for s_ in range(NSEQ):
                ev_o = ys_sem[0].add(nc.sync.dma_start(out=ncs[s_, HP - TC:HP, :], in_=YS[s_ * TC:(s_ + 1) * TC, 0, :]))
            ys_free[0] = ev_o
            xs_free[0] = cvo_sem.add(nc.sync.dma_start(out=dr["ncv_p"], in_=XS[0:HP, 0, :]))
        else:
            POOL.wait(conv_mm)
            glu_hist_ev = POOL.mark(POOL.raw.tensor_copy(out=GLU[:, :, 0:HP], in_=GLU[:, :, NT:NT + HP]))

        def ln_stats(src_bf, sq_bf, ready_evs):
            bM = BK.alloc(hold=True)
            bQ = BK.alloc(hold=True)
            for kc in range(8):
                PE.wait(ready_evs[kc])
                K.MM(PS[:, bM, 0:NT], lhsT=ONESB[:], rhs=src_bf[:, kc, 0:NT], start=(kc == 0),
                     stop=(kc == 7))
                last = K.MM(PS[:, bQ, 0:NT], lhsT=ONESB[:], rhs=sq_bf[:, kc, 0:NT], start=(kc == 0),
                            stop=(kc == 7))
            e_mm = PE.mark(last)
            ACT.wait(e_mm, state.get("tmp_free"))
            e = ACT.mark(ACT.raw.activation(out=TMPA[:, 0:NT], in_=PS[:, bM, 0:NT], func=AF.Square))
            DVE.wait(e)
            e = DVE.mark(DVE.raw.scalar_tensor_tensor(out=TMPB[:, 0:NT], in0=PS[:, bQ, 0:NT], scalar=EPS,
                                                      in1=TMPA[:, 0:NT], op0=ALU.add, op1=ALU.subtract))
            ACT.wait(e)
            e = ACT.mark(ACT.raw.activation(out=TMPB[:, 0:NT], in_=TMPB[:, 0:NT], func=AF.Ln))
            ACT.wait(e)
            e = ACT.mark(ACT.raw.activation(out=TMPA[:, 0:NT], in_=TMPB[:, 0:NT], func=AF.Exp, scale=-0.5))
            DVE.wait(e)
            e = DVE.mark(DVE.raw.scalar_tensor_tensor(out=PS[:, bQ, 0:NT], in0=PS[:, bM, 0:NT], scalar=-1.0,
                                                      in1=TMPA[:, 0:NT], op0=ALU.mult, op1=ALU.mult))
            DVE.wait(e)
            e = DVE.mark(DVE.raw.tensor_copy(out=PS[:, bM, 0:NT], in_=TMPA[:, 0:NT]))
            DVE.wait(e)
            return bM, bQ, e

        bR, bN, e_st = ln_stats(CV, CVSQ, cv_evs)
        chk("convln", ti)
        ga_evs, gb_evs = [], []

        def evac_gate(dst, lst, boff):
            def f(oc, b, e_mm):
                ACT.wait(e_mm)
                e = ACT.mark(ACT.raw.activation(out=dst[:, oc, :], in_=PS[:, b, 0:NT], func=AF.Sigmoid,
                                                bias=V("b_in", boff + oc), scale=1.0))
                BK.release(b, e)
                lst.append(e)
            return f

        ACT.wait(state.get("g16_free"))
        for cb in range(2):
            proj_unit(("mat", "in", 0, 8 + cb), lambda kc: XB[:, kc, 0:NT], 8, evac_gate(GB, gb_evs, 32), None,
                      oc0=4 * cb)
        for cb in range(2):
            e_xb_done = proj_unit(("mat", "in", 0, 6 + cb), lambda kc: XB[:, kc, 0:NT], 8,
                                  evac_gate(GA, ga_evs, 24), None, oc0=4 * cb)

        cvn_evs = []
        for c in range(8):
            tmp = TMPA if c % 2 == 0 else TMPB
            DVE.wait(state.get(f"tmpn_free{c % 2}"))
            e = DVE.mark(DVE.raw.tensor_tensor(out=tmp[:, 0:NT], in0=CV[:, c, 0:NT], in1=PS[:, bR, 0:NT], op=ALU.mult))
            DVE.wait(e)
            e = DVE.mark(DVE.raw.tensor_tensor(out=tmp[:, 0:NT], in0=tmp[:, 0:NT], in1=PS[:, bN, 0:NT], op=ALU.add))
            ACT.wait(e)
            e = ACT.mark(ACT.raw.activation(out=CVN[:, c, 0:NT], in_=tmp[:, 0:NT], func=AF.Silu,
                                            scale=V("conv_ln_g", c), bias=V("conv_ln_b", c)))
            state[f"tmpn_free{c % 2}"] = e
            cvn_evs.append(e)
        BK.release(bR, e)
        BK.release(bN, e)
        state["tmp_free"] = e

        chk("gates", ti)
        mt_evs = [None] * 8

        def evac_pb(oc, b, e_mm):
            DVE.wait(e_mm, gb_evs, e_xb_done)
            e = DVE.mark(DVE.raw.scalar_tensor_tensor(out=MT[:, oc, 0:NT], in0=PS[:, b, 0:NT],
                                                      scalar=V("b_b_out", oc), in1=GB[:, oc, :],
                                                      op0=ALU.add, op1=ALU.mult))
            BK.release(b, e)
            mt_evs[oc] = e

        for cb in range(2):
            e_cvn_done = proj_unit(("mat", "bout", 0, cb), lambda kc: CVN[:, kc, 0:NT], 8, evac_pb, cvn_evs, oc0=4 * cb)

        chk("wbout", ti)
        hb_ev = scan_done
        if NK - pk0 > 1:
            DVE.wait(scan_done)
            hb_ev = DVE.mark(DVE.raw.tensor_copy(out=HB[:, :, :, pk0 + 1:NK], in_=XH[:, :, :, pk0:NK - 1]))
        PE.wait(ev_cs, scan_done, hb_ev, pev["kb"])
        zt_evs = []
        ACT.wait(cvn_evs)
        for j in range(8):
            b = BK.alloc()
            fst = True
            utv = UT[:, j, 0:NT].rearrange("p (k s) -> p s k", s=TC)
            for off in range(TC):
                ns = TC - off
                K.MM(PS[:, b, off * NK:TC * NK], lhsT=KB[:, j, off, :], rhs=utv[:, 0:ns, :],
                     start=fst, stop=False, skip_group_check=True)
                fst = False
            for tp in range(TC):
                for qq in range(4):
                    q = 4 * j + qq
                    for ri in range(2):
                        last = K.MM(PS[32 * qq:32 * qq + 32, b, tp * NK:(tp + 1) * NK],
                                                lhsT=CSv[:, tp, q, ri, :], rhs=HB[:, q, ri, 0:NK], start=False,
                                                stop=(tp == TC - 1 and qq == 3 and ri == 1),
                                                skip_group_check=True, tile_position=(0, 32 * qq))
            e_mm = PE.mark(last)
            ACT.wait(e_mm)
            e = ACT.mark(ACT.raw.activation(
                out=ZT[:, j, 0:NT].rearrange("p (k t) -> p k t", t=TC),
                in_=PS[:, b, 0:NT].rearrange("p (t k) -> p k t", t=TC), func=AF.Gelu_apprx_tanh))
            BK.release(b, e)
            zt_evs.append(e)
        ssmw_free = e_mm
        if ti + 1 < len(tiles):
            issue_win(ssmw_free, False)
        state["hb_free"] = e_mm
        state["xh_free"] = [scan_done, hb_ev]
        ut_done = e_mm

        chk("ssm_out", ti)
        if isS:
            DVE.wait(scan_done, state.get("tmpn_free0"), state.get("tmpn_free1"))
            for ri in range(2):
                dst_s = dr["nre_s"] if ri == 0 else dr["nim_s"]
                dst_p = dr["nre_p"] if ri == 0 else dr["nim_p"]
                for g4 in range(5):
                    b = BK.alloc()
                    if g4 < 4:
                        e = DVE.mark(DVE.raw.tensor_copy(
                            out=TMPA[:, 0:128].rearrange("p (s q) -> p s q", q=32),
                            in_=XH[:, :, ri, 4 * g4:4 * g4 + 4].rearrange("p q s -> p s q")))
                        ncol = 128
                    else:
                        e = DVE.mark(DVE.raw.tensor_copy(out=TMPA[:, 0:32], in_=XH[:, :, ri, NK - 1]))
                        ncol = 32
                    PE.wait(e)
                    e_t = PE.mark(K.TR(PS[0:ncol, b, 0:128], TMPA[:, 0:ncol], IDF[:]))
                    osl = state.get("ost_i", 0) % 2
                    state["ost_i"] = state.get("ost_i", 0) + 1
                    DVE.wait(e_t, state.get(f"ost_free{osl}"))
                    e = DVE.mark(DVE.raw.tensor_copy(out=OST[0:ncol, osl, :], in_=PS[0:ncol, b, 0:128]))
                    BK.release(b, e)
                    SP.wait(e)
                    if g4 < 4:
                        ev_o = ost_sem[osl].add(nc.sync.dma_start(out=dst_s[128 * g4:128 * (g4 + 1), :], in_=OST[:, osl, :]))
                    else:
                        ev_o = ost_sem[osl].add(nc.sync.dma_start(out=dst_p, in_=OST[0:32, osl, :]))
                    state[f"ost_free{osl}"] = ev_o
            state["tmp_free2"] = e_t

        chk("stateout", ti)
        za_evs = [None] * 8

        def evac_glu(oc, b, e_mm):
            ACT.wait(e_mm, e_cvn_done)
            e = ACT.mark(ACT.raw.activation(out=ZSG[:, oc, 0:NT], in_=PS[:, b, 0:NT], func=AF.Sigmoid,
                                            bias=V("b_glu", oc), scale=1.0))
            BK.release(b, e)
            DVE.wait(e)
            e2 = DVE.mark(DVE.raw.tensor_tensor(out=ZSG[:, oc, 0:NT], in0=ZSG[:, oc, 0:NT], in1=ZT[:, oc, 0:NT],
                                                op=ALU.mult))
            za_evs[oc] = e2

        for cb in range(2):
            e_zt_done = proj_unit(("mat", "glu", 0, cb), lambda kc: ZT[:, kc, 0:NT], 8, evac_glu, zt_evs, oc0=4 * cb)

        chk("wglu", ti)
        mt2_evs = [None] * 8

        def evac_pa(oc, b, e_mm):
            tmp = TMPA if oc % 2 == 0 else TMPB
            DVE.wait(e_mm, ga_evs, mt_evs[oc], state.get("tmp_free2"))
            e = DVE.mark(DVE.raw.tensor_tensor(out=tmp[:, 0:NT], in0=PS[:, b, 0:NT], in1=GA[:, oc, :], op=ALU.mult))
            BK.release(b, e)
            DVE.wait(e)
            e2 = DVE.mark(DVE.raw.tensor_tensor(out=MT[:, oc, 0:NT], in0=tmp[:, 0:NT], in1=MT[:, oc, 0:NT],
                                                op=ALU.add))
            DVE.wait(e2)
            mt2_evs[oc] = e2

        for cb in range(2):
            e_za_done = proj_unit(("mat", "aout", 0, cb), lambda kc: ZSG[:, kc, 0:NT], 8, evac_pa, za_evs, oc0=4 * cb)
        state["b5_free"] = e_za_done

        chk("waout", ti)
        s1_evs = []

        def evac_o(oc, b, e_mm):
            DVE.wait(e_mm)
            e = DVE.mark(DVE.raw.tensor_tensor(out=R32[:, oc, 0:NT], in0=PS[:, b, 0:NT], in1=R32[:, oc, 0:NT],
                                               op=ALU.add))
            BK.release(b, e)
            s1_evs.append(e)

        for cb in range(2):
            e_mt_done = proj_unit(("mat", "o", 0, cb), lambda kc: MT[:, kc, 0:NT], 8, evac_o, mt2_evs, oc0=4 * cb)

        chk("wo", ti)
        def layer_norm_r32(ready, pre_free):
            ACT.wait(pre_free)
            evs = []
            for kc in range(8):
                ACT.wait(ready[kc])
                ACT.raw.activation(out=SBF[:, kc, 0:NT], in_=R32[:, kc, 0:NT], func=AF.Copy)
                evs.append(ACT.mark(ACT.raw.activation(out=SQ[:, kc, 0:NT], in_=R32[:, kc, 0:NT], func=AF.Square)))
            bR_, bN_, e_st_ = ln_stats(SBF, SQ, evs)
            nevs = []
            for kc in range(8):
                e = DVE.mark(DVE.raw.tensor_tensor(out=R32[:, kc, 0:NT], in0=R32[:, kc, 0:NT], in1=PS[:, bR_, 0:NT],
                                                   op=ALU.mult))
                DVE.wait(e)
                e = DVE.mark(DVE.raw.tensor_tensor(out=R32[:, kc, 0:NT], in0=R32[:, kc, 0:NT], in1=PS[:, bN_, 0:NT],
                                                   op=ALU.add))
                DVE.wait(e)
                nevs.append(e)
            BK.release(bR_, e)
            BK.release(bN_, e)
            state["tmp_free"] = e
            return nevs

        nevs = layer_norm_r32(s1_evs, [ut_done, e_zt_done])
        x1_evs = []
        for kc in range(8):
            ACT.wait(nevs[kc], e_mt_done)
            e = ACT.mark(ACT.raw.activation(out=X1B[:, kc, 0:NT], in_=R32[:, kc, 0:NT], func=AF.Identity,
                                            scale=V("ln1_g", kc), bias=V("ln1_b", kc)))
            ACT.wait(e)
            e = ACT.mark(ACT.raw.activation(out=R32[:, kc, 0:NT], in_=R32[:, kc, 0:NT], func=AF.Identity,
                                            scale=DERV[:, 2, kc:kc + 1], bias=DERV[:, 3, kc:kc + 1]))
            x1_evs.append(e)

        chk("ln1", ti)
        s2_evs = {}
        for hh in range(2):
            hd_evs = []

            def evac_ff1(oc, b, e_mm):
                ol = oc - 16 * hh
                ACT.wait(e_mm, state.get("hdn_free"), ga_evs, gb_evs)
                e = ACT.mark(ACT.raw.activation(out=HDN[:, ol, :], in_=PS[:, b, 0:NT], func=AF.Relu,
                                                bias=V("b_ff1", oc), scale=1.0))
                BK.release(b, e)
                POOL.wait(e)
                e2 = POOL.mark(POOL.raw.tensor_tensor(out=HDN[:, ol, :], in0=HDN[:, ol, :], in1=HDN[:, ol, :],
                                                      op=ALU.mult))
                hd_evs.append(e2)

            for cb in range(4):
                proj_unit(("mat", "ff1", 0, 4 * hh + cb), lambda kc: X1B[:, kc, 0:NT], 8, evac_ff1,
                          x1_evs + [e_za_done] if (hh == 0 and cb == 0) else None, oc0=16 * hh + 4 * cb)
            for cbo in range(2):
                banks = [BK.alloc() for _ in range(4)]
                for kgl in range(2):
                    def evac_ff2(oc, b, e_mm):
                        DVE.wait(e_mm, x1_evs)
                        e = DVE.mark(DVE.raw.tensor_tensor(out=R32[:, oc, 0:NT], in0=PS[:, b, 0:NT],
                                                           in1=R32[:, oc, 0:NT], op=ALU.add))
                        BK.release(b, e)
                        DVE.wait(e)
                        s2_evs[(hh, oc)] = e
                    e_h = proj_unit(("mat", "ff2", 2 * hh + kgl, cbo), lambda kc: HDN[:, kc, :], 8,
                                    evac_ff2 if kgl == 1 else None, hd_evs, oc0=4 * cbo, kc0=8 * kgl,
                                    bank_of=lambda o4: banks[o4], start0=(kgl == 0), stop_last=(kgl == 1))
            state["hdn_free"] = e_h
        state["g16_free"] = e_h

        chk("ffn", ti)
        nevs = layer_norm_r32([s2_evs[(1, kc)] for kc in range(8)], None)
        y_evs = []
        for kc in range(8):
            ACT.wait(nevs[kc])
            e = ACT.mark(ACT.raw.activation(out=R32[:, kc, 0:NT], in_=R32[:, kc, 0:NT], func=AF.Identity,
                                            scale=V("ln2_g", kc), bias=V("ln2_b", kc)))
            y_evs.append(e)
        state["b1_free"] = nevs[-1]
        state["b2_free"] = nevs[-1]
        nxt_i = 0
        nA = {}
        if ti + 1 < len(tiles):
            for m_ in range(min(2, len(subts[ti + 1]))):
                nA[m_] = entry_A(ti + 1, subts[ti + 1][m_])
        for st in subt:
            c0, R = st["c0"], st["R"]
            sl = state["ys_i"] % 2
            state["ys_i"] += 1
            PE.wait(y_evs)
            for half in range(2):
                b = BK.alloc()
                for k4 in range(4):
                    kc = half * 4 + k4
                    ins = K.TR(PS[0:R, b, k4 * 128:(k4 + 1) * 128], R32[:, kc, c0:c0 + R], IDF[:])
                e_t = PE.mark(ins)
                eng = ACT if half == 0 else DVE
                eng.wait(e_t, ys_free[sl])
                if eng is ACT:
                    e = ACT.mark(ACT.raw.activation(out=YS[0:R, sl, half * 512:(half + 1) * 512],
                                                    in_=PS[0:R, b, :], func=AF.Copy))
                else:
                    e = DVE.mark(DVE.raw.tensor_copy(out=YS[0:R, sl, half * 512:(half + 1) * 512],
                                                     in_=PS[0:R, b, :]))
                BK.release(b, e)
                SP.wait(e)
            for (dst, r0, r1) in st["dst"]:
                ev_o = ys_sem[sl].add(nc.sync.dma_start(out=dst, in_=YS[r0:r1, sl, :]))
            ys_free[sl] = ev_o
            if ti + 1 < len(tiles):
                done_cols = c0 + R
                while nxt_i < len(subts[ti + 1]) and subts[ti + 1][nxt_i]["c0"] + subts[ti + 1][nxt_i]["R"] <= done_cols:
                    entry_B(ti + 1, subts[ti + 1][nxt_i], nA[nxt_i], e_t)
                    if nxt_i + 2 < len(subts[ti + 1]):
                        nA[nxt_i + 2] = entry_A(ti + 1, subts[ti + 1][nxt_i + 2])
                    nxt_i += 1
        if ti + 1 < len(tiles):
            while nxt_i < len(subts[ti + 1]):
                entry_B(ti + 1, subts[ti + 1][nxt_i], nA[nxt_i], e_t)
                if nxt_i + 2 < len(subts[ti + 1]):
                    nA[nxt_i + 2] = entry_A(ti + 1, subts[ti + 1][nxt_i + 2])
                nxt_i += 1

def finish(K):
    h = K.handles
    SP = h["SP"]
    outs = h["outs"]
    for E_ in K.engs:
        if E_ is not SP and E_.cnt > 0:
            SP.raw.wait_ge(E_.sem, E_.cnt)
    for ds in K.dmasems:
        if ds.cnt > 0:
            SP.raw.wait_ge(ds.sem, ds.cnt)
    SP.raw.wait_ge(outs.sem, outs.cnt)
    K.close()
    return K.nc


def _pack_vecs(inp):
    cols = []
    for name, n in VEC_SPECS:
        a = np.asarray(inp[name], np.float32)
        if name == "conv_w":
            a = a.reshape(31, 8, 128).transpose(2, 0, 1).reshape(128, 31 * 8)
        else:
            a = a.reshape(n, 128).T
        cols.append(a)
    return np.ascontiguousarray(np.concatenate(cols, axis=1), dtype=np.float32)


def _sl(a):
    return np.asarray(a, np.float32).reshape(32, 2, 64).transpose(1, 2, 0).reshape(128, 32)


def host_prep(inp):
    f32 = np.float32
    sh = {}
    sh["meta"] = np.ascontiguousarray(inp["meta_tokens"], f32)
    sh["vecs"] = _pack_vecs(inp)
    ldt = np.asarray(inp["ssm_log_dt"], f32).reshape(32, 2)
    ldt_sl = np.broadcast_to(ldt.T[:, None, :], (2, 64, 32)).reshape(128, 32)
    sh["ssm_small"] = np.ascontiguousarray(
        np.concatenate([_sl(inp["ssm_a_re"][0]), _sl(inp["ssm_a_im"][0]), ldt_sl], axis=1), f32)
    for nm, key in (("bre", "ssm_b_re"), ("bim", "ssm_b_im")):
        a = np.asarray(inp[key][0], f32).reshape(32, 2, 64, 16).transpose(1, 2, 0, 3).reshape(128, 512)
        sh[nm] = np.ascontiguousarray(a)
    for nm, key in (("cre", "ssm_c_re"), ("cim", "ssm_c_im")):
        a = np.asarray(inp[key][0], f32).reshape(32, 2, 16, 64).transpose(1, 3, 0, 2).reshape(128, 512)
        sh[nm] = np.ascontiguousarray(a)
    for nm, key in (("w_in", "w_in"), ("w_glu", "w_glu"), ("w_a_out", "w_a_out"), ("w_b_out", "w_b_out"),
                    ("w_o", "w_o"), ("w_ff1", "w_ff1"), ("w_ff2", "w_ff2")):
        sh[nm] = np.ascontiguousarray(inp[key][0], f32)
    per = []
    for b in range(8):
        d = dict(sh)
        d["xp"] = np.ascontiguousarray(inp["x_prompt"][b], f32)
        d["xs"] = np.ascontiguousarray(inp["x_sample"][16 * b:16 * b + 16], f32).reshape(128, 1024)
        for nm, key in (("h0re", "state_ssm_re"), ("h0im", "state_ssm_im")):
            a = np.asarray(inp[key][0, 16 * b:16 * b + 16], f32)
            d[nm] = np.ascontiguousarray(a.reshape(16, 32, 2, 64).transpose(2, 3, 1, 0).reshape(128, 512))
        sc = np.asarray(inp["state_conv"][0, 16 * b:16 * b + 16], f32)
        d["sconv_nat"] = np.ascontiguousarray(sc.reshape(16 * 30, 1024))
        d["sconv_fm"] = np.ascontiguousarray(sc.reshape(16, 30, 8, 128).transpose(3, 2, 0, 1).reshape(128, 8 * 16 * 30))
        per.append(d)
    return per


_CACHE = {}


def kernel(**inputs):
    if "nc" not in _CACHE:
        K = build()
        main_loop(K)
        _CACHE["nc"] = finish(K)
    nc = _CACHE["nc"]
    per = host_prep(inputs)
    res = run_bass_kernel_spmd(nc, per, core_ids=list(range(8)))
    R = res.results
    f32 = np.float32
    y_prompt = np.stack([np.asarray(R[b]["yp"], f32) for b in range(8)], 0)
    y_sample = np.concatenate([np.asarray(R[b]["ys"], f32).reshape(16, 8, 1024) for b in range(8)], 0)
    nrp = np.stack([np.asarray(R[b]["nre_p"], f32).reshape(64, 64) for b in range(8)], 0)[None]
    nip = np.stack([np.asarray(R[b]["nim_p"], f32).reshape(64, 64) for b in range(8)], 0)[None]
    ncp = np.stack([np.asarray(R[b]["ncv_p"], f32) for b in range(8)], 0)[None]
    nrs = np.concatenate([np.asarray(R[b]["nre_s"], f32).reshape(16, 64, 64) for b in range(8)], 0)[None]
    nis = np.concatenate([np.asarray(R[b]["nim_s"], f32).reshape(16, 64, 64) for b in range(8)], 0)[None]
    ncs = np.concatenate([np.asarray(R[b]["ncv_s"], f32).reshape(16, 30, 1024) for b in range(8)], 0)[None]
    return (y_prompt, y_sample, nrp, nip, ncp, nrs, nis, ncs)
```

```python
import math
import numpy as np
import concourse.bass as bass
import concourse.mybir as mybir
from concourse.bass_utils import run_bass_kernel_spmd

F32 = mybir.dt.float32
BF16 = mybir.dt.bfloat16
I32 = mybir.dt.int32
AF = mybir.ActivationFunctionType
ALU = mybir.AluOpType

D = 1024
NCH = 8
DFF = 4096
HP = 30
TC = 8
ALPHA = 2.0 ** 0.25
EPS = 1e-5
NTMAX = 512
NKMAX = NTMAX // TC
SEQ = 2048
NMETA = 16
NSAMP_TOK = 128
NSEQ = 16

VEC_SPECS = [("b_in", 40), ("b_glu", 8), ("b_b_out", 8), ("b_o", 8), ("b_ff1", 32), ("b_ff2", 8),
             ("ln_in_g", 8), ("ln_in_b", 8), ("conv_b", 8), ("conv_ln_g", 8), ("conv_ln_b", 8),
             ("ln1_g", 8), ("ln1_b", 8), ("ln2_g", 8), ("ln2_b", 8), ("ssm_d", 8), ("conv_w", 31 * 8)]
VOFF = {}
_o = 0
for _n, _c in VEC_SPECS:
    VOFF[_n] = _o
    _o += _c
NV = _o


class Ev:
    __slots__ = ("sem", "val")

    def __init__(self, sem, val):
        self.sem = sem
        self.val = val


class Eng:
    def __init__(self, K, raw, name):
        self.K = K
        self.raw = raw
        self.name = name
        self.nsem = 0
        self.seen = {}
        self._new_sem()
        K.engs.append(self)

    def _new_sem(self):
        self.sem = self.K.nc.alloc_semaphore(f"s_{self.name}_{self.nsem}")
        self.nsem += 1
        self.cnt = 0

    def wait(self, *evs):
        for ev in evs:
            if ev is None:
                continue
            if isinstance(ev, (list, tuple)):
                self.wait(*ev)
                continue
            k = ev.sem.num if hasattr(ev.sem, "num") else id(ev.sem)
            if self.seen.get(k, 0) >= ev.val:
                continue
            self.raw.wait_ge(ev.sem, ev.val)
            self.seen[k] = ev.val

    def mark(self, ins):
        if self.cnt >= 6000:
            self._new_sem()
        self.cnt += 1
        ins.then_inc(self.sem, 1)
        return Ev(self.sem, self.cnt)


class StopBuild(Exception):
    pass


class DmaSem:
    def __init__(self, K, name):
        self.sem = K.nc.alloc_semaphore(name)
        self.cnt = 0
        K.dmasems.append(self)

    def add(self, ins):
        self.cnt += 16
        ins.then_inc(self.sem, 16)
        return Ev(self.sem, self.cnt)


class Kern:
    def __init__(self, debug=None):
        self.debug = debug or {}
        self.nc = bass.Bass("TRN2", target_bir_lowering=False)
        self.ctx = []
        self.dbg_outs = {}
        self.dmasems = []
        self.engs = []
        self.pe_n = 0
        self.phase_log = []

    def MM(self, *a, **kw):
        self.pe_n += 1
        return self.nc.tensor.matmul(*a, **kw)

    def TR(self, *a, **kw):
        self.pe_n += 1
        return self.nc.tensor.transpose(*a, **kw)

    def sb(self, name, shape, dt):
        g = self.nc.sbuf_tensor(name, list(shape), dt)
        t = g.__enter__()
        self.ctx.append(g)
        return t

    def push_scope(self):
        self.ctx.append("SCOPE")

    def pop_scope(self):
        while True:
            g = self.ctx.pop()
            if g == "SCOPE":
                break
            g.__exit__(None, None, None)

    def close(self):
        for g in reversed(self.ctx):
            if g != "SCOPE":
                g.__exit__(None, None, None)
        self.ctx = []

    def din(self, name, shape, dt=F32):
        return self.nc.dram_tensor(name, list(shape), dt, kind="ExternalInput").ap()

    def dout(self, name, shape, dt=F32):
        return self.nc.dram_tensor(name, list(shape), dt, kind="ExternalOutput").ap()

    def dscr(self, name, shape, dt):
        return self.nc.dram_tensor(name, list(shape), dt).ap()


def view(t, off, dims):
    full = t[:]
    pstep = full.ap[0][0]
    npart = full.ap[0][1]
    return bass.AP(full.tensor, off, [[pstep, npart]] + [[s, c] for s, c in dims])


def build(debug=None):
    K = Kern(debug)
    nc = K.nc
    dbg = K.debug

    xp = K.din("xp", [SEQ, D])
    xs = K.din("xs", [NSAMP_TOK, D])
    meta = K.din("meta", [NMETA, D])
    vecs_d = K.din("vecs", [128, NV])
    ssm_small = K.din("ssm_small", [128, 96])
    bre_d = K.din("bre", [128, 512])
    bim_d = K.din("bim", [128, 512])
    cre_d = K.din("cre", [128, 512])
    cim_d = K.din("cim", [128, 512])
    h0re_d = K.din("h0re", [128, 512])
    h0im_d = K.din("h0im", [128, 512])
    sconv_fm = K.din("sconv_fm", [128, NCH * NSEQ * HP])
    sconv_nat = K.din("sconv_nat", [NSEQ * HP, D])
    w_in_d = K.din("w_in", [D, 5 * D])
    w_glu_d = K.din("w_glu", [D, D])
    w_aout_d = K.din("w_a_out", [D, D])
    w_bout_d = K.din("w_b_out", [D, D])
    w_o_d = K.din("w_o", [D, D])
    w_ff1_d = K.din("w_ff1", [D, DFF])
    w_ff2_d = K.din("w_ff2", [DFF, D])

    yp = K.dout("yp", [SEQ, D])
    ys = K.dout("ys", [NSAMP_TOK, D])
    nre_p = K.dout("nre_p", [32, 128])
    nim_p = K.dout("nim_p", [32, 128])
    ncv_p = K.dout("ncv_p", [HP, D])
    nre_s = K.dout("nre_s", [NSEQ * 32, 128])
    nim_s = K.dout("nim_s", [NSEQ * 32, 128])
    ncv_s = K.dout("ncv_s", [NSEQ * HP, D])

    wsc = {
        "in": K.dscr("wsc_in", [D, 5 * D], BF16),
        "glu": K.dscr("wsc_glu", [D, D], BF16),
        "aout": K.dscr("wsc_aout", [D, D], BF16),
        "bout": K.dscr("wsc_bout", [D, D], BF16),
        "o": K.dscr("wsc_o", [D, D], BF16),
        "ff1": K.dscr("wsc_ff1", [D, DFF], BF16),
        "ff2": K.dscr("wsc_ff2", [DFF, D], BF16),
    }
    wsrc = {"in": w_in_d, "glu": w_glu_d, "aout": w_aout_d, "bout": w_bout_d, "o": w_o_d,
            "ff1": w_ff1_d, "ff2": w_ff2_d}
    dsc = K.dscr("dsc", [NCH, 128, 31 * 128], BF16)
    winsc = K.dscr("winsc", [128, 8 * 8 * 2 * 128], BF16)
    cssc = K.dscr("cssc", [128, 32 * 8 * 2 * 32], BF16)

    PE = Eng(K, nc.tensor, "pe")
    ACT = Eng(K, nc.scalar, "act")
    DVE = Eng(K, nc.vector, "dve")
    POOL = Eng(K, nc.gpsimd, "pool")
    SP = Eng(K, nc.sync, "sp")

    IDF = K.sb("IDF", [128, 128], F32)
    IDB = K.sb("IDB", [128, 128], BF16)
    ONESB = K.sb("ONESB", [128, 128], BF16)
    NHALF = K.sb("NHALF", [128, 1], F32)
    VECS = K.sb("VECS", [128, NV], F32)
    DERV = K.sb("DERV", [128, 4, 8], F32)
    ARt = K.sb("ARt", [128, 32, 2], F32)
    AI2t = K.sb("AI2t", [128, 32, 2], F32)
    HCAR = K.sb("HCAR", [128, 32, 2], F32)
    KB = K.sb("KB", [128, 8, 8, 128], BF16)
    PSUM_g = nc.psum_tensor("PS", [128, 8, 512], F32)
    PS = PSUM_g.__enter__()
    K.ctx.append(PSUM_g)

    def V(name, k=None):
        o = VOFF[name]
        if k is None:
            return o
        return VECS[:, o + k:o + k + 1]

    class Banks:
        def __init__(self):
            self.nxt = 0
            self.free = [[] for _ in range(8)]
            self.held = set()

        def alloc(self, hold=False):
            b = self.nxt
            while b in self.held:
                b = (b + 1) % 8
            self.nxt = (b + 1) % 8
            PE.wait(self.free[b])
            self.free[b] = []
            if hold:
                self.held.add(b)
            return b

        def release(self, b, *evs):
            self.free[b].extend(evs)
            self.held.discard(b)

    BK = Banks()

    pl = DmaSem(K, "pl")
    outs = DmaSem(K, "outs")
    scr = DmaSem(K, "scr")

    conv_ev = {}

    def conv_dma(key, name, rows, cols):
        s = DmaSem(K, f"cv_{key}")
        ins = nc.gpsimd.dma_start(out=wsc[name][rows[0]:rows[1], cols[0]:cols[1]],
                                  in_=wsrc[name][rows[0]:rows[1], cols[0]:cols[1]])
        conv_ev[key] = s.add(ins)

    ld = []
    ld.append(pl.add(nc.sync.dma_start(out=VECS[:], in_=vecs_d)))
    K.push_scope()
    DG = K.sb("DG", [128, 2, 31, 128], BF16)
    CSF = K.sb("CSF", [128, 8, 32, 2, 2, 16], BF16)
    NWT = 6
    WT = K.sb("WT", [128, NWT, 8, 2, 128], BF16)
    SSMP = K.sb("SSMP", [128, 96], F32)
    BRE = K.sb("BRE", [128, 32, 16], F32)
    BIM = K.sb("BIM", [128, 32, 16], F32)
    CRE = K.sb("CRE", [128, 32, 16], F32)
    CIM = K.sb("CIM", [128, 32, 16], F32)
    pl.add(nc.sync.dma_start(out=SSMP[:], in_=ssm_small))
    pl.add(nc.sync.dma_start(out=BRE[:], in_=bre_d.rearrange("p (q c) -> p q c", c=16)))
    pl.add(nc.sync.dma_start(out=BIM[:], in_=bim_d.rearrange("p (q c) -> p q c", c=16)))
    pl.add(nc.sync.dma_start(out=CRE[:], in_=cre_d.rearrange("p (q c) -> p q c", c=16)))
    ev_pl = pl.add(nc.sync.dma_start(out=CIM[:], in_=cim_d.rearrange("p (q c) -> p q c", c=16)))

    conv_dma("in0", "in", (0, D), (0, 1024))
    conv_dma("in1", "in", (0, D), (1024, 2048))
    conv_dma("in2", "in", (0, D), (2048, 3072))
    conv_dma("in4", "in", (0, D), (4096, 5120))
    conv_dma("in3", "in", (0, D), (3072, 4096))
    conv_dma("bout", "bout", (0, D), (0, D))
    conv_dma("glu", "glu", (0, D), (0, D))
    conv_dma("aout", "aout", (0, D), (0, D))
    conv_dma("o", "o", (0, D), (0, D))

    IDX = K.sb("IDX", [128, 128], I32)
    POOL.raw.iota(IDX[:], pattern=[[1, 128]], base=0, channel_multiplier=-1)
    POOL.raw.memset(NHALF[:], -0.5)
    e_pc = POOL.mark(POOL.raw.memset(ONESB[:], 1.0 / 1024.0))
    DVE.wait(e_pc)
    DVE.raw.tensor_scalar(out=IDF[:], in0=IDX[:], scalar1=0.0, scalar2=None, op0=ALU.is_equal)
    e_id = DVE.mark(DVE.raw.tensor_scalar(out=IDB[:], in0=IDX[:], scalar1=0.0, scalar2=None, op0=ALU.is_equal))

    DVE.wait(ev_pl)
    g_in = VECS[:, V("ln_in_g"):V("ln_in_g") + 8]
    b_in_ln = VECS[:, V("ln_in_b"):V("ln_in_b") + 8]
    DVE.raw.tensor_scalar(out=DERV[:, 0, :], in0=g_in, scalar1=ALPHA, scalar2=None, op0=ALU.mult)
    DVE.raw.scalar_tensor_tensor(out=DERV[:, 1, :], in0=b_in_ln, scalar=ALPHA,
                                 in1=VECS[:, V("b_o"):V("b_o") + 8], op0=ALU.mult, op1=ALU.add)
    DVE.raw.tensor_scalar(out=DERV[:, 2, :], in0=VECS[:, V("ln1_g"):V("ln1_g") + 8], scalar1=ALPHA,
                          scalar2=None, op0=ALU.mult)
    e_derv = DVE.mark(DVE.raw.scalar_tensor_tensor(
        out=DERV[:, 3, :], in0=VECS[:, V("ln1_b"):V("ln1_b") + 8], scalar=ALPHA,
        in1=VECS[:, V("b_ff2"):V("b_ff2") + 8], op0=ALU.mult, op1=ALU.add))

    SM = K.sb("SM", [128, 24, 32], F32)
    SMI = K.sb("SMI", [128, 2, 32], I32)
    PW = K.sb("PW", [128, 9, 2, 32], F32)
    are = SSMP[:, 0:32]
    aim = SSMP[:, 32:64]
    ldt = SSMP[:, 64:96]
    (DT, ZR, TH, MAG, GS, GC, FS, FCc, SINT, COST, ABR, ABI, DEN, RDEN, ZR1, FR, FI, T1, T2, T3) = \
        [SM[:, i, :] for i in range(20)]

    wpend, rpend = {}, {}

    def _k(ap):
        return (ap.tensor.name, int(ap.offset))

    def dvl(make, out, reads):
        need = []
        for ap in reads:
            if _k(ap) in wpend:
                need.append(wpend[_k(ap)])
        ko = _k(out)
        if ko in wpend:
            need.append(wpend[ko])
        if ko in rpend:
            need.append(rpend[ko])
        DVE.wait(*need)
        e = DVE.mark(make())
        wpend[ko] = e
        for ap in reads:
            rpend[_k(ap)] = e
        return e

    def dv(ins):
        e = DVE.mark(ins)
        DVE.wait(e)
        return e

    def tt(out, a, b, op):
        return dvl(lambda: DVE.raw.tensor_tensor(out=out, in0=a, in1=b, op=op), out, [a, b])

    def ts(out, a, s1, op0, s2=None, op1=None):
        if op1 is None:
            return dvl(lambda: DVE.raw.tensor_scalar(out=out, in0=a, scalar1=s1, scalar2=None, op0=op0), out, [a])
        return dvl(lambda: DVE.raw.tensor_scalar(out=out, in0=a, scalar1=s1, scalar2=s2, op0=op0, op1=op1), out, [a])

    def cp(out, a):
        return dvl(lambda: DVE.raw.tensor_copy(out=out, in_=a), out, [a])

    POOL.wait(ev_pl)
    e = POOL.mark(POOL.raw.memset(T3, math.e))
    POOL.wait(e)
    e = POOL.mark(POOL.raw.tensor_tensor(out=DT, in0=T3, in1=ldt, op=ALU.pow))
    DVE.wait(e)
    tt(ZR, are, DT, ALU.mult)
    e_zr = tt(TH, aim, DT, ALU.mult)
    POOL.wait(e_zr)
    e_mag = POOL.mark(POOL.raw.tensor_tensor(out=MAG, in0=T3, in1=ZR, op=ALU.pow))
    INV2PI = 1.0 / (2.0 * math.pi)
    ts(GS, TH, INV2PI, ALU.mult)
    ts(GC, TH, INV2PI, ALU.mult, 0.25, ALU.add)

    def frac_center(dst, src, ii):
        cp(SMI[:, ii, :], src)
        cp(T1, SMI[:, ii, :])
        tt(dst, src, T1, ALU.subtract)
        ts(T2, dst, 0.5, ALU.is_gt)
        tt(dst, dst, T2, ALU.subtract)
        ts(T2, dst, -0.5, ALU.is_lt)
        return tt(dst, dst, T2, ALU.add)

    frac_center(FS, GS, 0)
    e_f = frac_center(FCc, GC, 1)
    ACT.wait(e_f)
    TWO_PI_SAFE = 6.283185
    ACT.raw.activation(out=SINT, in_=FS, func=AF.Sin, scale=TWO_PI_SAFE)
    e_sc = ACT.mark(ACT.raw.activation(out=COST, in_=FCc, func=AF.Sin, scale=TWO_PI_SAFE))
    DVE.wait(e_sc, e_mag)
    tt(ABR, MAG, COST, ALU.mult)
    tt(ABI, MAG, SINT, ALU.mult)
    tt(DEN, are, are, ALU.mult)
    tt(T1, aim, aim, ALU.mult)
    tt(DEN, DEN, T1, ALU.add)
    dvl(lambda: DVE.raw.reciprocal(out=RDEN, in_=DEN), RDEN, [DEN])
    ts(ZR1, ABR, -1.0, ALU.add)
    tt(T1, ZR1, are, ALU.mult)
    tt(T2, ABI, aim, ALU.mult)
    tt(T1, T1, T2, ALU.add)
    tt(FR, T1, RDEN, ALU.mult)
    tt(T1, ABI, are, ALU.mult)
    tt(T2, ZR1, aim, ALU.mult)
    tt(T1, T1, T2, ALU.subtract)
    tt(FI, T1, RDEN, ALU.mult)
    dvl(lambda: DVE.raw.memset(PW[:, 0, 0, :], 1.0), PW[:, 0, 0, :], [])
    dvl(lambda: DVE.raw.memset(PW[:, 0, 1, :], 0.0), PW[:, 0, 1, :], [])
    cp(PW[:, 1, 0, :], ABR)
    cp(PW[:, 1, 1, :], ABI)
    for s in range(1, 8):
        pr, pi = PW[:, s, 0, :], PW[:, s, 1, :]
        tt(T1, pr, ABR, ALU.mult)
        tt(T2, pi, ABI, ALU.mult)
        tt(PW[:, s + 1, 0, :], T1, T2, ALU.subtract)
        tt(T1, pr, ABI, ALU.mult)
        tt(T2, pi, ABR, ALU.mult)
        tt(PW[:, s + 1, 1, :], T1, T2, ALU.add)
    cp(ARt[:, :, 0], PW[:, 8, 0, :])
    cp(ARt[:, :, 1], PW[:, 8, 0, :])
    cp(AI2t[:, :, 0], PW[:, 8, 1, :])
    ts(AI2t[:, :, 1], PW[:, 8, 1, :], -1.0, ALU.mult)
    e_ar = dv(DVE.raw.memset(T3, 0.0))

    dg_free = [None, None]
    dg_sem = [DmaSem(K, "dg0"), DmaSem(K, "dg1")]

    def emit_diag(c):
        slot = c % 2
        ACT.wait(dg_free[slot], e_id, ev_pl)
        for jt in range(31):
            col = V("conv_w") + jt * 8 + c
            ins = ACT.raw.activation(out=DG[:, slot, jt, :], in_=IDB[:], func=AF.Identity,
                                     scale=VECS[:, col:col + 1])
        e_dg = ACT.mark(ins)
        SP.wait(e_dg)
        dg_free[slot] = dg_sem[slot].add(nc.sync.dma_start(out=dsc[c], in_=DG[:, slot, :, :].rearrange("p t m -> p (t m)")))

    def bq(ap2d):
        return ap2d.unsqueeze(2).broadcast_to([128, 32, 16])

    BBR = K.sb("BBR", [128, 32, 16], F32)
    BBI = K.sb("BBI", [128, 32, 16], F32)
    TA = K.sb("TA", [128, 32, 16], F32)
    TB = K.sb("TB", [128, 32, 16], F32)
    tt(TA[:], BRE[:], bq(FR), ALU.mult)
    tt(TB[:], BIM[:], bq(FI), ALU.mult)
    tt(BBR[:], TA[:], TB[:], ALU.subtract)
    tt(TA[:], BIM[:], bq(FR), ALU.mult)
    tt(TB[:], BRE[:], bq(FI), ALU.mult)
    tt(BBI[:], TA[:], TB[:], ALU.add)

    WP = K.sb("WP", [128, 2, 32, 2, 16], BF16)
    BBP = K.sb("BBP", [128, 32, 2, 2, 16], BF16)
    CP0 = K.sb("CP0", [128, 32, 2, 2, 16], BF16)
    POOL.raw.memset(BBP[:], 0.0)
    e_zb = POOL.mark(POOL.raw.memset(WP[:], 0.0))
    POOL.raw.memset(CP0[:], 0.0)
    POOL.raw.memset(HCAR[:], 0.0)
    POOL.raw.memset(KB[:], 0.0)
    e_z = POOL.mark(POOL.raw.memset(CSF[:], 0.0))
    DVE.wait(e_zb)

    def put_pad(dst_fn, src, negate=False):
        ks = _k(src[:])
        if ks in wpend:
            DVE.wait(wpend[ks])
        last = None
        for gm in range(2):
            sl = slice(64 * gm, 64 * gm + 64)
            if negate:
                last = DVE.raw.tensor_scalar(out=dst_fn(sl, gm), in0=src[sl], scalar1=-1.0, scalar2=None,
                                             op0=ALU.mult)
            else:
                last = DVE.raw.tensor_copy(out=dst_fn(sl, gm), in_=src[sl])
        e = DVE.mark(last)
        rpend[ks] = e
        return e

    put_pad(lambda sl, gm: BBP[sl, :, 0, gm, :], BBR)
    put_pad(lambda sl, gm: BBP[sl, :, 1, gm, :], BBI)

    cs_sem = DmaSem(K, "cssem")
    DVE.wait(e_z)
    put_pad(lambda sl, gm: CP0[sl, :, 0, gm, :], CRE)
    put_pad(lambda sl, gm: CP0[sl, :, 1, gm, :], CIM, negate=True)
    for tp in range(8):
        pr, pi = PW[:, tp + 1, 0, :], PW[:, tp + 1, 1, :]
        tt(TA[:], CRE[:], bq(pr), ALU.mult)
        tt(TB[:], CIM[:], bq(pi), ALU.mult)
        tt(TA[:], TA[:], TB[:], ALU.subtract)
        put_pad(lambda sl, gm: CSF[sl, tp, :, 0, gm, :], TA)
        tt(TA[:], CRE[:], bq(pi), ALU.mult)
        tt(TB[:], CIM[:], bq(pr), ALU.mult)
        tt(TA[:], TA[:], TB[:], ALU.add)
        e_cs = put_pad(lambda sl, gm: CSF[sl, tp, :, 1, gm, :], TA, negate=True)
        SP.wait(e_cs)
        ev_cssc = cs_sem.add(nc.sync.dma_start(
            out=cssc.rearrange("p (t x) -> p t x", t=8)[:, tp, :],
            in_=CSF[:, tp, :, :, :, :].rearrange("p q r g c -> p (q r g c)")))

    wt_free = [None] * NWT
    wt_sem = [DmaSem(K, f"wt{i}") for i in range(NWT)]
    for s in range(8):
        e_ = 7 - s
        pr, pi = PW[:, e_, 0, :], PW[:, e_, 1, :]
        PE_done_prev = None
        tt(TA[:], BBR[:], bq(pr), ALU.mult)
        tt(TB[:], BBI[:], bq(pi), ALU.mult)
        tt(TA[:], TA[:], TB[:], ALU.subtract)
        if s > 0:
            DVE.wait(e_wp_read)
        put_pad(lambda sl, gm: WP[sl, 0, :, gm, :], TA)
        tt(TA[:], BBR[:], bq(pi), ALU.mult)
        tt(TB[:], BBI[:], bq(pr), ALU.mult)
        tt(TA[:], TA[:], TB[:], ALU.add)
        e_wp = put_pad(lambda sl, gm: WP[sl, 1, :, gm, :], TA)
        PE.wait(e_wp)
        slot = s % NWT
        bA = BK.alloc()
        bB = BK.alloc()
        for ri in range(2):
            bb = bA if ri == 0 else bB
            pb16 = PS[:, bb, :].bitcast(BF16)
            for j in range(8):
                ins = K.TR(
                    pb16[:, j * 128:(j + 1) * 128],
                    WP[:, ri, 4 * j:4 * j + 4, :, :].rearrange("p q g c -> p (q g c)"), IDB[:])
        e_wp_read = PE.mark(ins)
        ACT.wait(e_wp_read, wt_free[slot])
        for ri in range(2):
            bb = bA if ri == 0 else bB
            pb16 = PS[:, bb, :].bitcast(BF16)
            ins = ACT.raw.activation(out=WT[:, slot, :, ri, :],
                                     in_=pb16.rearrange("p (j m) -> p j m", m=128), func=AF.Copy)
        e_wt = ACT.mark(ins)
        BK.release(bA, e_wt)
        BK.release(bB, e_wt)
        SP.wait(e_wt)
        dst = winsc.rearrange("p (s j r m) -> p s j r m", j=8, s=8, r=2)[:, s, :, :, :]
        wt_free[slot] = wt_sem[slot].add(nc.sync.dma_start(out=dst, in_=WT[:, slot, :, :, :]))
        emit_diag(s)
    ev_winsc = [e_ for e_ in wt_free if e_ is not None]
    ev_winsc0 = None
    ev_dsc = [dg_free[0], dg_free[1]]

    PE.wait(e_cs, e_id)
    kb_evs = []
    for j in range(8):
        b = BK.alloc() if j % 2 == 0 else b
        base = (j % 2) * 256
        for qq in range(4):
            q = 4 * j + qq
            osl = PS[32 * qq:32 * qq + 32, b, base:base + 256]
            first = (j % 2 == 0)
            for ri in range(2):
                K.MM(osl[:, 0:32], lhsT=BBP[:, q, ri, :, :].rearrange("p g c -> p (g c)"),
                                 rhs=CP0[:, q, ri, :, :].rearrange("p g c -> p (g c)"),
                                 start=(first and ri == 0), stop=False, skip_group_check=True,
                                 tile_position=(0, 32 * qq))
            for ri in range(2):
                ins = K.MM(osl[:, 32:256].rearrange("p (t m) -> p t m", m=32),
                                       lhsT=BBP[:, q, ri, :, :].rearrange("p g c -> p (g c)"),
                                       rhs=CSF[:, 0:7, q, ri, :, :].rearrange("p t g c -> p t (g c)"),
                                       start=False, stop=(ri == 1), skip_group_check=True,
                                       tile_position=(0, 32 * qq))
        if j % 2 == 1:
            e_mm = PE.mark(ins)
            DVE.wait(e_mm)
            for jj in (j - 1, j):
                bs = (jj % 2) * 256
                for qq in range(4):
                    last = DVE.raw.tensor_copy(
                        out=KB[32 * qq:32 * qq + 32, jj, :, 32 * qq:32 * qq + 32],
                        in_=PS[32 * qq:32 * qq + 32, b, bs:bs + 256].rearrange("p (t m) -> p t m", m=32))
            e_kb = dv(last)
            BK.release(b, e_kb)
    for j in range(8):
        last = DVE.raw.scalar_tensor_tensor(out=KB[:, j, 0, :], in0=IDF[:], scalar=V("ssm_d", j),
                                            in1=KB[:, j, 0, :], op0=ALU.mult, op1=ALU.add)
    e_kbd = dv(last)

    if "prologue" in dbg:
        d_pw = K.dout("d_pw", [128, 9 * 2 * 32])
        d_kb = K.dout("d_kb", [128, 8 * 8 * 128], BF16)
        d_f = K.dout("d_f", [128, 2, 32])
        SP.wait(e_kbd, e_ar)
        outs.add(nc.sync.dma_start(out=d_pw, in_=PW[:].rearrange("p s r q -> p (s r q)")))
        outs.add(nc.sync.dma_start(out=d_kb, in_=KB[:].rearrange("p j o m -> p (j o m)")))
        outs.add(nc.sync.dma_start(out=d_f[:, 0, :], in_=FR))
        outs.add(nc.sync.dma_start(out=d_f[:, 1, :], in_=FI))

    conv_dma("ff1a", "ff1", (0, D), (0, 2048))
    conv_dma("ff2a", "ff2", (0, 2048), (0, D))
    conv_dma("ff1b", "ff1", (0, D), (2048, 4096))
    conv_dma("ff2b", "ff2", (2048, 4096), (0, D))

    e_end_dve = DVE.mark(DVE.raw.memset(T3, 0.0))
    for E_ in (PE, ACT, POOL, SP):
        E_.wait(e_end_dve, e_kbd, e_wt)
    K.pop_scope()

    K.prologue_events = dict(cssc=ev_cssc, winsc=[ev_winsc0, ev_winsc], dsc=ev_dsc, conv=conv_ev,
                             derv=e_derv, ar=e_ar, kb=e_kbd, ident=e_id)
    K.handles = dict(PE=PE, ACT=ACT, DVE=DVE, POOL=POOL, SP=SP, BK=BK, PS=PS, outs=outs, V=V,
                     VECS=VECS, DERV=DERV, ARt=ARt, AI2t=AI2t, HCAR=HCAR, KB=KB, IDF=IDF, IDB=IDB,
                     ONESB=ONESB, NHALF=NHALF,
                     dram=dict(xp=xp, xs=xs, meta=meta, h0re=h0re_d, h0im=h0im_d, sconv_fm=sconv_fm,
                               sconv_nat=sconv_nat, yp=yp, ys=ys, nre_p=nre_p, nim_p=nim_p, ncv_p=ncv_p,
                               nre_s=nre_s, nim_s=nim_s, ncv_s=ncv_s, wsc=wsc, dsc=dsc, winsc=winsc,
                               cssc=cssc))
    return K


def main_loop(K, ntiles=5, stop_after=None):
    nc = K.nc
    dbg = K.debug

    def chk(name, ti):
        K.phase_log.append((name, ti, K.pe_n))
        if stop_after is not None and stop_after == (name, ti):
            raise StopBuild()
    h = K.handles
    PE, ACT, DVE, POOL, SP, BK, PS, outs, V = (h[k] for k in ("PE", "ACT", "DVE", "POOL", "SP", "BK", "PS", "outs", "V"))
    VECS, DERV, ARt, AI2t, HCAR, KB, IDF, IDB, ONESB, NHALF = (h[k] for k in (
        "VECS", "DERV", "ARt", "AI2t", "HCAR", "KB", "IDF", "IDB", "ONESB", "NHALF"))
    dr = h["dram"]
    pev = K.prologue_events

    YS = K.sb("YS", [128, 2, D], F32)
    B2 = K.sb("B2", [128, NCH, NTMAX], BF16)
    B5 = K.sb("B5", [128, NCH, NTMAX], BF16)
    G16 = K.sb("G16", [128, 2 * NCH * NTMAX], BF16)
    GLU = K.sb("GLU", [128, NCH, HP + NTMAX], BF16)
    TMPA = K.sb("TMPA", [128, NTMAX], F32)
    TMPB = K.sb("TMPB", [128, NTMAX], F32)
    SG = K.sb("SG", [128, 2, NTMAX], BF16)
    XH = K.sb("XH", [128, 32, 2, NKMAX], F32)
    HB = K.sb("HB", [128, 32, 2, NKMAX], BF16)
    XS = K.sb("XS", [128, 2, D], F32)
    R32 = K.sb("R32", [128, NCH, NTMAX], F32)
    B0 = K.sb("B0", [128, NCH, NTMAX], BF16)
    B1 = K.sb("B1", [128, NCH, NTMAX], BF16)
    SCR = K.sb("SCR", [128, 3, 32, 2], F32)
    WSL = K.sb("WSL", [128, 3, 4096], BF16)
    SSMW = K.sb("SSMW", [128, 16384], BF16)
    ST6 = K.sb("ST6", [128, 2, 12], F32)
    MV = K.sb("MV", [128, 2, 2], F32)
    RS = K.sb("RS", [128, 2, 2], F32)
    OST = K.sb("OST", [128, 2, 128], F32)
    H0S = view(YS, D, [(2 * NSEQ, 32), (NSEQ, 2), (1, NSEQ)])

    WINv = SSMW[:, :].rearrange("p (s j r m) -> p s j r m", j=8, s=8, r=2)
    CSv = SSMW[:, :].rearrange("p (t q r m) -> p t q r m", q=32, t=8, r=2)

    late_guard = [pev["cssc"], pev["dsc"], pev["kb"], pev["winsc"]]
    POOL.wait(late_guard)
    zero_ev = POOL.mark(POOL.raw.memset(GLU[:, :, 0:HP], 0.0))

    def tile_units():
        u = [("mat", "in", 0, 0), ("mat", "in", 0, 1),
             ("mat", "in", 0, 2), ("mat", "in", 0, 4), ("mat", "in", 0, 3), ("mat", "in", 0, 5)]
        u += [("diag", c) for c in range(8)]
        u += [("mat", "in", 0, 8), ("mat", "in", 0, 9), ("mat", "in", 0, 6), ("mat", "in", 0, 7)]
        u += [("mat", "bout", 0, 0), ("mat", "bout", 0, 1), ("mat", "glu", 0, 0), ("mat", "glu", 0, 1),
              ("mat", "aout", 0, 0), ("mat", "aout", 0, 1), ("mat", "o", 0, 0), ("mat", "o", 0, 1)]
        for hh in range(2):
            u += [("mat", "ff1", 0, 4 * hh + cb) for cb in range(4)]
            for cbo in range(2):
                u += [("mat", "ff2", 2 * hh + kgl, cbo) for kgl in range(2)]
        return u

    class WStream:
        def __init__(self, ntile):
            self.specs = []
            for t in range(ntile):
                self.specs += [(sp, t == 0) for sp in tile_units()]
            self.issued = 0
            self.taken = 0
            self.released = {}
            self.handles = {}
            self.sem = [DmaSem(K, f"w{i}") for i in range(3)]

        def _issue(self, j):
            spec, first_tile = self.specs[j]
            sl = j % 3
            if j >= 3:
                SP.wait(self.released[j - 3])
            if spec[0] == "mat":
                _, name, kg, cb = spec
                if first_tile:
                    if name == "in":
                        key = f"in{cb // 2}"
                    elif name == "ff1":
                        key = "ff1a" if cb < 4 else "ff1b"
                    elif name == "ff2":
                        key = "ff2a" if kg < 2 else "ff2b"
                    else:
                        key = name
                    SP.wait(pev["conv"][key])
                src = dr["wsc"][name][kg * 1024:(kg + 1) * 1024, cb * 512:(cb + 1) * 512].rearrange(
                    "(kc p) n -> p kc n", p=128)
                dst = WSL[:, sl, :].rearrange("p (kc n) -> p kc n", n=512)
            else:
                _, c = spec
                if first_tile:
                    SP.wait(pev["dsc"])
                src = dr["dsc"][c]
                dst = WSL[:, sl, 0:31 * 128]
            ev = self.sem[sl].add(nc.sync.dma_start(out=dst, in_=src))
            self.handles[j] = (j, ev)

        def pump(self, upto):
            while self.issued <= min(upto, len(self.specs) - 1):
                j = self.issued
                if j >= 3 and (j - 3) not in self.released:
                    break
                self._issue(j)
                self.issued += 1

        def load(self, spec, first_tile):
            idx = self.taken
            self.taken += 1
            assert self.specs[idx][0] == spec, (self.specs[idx], spec)
            self.pump(idx + 2)
            assert idx in self.handles
            j, ev = self.handles[idx]
            return j, ev

        def release(self, j, ev):
            self.released[j] = ev
            self.pump(self.taken + 1)

    WS = None

    def wmat(j):
        return WSL[:, j % 3, :].rearrange("p (kc n) -> p kc n", n=512)

    xs_free = [None, None]
    xs_sem = [DmaSem(K, "xs0"), DmaSem(K, "xs1")]
    ys_free = [None, None]
    ssmw_sem = DmaSem(K, "ssmw")
    h0_sem = DmaSem(K, "h0")
    hist_sem = DmaSem(K, "hist")
    ys_sem = [DmaSem(K, "ys0"), DmaSem(K, "ys1")]
    ost_sem = [DmaSem(K, "ost0"), DmaSem(K, "ost1")]
    cvo_sem = DmaSem(K, "cvo")
    ssmw_free = None
    glu_hist_ev = zero_ev
    hcar_ev = None
    state = dict(xs_i=0, ys_i=0)

    tiles = [dict(NT=512, kind="P", tok0=512 * i) for i in range(4)] + [dict(NT=144, kind="S")]
    tiles = tiles[:ntiles] if ntiles < 5 else tiles
    if dbg.get("only_last"):
        tiles = [tiles[-1]]
    WS = WStream(len(tiles))

    def make_subt(T):
        if T["kind"] == "S":
            return [dict(c0=0, R=128, src=[(dr["xs"][0:128, :], 0, 128)], dst=[(dr["ys"][0:128, :], 0, 128)]),
                    dict(c0=128, R=16, src=[(dr["xp"][2032:2048, :], 0, 16)], dst=[(dr["yp"][2032:2048, :], 0, 16)])]
        subt = []
        for m in range(4):
            t0 = T["tok0"] + 128 * m
            if t0 == 0:
                src = [(dr["meta"][0:16, :], 0, 16), (dr["xp"][0:112, :], 16, 128)]
                dst = [(dr["yp"][0:112, :], 16, 128)]
            else:
                src = [(dr["xp"][t0 - 16:t0 + 112, :], 0, 128)]
                dst = [(dr["yp"][t0 - 16:t0 + 112, :], 0, 128)]
            subt.append(dict(c0=128 * m, R=128, src=src, dst=dst))
        return subt

    entry_evs_by_tile = {}
    subts = [make_subt(T) for T in tiles]

    def entry_A(ti_, st):
        c0, R = st["c0"], st["R"]
        sl = state["xs_i"] % 2
        state["xs_i"] += 1
        SP.wait(xs_free[sl])
        for (src, r0, r1) in st["src"]:
            ev_ld = xs_sem[sl].add(nc.sync.dma_start(out=XS[r0:r1, sl, :], in_=src))
        DVE.wait(ev_ld)
        DVE.raw.bn_stats(ST6[0:R, sl, 0:6], XS[0:R, sl, 0:512])
        e = DVE.mark(DVE.raw.bn_stats(ST6[0:R, sl, 6:12], XS[0:R, sl, 512:1024]))
        DVE.wait(e)
        e = DVE.mark(DVE.raw.bn_aggr(MV[0:R, sl, :], ST6[0:R, sl, :]))
        DVE.wait(e)
        e = DVE.mark(DVE.raw.tensor_scalar(out=RS[0:R, sl, 0:1], in0=MV[0:R, sl, 1:2], scalar1=EPS,
                                           scalar2=None, op0=ALU.add))
        ACT.wait(e)
        e = ACT.mark(ACT.raw.activation(out=RS[0:R, sl, 1:2], in_=RS[0:R, sl, 0:1], func=AF.Ln))
        ACT.wait(e)
        e = ACT.mark(ACT.raw.activation(out=RS[0:R, sl, 1:2], in_=RS[0:R, sl, 1:2], func=AF.Exp, scale=-0.5))
        DVE.wait(e)
        e_n = DVE.mark(DVE.raw.tensor_scalar(out=XS[0:R, sl, :], in0=XS[0:R, sl, :], scalar1=MV[0:R, sl, 0:1],
                                             scalar2=RS[0:R, sl, 1:2], op0=ALU.subtract, op1=ALU.mult))
        return dict(sl=sl, e_n=e_n)

    def entry_B(ti_, st, A_, guard_ev):
        XB = B0
        entry_evs = entry_evs_by_tile.setdefault(ti_, [])
        c0, R = st["c0"], st["R"]
        sl, e_n = A_["sl"], A_["e_n"]
        PE.wait(e_n, pev["ident"])
        for half in range(2):
            b = BK.alloc()
            for k4 in range(4):
                kc = half * 4 + k4
                ins = K.TR(PS[:, b, k4 * 128:k4 * 128 + R], XS[0:R, sl, kc * 128:(kc + 1) * 128],
                                          IDF[0:R, 0:R])
            e_t = PE.mark(ins)
            if half == 1:
                xs_free[sl] = e_t
            eng = ACT if half == 0 else DVE
            eng.wait(e_t, pev["derv"], guard_ev)
            for k4 in range(4):
                kc = half * 4 + k4
                src = PS[:, b, k4 * 128:k4 * 128 + R]
                if eng is ACT:
                    ACT.raw.activation(out=XB[:, kc, c0:c0 + R], in_=src, func=AF.Identity,
                                       scale=V("ln_in_g", kc), bias=V("ln_in_b", kc))
                    ins = ACT.raw.activation(out=R32[:, kc, c0:c0 + R], in_=src, func=AF.Identity,
                                             scale=DERV[:, 0, kc:kc + 1], bias=DERV[:, 1, kc:kc + 1])
                else:
                    DVE.raw.tensor_scalar(out=XB[:, kc, c0:c0 + R], in0=src, scalar1=V("ln_in_g", kc),
                                          scalar2=V("ln_in_b", kc), op0=ALU.mult, op1=ALU.add)
                    ins = DVE.raw.tensor_scalar(out=R32[:, kc, c0:c0 + R], in0=src,
                                                scalar1=DERV[:, 0, kc:kc + 1], scalar2=DERV[:, 1, kc:kc + 1],
                                                op0=ALU.mult, op1=ALU.add)
            ee = eng.mark(ins)
            BK.release(b, ee)
            entry_evs.append(ee)

    def issue_win(free_ev, first_):
        SP.wait(free_ev)
        if first_:
            SP.wait(pev["winsc"])
        state["ev_win"] = ssmw_sem.add(nc.sync.dma_start(out=SSMW[:, :], in_=dr["winsc"]))

    issue_win(None, True)
    _A = {}
    for m0, st0 in enumerate(subts[0]):
        if m0 < 2:
            _A[m0] = entry_A(0, st0)
    for m0, st0 in enumerate(subts[0]):
        entry_B(0, st0, _A[m0], None)
        if m0 + 2 < len(subts[0]):
            _A[m0 + 2] = entry_A(0, subts[0][m0 + 2])
    chk("start", 0)
    for ti, T in enumerate(tiles):
        NT = T["NT"]
        NK = NT // TC
        first = (ti == 0)
        isS = T["kind"] == "S"
        if isS:
            GA = view(G16, 0, [(NT, 8), (1, NT)])
            GB = view(G16, 8 * NT, [(NT, 8), (1, NT)])
            HDN = view(G16, 0, [(NT, 16), (1, NT)])
            NSQ2 = NSEQ + 2
            GLUS = view(G16, 16 * NT, [(NSQ2 * 38, 8), (38, NSQ2), (1, 38)])
            GNEW = XS[:, 1, 0:512].bitcast(BF16).rearrange("p (k n) -> p k n", n=128)
        else:
            GA = view(G16, 0, [(NTMAX, 8), (1, NT)])
            GB = view(G16, 8 * NTMAX, [(NTMAX, 8), (1, NT)])
            HDN = view(G16, 0, [(NTMAX, 16), (1, NT)])
            GLUS = None
        XB = B0
        MT = B0
        X1B = B0
        UT = B1
        SBF = B1
        CV = B2
        ZT = B2
        SQ = B2
        CVSQ = B5
        CVN = B5
        ZSG = B5
        subt = subts[ti]

        if isS:
            SP.wait(ys_free[1])
            h0_sem.add(nc.sync.dma_start(out=H0S[:, :, 0, :], in_=dr["h0re"].rearrange("p (q s) -> p q s", s=NSEQ)))
            ev_h0 = h0_sem.add(nc.sync.dma_start(out=H0S[:, :, 1, :], in_=dr["h0im"].rearrange("p (q s) -> p q s", s=NSEQ)))
            POOL.wait(state.get("g16_free"))
            scf = dr["sconv_fm"].rearrange("p (k s r) -> p k s r", k=NCH, s=NSEQ)
            for k2 in range(NCH):
                ev_hist = hist_sem.add(nc.gpsimd.dma_start(out=GLUS[:, k2, 0:NSEQ, 0:HP], in_=scf[:, k2, :, :]))
            outs.add(nc.sync.dma_start(
                out=dr["ncv_s"].rearrange("(s r) d -> s r d", r=HP)[:, 0:HP - TC, :],
                in_=dr["sconv_nat"].rearrange("(s r) d -> s r d", r=HP)[:, TC:HP, :]))

        entry_evs = entry_evs_by_tile[ti]
        xb_ready = list(entry_evs)
        chk("entry", ti)

        def proj_unit(spec, rhs_fn, nk, evac_fn, rhs_wait, oc0, kc0=0, bank_of=None, start0=True, stop_last=True,
                      release=True):
            sl, ev_w = WS.load(spec, first)
            PE.wait(ev_w, rhs_wait)
            wm = wmat(sl)
            last = None
            for o4 in range(4):
                if bank_of is None:
                    b = BK.alloc()
                else:
                    b = bank_of(o4)
                for kc in range(nk):
                    last = K.MM(PS[:, b, 0:NT], lhsT=wm[:, kc, o4 * 128:(o4 + 1) * 128],
                                            rhs=rhs_fn(kc0 + kc), start=(start0 and kc == 0),
                                            stop=(stop_last and kc == nk - 1))
                if evac_fn is not None:
                    e_mm = PE.mark(last)
                    evac_fn(oc0 + o4, b, e_mm)
            e_rel = PE.mark(last) if evac_fn is None else e_mm
            WS.release(sl, e_rel)
            return e_rel

        ut_evs = []

        def evac_u(oc, b, e_mm):
            ACT.wait(e_mm)
            e = ACT.mark(ACT.raw.activation(out=UT[:, oc, 0:NT], in_=PS[:, b, 0:NT], func=AF.Identity,
                                            bias=V("b_in", oc), scale=1.0))
            BK.release(b, e)
            ut_evs.append(e)

        ACT.wait(state.get("b1_free"))
        for cb in range(2):
            proj_unit(("mat", "in", 0, cb), lambda kc: XB[:, kc, 0:NT], 8, evac_u, xb_ready, oc0=4 * cb)

        chk("inproj_u", ti)
        ev_win = state["ev_win"]
        PE.wait(ev_win, ut_evs)
        x_evs = []
        for jr in range(2):
            banks = [BK.alloc() for _ in range(4)]
            fst = [True] * 4
            for j4 in range(4):
                j = jr * 4 + j4
                for ri in range(2):
                    col = (j4 * 2 + ri) * NK
                    for s_ in range(TC):
                        for qq in range(4):
                            rows = slice(32 * qq, 32 * qq + 32)
                            last = K.MM(
                                PS[:, banks[qq], col:col + NK], lhsT=WINv[rows, s_, j, ri, :],
                                rhs=UT[rows, j, s_:NT:TC], start=fst[qq], stop=(s_ == TC - 1),
                                skip_group_check=True, tile_position=(32 * qq, 0))
                            fst[qq] = False
            e_mm = PE.mark(last)
            XHv = XH[:, :, :, :].rearrange("p (j q) r k -> p j q r k", q=4)
            for qq in range(4):
                eng = ACT if qq % 2 == 0 else DVE
                eng.wait(e_mm, state.get("xh_free"))
                src = PS[:, banks[qq], 0:8 * NK].rearrange("p (j r k) -> p j r k", j=4, r=2)
                dst = XHv[:, jr * 4:jr * 4 + 4, qq, :, 0:NK]
                if eng is ACT:
                    e = ACT.mark(ACT.raw.activation(out=dst, in_=src, func=AF.Copy))
                else:
                    e = DVE.mark(DVE.raw.tensor_copy(out=dst, in_=src))
                BK.release(banks[qq], e)
                x_evs.append(e)
        ssmw_free = e_mm
        SP.wait(ssmw_free)
        if first:
            SP.wait(pev["cssc"])
        ev_cs = ssmw_sem.add(nc.sync.dma_start(out=SSMW[:, :], in_=dr["cssc"]))

        chk("ssm_in", ti)
        POOL.wait(x_evs, hcar_ev, pev["ar"], state.get("hb_free"))

        def pw(ins):
            e = POOL.mark(ins)
            POOL.wait(e)
            return e

        def scan_step(hprev, xk_view):
            m1, m2, t_ = SCR[:, 0], SCR[:, 1], SCR[:, 2]
            POOL.raw.tensor_tensor(out=m1, in0=hprev, in1=ARt[:], op=ALU.mult)
            pw(POOL.raw.tensor_tensor(out=m2, in0=hprev, in1=AI2t[:], op=ALU.mult))
            pw(POOL.raw.tensor_tensor(out=t_, in0=m1, in1=xk_view, op=ALU.add))
            return pw(POOL.raw.tensor_tensor(out=xk_view, in0=t_, in1=m2[:, :, ::-1], op=ALU.add))

        if isS:
            POOL.wait(ev_h0)
            pw(POOL.raw.tensor_copy(out=HB[:, :, :, 0:NSEQ], in_=H0S))
            for s_ in range(NSEQ):
                scan_step(H0S[:, :, :, s_], XH[:, :, :, s_])
            pk0 = NSEQ
        else:
            pk0 = 0
        pw(POOL.raw.tensor_copy(out=HB[:, :, :, pk0], in_=HCAR[:]))
        for k in range(pk0, NK):
            hprev = HCAR[:] if k == pk0 else XH[:, :, :, k - 1]
            e_sc = scan_step(hprev, XH[:, :, :, k])
        hcar_ev = pw(POOL.raw.tensor_copy(out=HCAR[:], in_=XH[:, :, :, NK - 1]))
        scan_done = hcar_ev

        chk("scan", ti)
        if first:
            ACT.wait(late_guard)
            DVE.wait(late_guard)
            SP.wait(late_guard)
        glu_evs = [None] * 8
        for hh in range(2):
            slA, evA = WS.load(("mat", "in", 0, 2 + hh), first)
            slB, evB = WS.load(("mat", "in", 0, 4 + hh), first)
            PE.wait(evA, evB)
            for c4 in range(4):
                c = hh * 4 + c4
                bA = BK.alloc()
                for kc in range(8):
                    la = K.MM(PS[:, bA, 0:NT], lhsT=wmat(slA)[:, kc, c4 * 128:(c4 + 1) * 128],
                                          rhs=XB[:, kc, 0:NT], start=(kc == 0), stop=(kc == 7))
                bB = BK.alloc()
                for kc in range(8):
                    lb = K.MM(PS[:, bB, 0:NT], lhsT=wmat(slB)[:, kc, c4 * 128:(c4 + 1) * 128],
                                          rhs=XB[:, kc, 0:NT], start=(kc == 0), stop=(kc == 7))
                e_mm = PE.mark(lb)
                sgs = SG[:, c % 2, 0:NT]
                ACT.wait(e_mm, state.get(f"sg_free{c % 2}"))
                e_sg = ACT.mark(ACT.raw.activation(out=sgs, in_=PS[:, bB, 0:NT], func=AF.Sigmoid,
                                                   bias=V("b_in", 16 + c), scale=1.0))
                DVE.wait(e_sg, glu_hist_ev)
                if isS:
                    DVE.wait(ev_hist, xs_free[1])
                    e_gn = DVE.mark(DVE.raw.scalar_tensor_tensor(
                        out=GNEW[:, c, :], in0=PS[:, bA, 0:128], scalar=V("b_in", 8 + c), in1=SG[:, c % 2, 0:128],
                        op0=ALU.add, op1=ALU.mult))
                    e_gp = DVE.mark(DVE.raw.scalar_tensor_tensor(
                        out=GLU[:, c, HP:HP + 16], in0=PS[:, bA, 128:144], scalar=V("b_in", 8 + c),
                        in1=SG[:, c % 2, 128:144], op0=ALU.add, op1=ALU.mult))
                    DVE.wait(e_gn, e_gp)
                    DVE.raw.tensor_copy(out=GLUS[:, c, 0:NSEQ, HP:HP + TC],
                                        in_=GNEW[:, c, :].rearrange("p (s t) -> p s t", t=TC))
                    ins = DVE.raw.tensor_copy(out=GLUS[:, c, NSEQ:NSEQ + 2, :],
                                              in_=view(GLU, c * (HP + NTMAX), [(TC, 2), (1, 38)]))
                else:
                    ins = DVE.raw.scalar_tensor_tensor(
                        out=GLU[:, c, HP:HP + NT], in0=PS[:, bA, 0:NT], scalar=V("b_in", 8 + c), in1=sgs,
                        op0=ALU.add, op1=ALU.mult)
                e_g = DVE.mark(ins)
                state[f"sg_free{c % 2}"] = e_g
                BK.release(bA, e_g)
                BK.release(bB, e_sg)
                glu_evs[c] = e_g
            WS.release(slA, e_mm)
            WS.release(slB, e_mm)

        chk("glu", ti)
        cv_evs = []
        conv_mm = None
        ACT.wait(state.get("b2_free"), state.get("b5_free"))
        for c in range(8):
            sl, ev_w = WS.load(("diag", c), first)
            PE.wait(ev_w, glu_evs[c])
            dg = WSL[:, sl % 3, 0:31 * 128].rearrange("p (t m) -> p t m", m=128)
            b = BK.alloc()
            for jt in range(31):
                if isS:
                    last = K.MM(PS[:, b, 0:NT], lhsT=dg[:, jt, :], rhs=GLUS[:, c, :, jt:jt + TC],
                                start=(jt == 0), stop=(jt == 30))
                else:
                    last = K.MM(PS[:, b, 0:NT], lhsT=dg[:, jt, :], rhs=GLU[:, c, jt:jt + NT],
                                            start=(jt == 0), stop=(jt == 30))
            e_mm = PE.mark(last)
            conv_mm = e_mm
            WS.release(sl, e_mm)
            ACT.wait(e_mm)
            ACT.raw.activation(out=CV[:, c, 0:NT], in_=PS[:, b, 0:NT], func=AF.Identity, bias=V("conv_b", c),
                               scale=1.0)
            e = ACT.mark(ACT.raw.activation(out=CVSQ[:, c, 0:NT], in_=PS[:, b, 0:NT], func=AF.Square,
                                            bias=V("conv_b", c), scale=1.0))
            BK.release(b, e)
            cv_evs.append(e)

        chk("conv", ti)
        if isS:
            tr_evs = []
            for half in range(2):
                b = BK.alloc()
                pb16 = PS[:, b, :].bitcast(BF16)
                for k4 in range(4):
                    kc = half * 4 + k4
                    K.TR(pb16[0:128, k4 * 128:(k4 + 1) * 128], GNEW[:, kc, :], IDB[:])
                    ins = K.TR(pb16[0:HP, 512 + k4 * 128:512 + (k4 + 1) * 128],
                                              GLU[:, kc, 16:16 + HP], IDB[:])
                e_t = PE.mark(ins)
                ACT.wait(e_t, ys_free[0])
                ACT.raw.activation(out=YS[0:128, 0, half * 512:(half + 1) * 512], in_=pb16[0:128, 0:512],
                                   func=AF.Copy)
                e = ACT.mark(ACT.raw.activation(out=XS[0:HP, 0, half * 512:(half + 1) * 512],
                                                in_=pb16[0:HP, 512:1024], func=AF.Copy))
                BK.release(b, e)
                tr_evs.append(e)
            SP.wait(tr_evs)
            ncs = dr["ncv_s"].rearrange("(s r) d -> s r d", r=HP)
            for s_ in range(NSEQ):
                ev_o = ys_sem[0].add(nc.sync.dma_start(out=ncs[s_, HP - TC:HP, :], in_=YS[s_ * TC:(s_ + 1) * TC, 0, :]))
            ys_free[0] = ev_o
            xs_free[0] = cvo_sem.add(nc.sync.dma_start(out=dr["ncv_p"], in_=XS[0:HP, 0, :]))
        else:
            POOL.wait(conv_mm)
            glu_hist_ev = POOL.mark(POOL.raw.tensor_copy(out=GLU[:, :, 0:HP], in_=GLU[:, :, NT:NT + HP]))

        def ln_stats(src_bf, sq_bf, ready_evs):
            bM = BK.alloc(hold=True)
            bQ = BK.alloc(hold=True)
            for kc in range(8):
                PE.wait(ready_evs[kc])
                K.MM(PS[:, bM, 0:NT], lhsT=ONESB[:], rhs=src_bf[:, kc, 0:NT], start=(kc == 0),
                     stop=(kc == 7))
                last = K.MM(PS[:, bQ, 0:NT], lhsT=ONESB[:], rhs=sq_bf[:, kc, 0:NT], start=(kc == 0),
                            stop=(kc == 7))
            e_mm = PE.mark(last)
            ACT.wait(e_mm, state.get("tmp_free"))
            e = ACT.mark(ACT.raw.activation(out=TMPA[:, 0:NT], in_=PS[:, bM, 0:NT], func=AF.Square))
            DVE.wait(e)
            e = DVE.mark(DVE.raw.scalar_tensor_tensor(out=TMPB[:, 0:NT], in0=PS[:, bQ, 0:NT], scalar=EPS,
                                                      in1=TMPA[:, 0:NT], op0=ALU.add, op1=ALU.subtract))
            ACT.wait(e)
            e = ACT.mark(ACT.raw.activation(out=TMPB[:, 0:NT], in_=TMPB[:, 0:NT], func=AF.Ln))
            ACT.wait(e)
            e = ACT.mark(ACT.raw.activation(out=TMPA[:, 0:NT], in_=TMPB[:, 0:NT], func=AF.Exp, scale=-0.5))
            DVE.wait(e)
            e = DVE.mark(DVE.raw.scalar_tensor_tensor(out=PS[:, bQ, 0:NT], in0=PS[:, bM, 0:NT], scalar=-1.0,
                                                      in1=TMPA[:, 0:NT], op0=ALU.mult, op1=ALU.mult))
            DVE.wait(e)
            e = DVE.mark(DVE.raw.tensor_copy(out=PS[:, bM, 0:NT], in_=TMPA[:, 0:NT]))
            DVE.wait(e)
            return bM, bQ, e

        bR, bN, e_st = ln_stats(CV, CVSQ, cv_evs)
        chk("convln", ti)
        ga_evs, gb_evs = [], []

        def evac_gate(dst, lst, boff):
            def f(oc, b, e_mm):
                ACT.wait(e_mm)
                e = ACT.mark(ACT.raw.activation(out=dst[:, oc, :], in_=PS[:, b, 0:NT], func=AF.Sigmoid,
                                                bias=V("b_in", boff + oc), scale=1.0))
                BK.release(b, e)
                lst.append(e)
            return f

        ACT.wait(state.get("g16_free"))
        for cb in range(2):
            proj_unit(("mat", "in", 0, 8 + cb), lambda kc: XB[:, kc, 0:NT], 8, evac_gate(GB, gb_evs, 32), None,
                      oc0=4 * cb)
        for cb in range(2):
            e_xb_done = proj_unit(("mat", "in", 0, 6 + cb), lambda kc: XB[:, kc, 0:NT], 8,
                                  evac_gate(GA, ga_evs, 24), None, oc0=4 * cb)

        cvn_evs = []
        for c in range(8):
            tmp = TMPA if c % 2 == 0 else TMPB
            DVE.wait(state.get(f"tmpn_free{c % 2}"))
            e = DVE.mark(DVE.raw.tensor_tensor(out=tmp[:, 0:NT], in0=CV[:, c, 0:NT], in1=PS[:, bR, 0:NT], op=ALU.mult))
            DVE.wait(e)
            e = DVE.mark(DVE.raw.tensor_tensor(out=tmp[:, 0:NT], in0=tmp[:, 0:NT], in1=PS[:, bN, 0:NT], op=ALU.add))
            ACT.wait(e)
            e = ACT.mark(ACT.raw.activation(out=CVN[:, c, 0:NT], in_=tmp[:, 0:NT], func=AF.Silu,
                                            scale=V("conv_ln_g", c), bias=V("conv_ln_b", c)))
            state[f"tmpn_free{c % 2}"] = e
            cvn_evs.append(e)
        BK.release(bR, e)
        BK.release(bN, e)
        state["tmp_free"] = e

        chk("gates", ti)
        mt_evs = [None] * 8

        def evac_pb(oc, b, e_mm):
            DVE.wait(e_mm, gb_evs, e_xb_done)
            e = DVE.mark(DVE.raw.scalar_tensor_tensor(out=MT[:, oc, 0:NT], in0=PS[:, b, 0:NT],
                                                      scalar=V("b_b_out", oc), in1=GB[:, oc, :],
                                                      op0=ALU.add, op1=ALU.mult))
            BK.release(b, e)
            mt_evs[oc] = e

        for cb in range(2):
            e_cvn_done = proj_unit(("mat", "bout", 0, cb), lambda kc: CVN[:, kc, 0:NT], 8, evac_pb, cvn_evs, oc0=4 * cb)

        chk("wbout", ti)
        hb_ev = scan_done
        if NK - pk0 > 1:
            DVE.wait(scan_done)
            hb_ev = DVE.mark(DVE.raw.tensor_copy(out=HB[:, :, :, pk0 + 1:NK], in_=XH[:, :, :, pk0:NK - 1]))
        PE.wait(ev_cs, scan_done, hb_ev, pev["kb"])
        zt_evs = []
        ACT.wait(cvn_evs)
        for j in range(8):
            b = BK.alloc()
            fst = True
            utv = UT[:, j, 0:NT].rearrange("p (k s) -> p s k", s=TC)
            for off in range(TC):
                ns = TC - off
                K.MM(PS[:, b, off * NK:TC * NK], lhsT=KB[:, j, off, :], rhs=utv[:, 0:ns, :],
                     start=fst, stop=False, skip_group_check=True)
                fst = False
            for tp in range(TC):
                for qq in range(4):
                    q = 4 * j + qq
                    for ri in range(2):
                        last = K.MM(PS[32 * qq:32 * qq + 32, b, tp * NK:(tp + 1) * NK],
                                                lhsT=CSv[:, tp, q, ri, :], rhs=HB[:, q, ri, 0:NK], start=False,
                                                stop=(tp == TC - 1 and qq == 3 and ri == 1),
                                                skip_group_check=True, tile_position=(0, 32 * qq))
            e_mm = PE.mark(last)
            ACT.wait(e_mm)
            e = ACT.mark(ACT.raw.activation(
                out=ZT[:, j, 0:NT].rearrange("p (k t) -> p k t", t=TC),
                in_=PS[:, b, 0:NT].rearrange("p (t k) -> p k t", t=TC), func=AF.Gelu_apprx_tanh))
            BK.release(b, e)
            zt_evs.append(e)
        ssmw_free = e_mm
        if ti + 1 < len(tiles):
            issue_win(ssmw_free, False)
        state["hb_free"] = e_mm
        state["xh_free"] = [scan_done, hb_ev]
        ut_done = e_mm

        chk("ssm_out", ti)
        if isS:
            DVE.wait(scan_done, state.get("tmpn_free0"), state.get("tmpn_free1"))
            for ri in range(2):
                dst_s = dr["nre_s"] if ri == 0 else dr["nim_s"]
                dst_p = dr["nre_p"] if ri == 0 else dr["nim_p"]
                for g4 in range(5):
                    b = BK.alloc()
                    if g4 < 4:
                        e = DVE.mark(DVE.raw.tensor_copy(
                            out=TMPA[:, 0:128].rearrange("p (s q) -> p s q", q=32),
                            in_=XH[:, :, ri, 4 * g4:4 * g4 + 4].rearrange("p q s -> p s q")))
                        ncol = 128
                    else:
                        e = DVE.mark(DVE.raw.tensor_copy(out=TMPA[:, 0:32], in_=XH[:, :, ri, NK - 1]))
                        ncol = 32
                    PE.wait(e)
                    e_t = PE.mark(K.TR(PS[0:ncol, b, 0:128], TMPA[:, 0:ncol], IDF[:]))
                    osl = state.get("ost_i", 0) % 2
                    state["ost_i"] = state.get("ost_i", 0) + 1
                    DVE.wait(e_t, state.get(f"ost_free{osl}"))
                    e = DVE.mark(DVE.raw.tensor_copy(out=OST[0:ncol, osl, :], in_=PS[0:ncol, b, 0:128]))
                    BK.release(b, e)
                    SP.wait(e)
                    if g4 < 4:
                        ev_o = ost_sem[osl].add(nc.sync.dma_start(out=dst_s[128 * g4:128 * (g4 + 1), :], in_=OST[:, osl, :]))
                    else:
                        ev_o = ost_sem[osl].add(nc.sync.dma_start(out=dst_p, in_=OST[0:32, osl, :]))
                    state[f"ost_free{osl}"] = ev_o
            state["tmp_free2"] = e_t

        chk("stateout", ti)
        za_evs = [None] * 8

        def evac_glu(oc, b, e_mm):
            ACT.wait(e_mm, e_cvn_done)
            e = ACT.mark(ACT.raw.activation(out=ZSG[:, oc, 0:NT], in_=PS[:, b, 0:NT], func=AF.Sigmoid,
                                            bias=V("b_glu", oc), scale=1.0))
            BK.release(b, e)
            DVE.wait(e)
            e2 = DVE.mark(DVE.raw.tensor_tensor(out=ZSG[:, oc, 0:NT], in0=ZSG[:, oc, 0:NT], in1=ZT[:, oc, 0:NT],
                                                op=ALU.mult))
            za_evs[oc] = e2

        for cb in range(2):
            e_zt_done = proj_unit(("mat", "glu", 0, cb), lambda kc: ZT[:, kc, 0:NT], 8, evac_glu, zt_evs, oc0=4 * cb)

        chk("wglu", ti)
        mt2_evs = [None] * 8

        def evac_pa(oc, b, e_mm):
            tmp = TMPA if oc % 2 == 0 else TMPB
            DVE.wait(e_mm, ga_evs, mt_evs[oc], state.get("tmp_free2"))
            e = DVE.mark(DVE.raw.tensor_tensor(out=tmp[:, 0:NT], in0=PS[:, b, 0:NT], in1=GA[:, oc, :], op=ALU.mult))
            BK.release(b, e)
            DVE.wait(e)
            e2 = DVE.mark(DVE.raw.tensor_tensor(out=MT[:, oc, 0:NT], in0=tmp[:, 0:NT], in1=MT[:, oc, 0:NT],
                                                op=ALU.add))
            DVE.wait(e2)
            mt2_evs[oc] = e2

        for cb in range(2):
            e_za_done = proj_unit(("mat", "aout", 0, cb), lambda kc: ZSG[:, kc, 0:NT], 8, evac_pa, za_evs, oc0=4 * cb)
        state["b5_free"] = e_za_done

        chk("waout", ti)
        s1_evs = []

        def evac_o(oc, b, e_mm):
            DVE.wait(e_mm)
            e = DVE.mark(DVE.raw.tensor_tensor(out=R32[:, oc, 0:NT], in0=PS[:, b, 0:NT], in1=R32[:, oc, 0:NT],
                                               op=ALU.add))
            BK.release(b, e)
            s1_evs.append(e)

        for cb in range(2):
            e_mt_done = proj_unit(("mat", "o", 0, cb), lambda kc: MT[:, kc, 0:NT], 8, evac_o, mt2_evs, oc0=4 * cb)

        chk("wo", ti)
        def layer_norm_r32(ready, pre_free):
            ACT.wait(pre_free)
            evs = []
            for kc in range(8):
                ACT.wait(ready[kc])
                ACT.raw.activation(out=SBF[:, kc, 0:NT], in_=R32[:, kc, 0:NT], func=AF.Copy)
                evs.append(ACT.mark(ACT.raw.activation(out=SQ[:, kc, 0:NT], in_=R32[:, kc, 0:NT], func=AF.Square)))
            bR_, bN_, e_st_ = ln_stats(SBF, SQ, evs)
            nevs = []
            for kc in range(8):
                e = DVE.mark(DVE.raw.tensor_tensor(out=R32[:, kc, 0:NT], in0=R32[:, kc, 0:NT], in1=PS[:, bR_, 0:NT],
                                                   op=ALU.mult))
                DVE.wait(e)
                e = DVE.mark(DVE.raw.tensor_tensor(out=R32[:, kc, 0:NT], in0=R32[:, kc, 0:NT], in1=PS[:, bN_, 0:NT],
                                                   op=ALU.add))
                DVE.wait(e)
                nevs.append(e)
            BK.release(bR_, e)
            BK.release(bN_, e)
            state["tmp_free"] = e
            return nevs

        nevs = layer_norm_r32(s1_evs, [ut_done, e_zt_done])
        x1_evs = []
        for kc in range(8):
            ACT.wait(nevs[kc], e_mt_done)
            e = ACT.mark(ACT.raw.activation(out=X1B[:, kc, 0:NT], in_=R32[:, kc, 0:NT], func=AF.Identity,
                                            scale=V("ln1_g", kc), bias=V("ln1_b", kc)))
            ACT.wait(e)
            e = ACT.mark(ACT.raw.activation(out=R32[:, kc, 0:NT], in_=R32[:, kc, 0:NT], func=AF.Identity,
                                            scale=DERV[:, 2, kc:kc + 1], bias=DERV[:, 3, kc:kc + 1]))
            x1_evs.append(e)

        chk("ln1", ti)
        s2_evs = {}
        for hh in range(2):
            hd_evs = []

            def evac_ff1(oc, b, e_mm):
                ol = oc - 16 * hh
                ACT.wait(e_mm, state.get("hdn_free"), ga_evs, gb_evs)
                e = ACT.mark(ACT.raw.activation(out=HDN[:, ol, :], in_=PS[:, b, 0:NT], func=AF.Relu,
                                                bias=V("b_ff1", oc), scale=1.0))
                BK.release(b, e)
                POOL.wait(e)
                e2 = POOL.mark(POOL.raw.tensor_tensor(out=HDN[:, ol, :], in0=HDN[:, ol, :], in1=HDN[:, ol, :],
                                                      op=ALU.mult))
                hd_evs.append(e2)

            for cb in range(4):
                proj_unit(("mat", "ff1", 0, 4 * hh + cb), lambda kc: X1B[:, kc, 0:NT], 8, evac_ff1,
                          x1_evs + [e_za_done] if (hh == 0 and cb == 0) else None, oc0=16 * hh + 4 * cb)
            for cbo in range(2):
                banks = [BK.alloc() for _ in range(4)]
                for kgl in range(2):
                    def evac_ff2(oc, b, e_mm):
                        DVE.wait(e_mm, x1_evs)
                        e = DVE.mark(DVE.raw.tensor_tensor(out=R32[:, oc, 0:NT], in0=PS[:, b, 0:NT],
                                                           in1=R32[:, oc, 0:NT], op=ALU.add))
                        BK.release(b, e)
                        DVE.wait(e)
                        s2_evs[(hh, oc)] = e
                    e_h = proj_unit(("mat", "ff2", 2 * hh + kgl, cbo), lambda kc: HDN[:, kc, :], 8,
                                    evac_ff2 if kgl == 1 else None, hd_evs, oc0=4 * cbo, kc0=8 * kgl,
                                    bank_of=lambda o4: banks[o4], start0=(kgl == 0), stop_last=(kgl == 1))
            state["hdn_free"] = e_h
        state["g16_free"] = e_h

        chk("ffn", ti)
        nevs = layer_norm_r32([s2_evs[(1, kc)] for kc in range(8)], None)
        y_evs = []
        for kc in range(8):
            ACT.wait(nevs[kc])
            e = ACT.mark(ACT.raw.activation(out=R32[:, kc, 0:NT], in_=R32[:, kc, 0:NT], func=AF.Identity,
                                            scale=V("ln2_g", kc), bias=V("ln2_b", kc)))
            y_evs.append(e)
        state["b1_free"] = nevs[-1]
        state["b2_free"] = nevs[-1]
        nxt_i = 0
        nA = {}
        if ti + 1 < len(tiles):
            for m_ in range(min(2, len(subts[ti + 1]))):
                nA[m_] = entry_A(ti + 1, subts[ti + 1][m_])
        for st in subt:
            c0, R = st["c0"], st["R"]
            sl = state["ys_i"] % 2
            state["ys_i"] += 1
            PE.wait(y_evs)
            for half in range(2):
                b = BK.alloc()
                for k4 in range(4):
                    kc = half * 4 + k4
                    ins = K.TR(PS[0:R, b, k4 * 128:(k4 + 1) * 128], R32[:, kc, c0:c0 + R], IDF[:])
                e_t = PE.mark(ins)
                eng = ACT if half == 0 else DVE
                eng.wait(e_t, ys_free[sl])
                if eng is ACT:
                    e = ACT.mark(ACT.raw.activation(out=YS[0:R, sl, half * 512:(half + 1) * 512],
                                                    in_=PS[0:R, b, :], func=AF.Copy))
                else:
                    e = DVE.mark(DVE.raw.tensor_copy(out=YS[0:R, sl, half * 512:(half + 1) * 512],
                                                     in_=PS[0:R, b, :]))
                BK.release(b, e)
                SP.wait(e)
            for (dst, r0, r1) in st["dst"]:
                ev_o = ys_sem[sl].add(nc.sync.dma_start(out=dst, in_=YS[r0:r1, sl, :]))
            ys_free[sl] = ev_o
            if ti + 1 < len(tiles):
                done_cols = c0 + R
                while nxt_i < len(subts[ti + 1]) and subts[ti + 1][nxt_i]["c0"] + subts[ti + 1][nxt_i]["R"] <= done_cols:
                    entry_B(ti + 1, subts[ti + 1][nxt_i], nA[nxt_i], e_t)
                    if nxt_i + 2 < len(subts[ti + 1]):
                        nA[nxt_i + 2] = entry_A(ti + 1, subts[ti + 1][nxt_i + 2])
                    nxt_i += 1
        if ti + 1 < len(tiles):
            while nxt_i < len(subts[ti + 1]):
                entry_B(ti + 1, subts[ti + 1][nxt_i], nA[nxt_i], e_t)
                if nxt_i + 2 < len(subts[ti + 1]):
                    nA[nxt_i + 2] = entry_A(ti + 1, subts[ti + 1][nxt_i + 2])
                nxt_i += 1

def finish(K):
    h = K.handles
    SP = h["SP"]
    outs = h["outs"]
    for E_ in K.engs:
        if E_ is not SP and E_.cnt > 0:
            SP.raw.wait_ge(E_.sem, E_.cnt)
    for ds in K.dmasems:
        if ds.cnt > 0:
            SP.raw.wait_ge(ds.sem, ds.cnt)
    SP.raw.wait_ge(outs.sem, outs.cnt)
    K.close()
    return K.nc


def _pack_vecs(inp):
    cols = []
    for name, n in VEC_SPECS:
        a = np.asarray(inp[name], np.float32)
        if name == "conv_w":
            a = a.reshape(31, 8, 128).transpose(2, 0, 1).reshape(128, 31 * 8)
        else:
            a = a.reshape(n, 128).T
        cols.append(a)
    return np.ascontiguousarray(np.concatenate(cols, axis=1), dtype=np.float32)


def _sl(a):
    return np.asarray(a, np.float32).reshape(32, 2, 64).transpose(1, 2, 0).reshape(128, 32)


def host_prep(inp):
    f32 = np.float32
    sh = {}
    sh["meta"] = np.ascontiguousarray(inp["meta_tokens"], f32)
    sh["vecs"] = _pack_vecs(inp)
    ldt = np.asarray(inp["ssm_log_dt"], f32).reshape(32, 2)
    ldt_sl = np.broadcast_to(ldt.T[:, None, :], (2, 64, 32)).reshape(128, 32)
    sh["ssm_small"] = np.ascontiguousarray(
        np.concatenate([_sl(inp["ssm_a_re"][0]), _sl(inp["ssm_a_im"][0]), ldt_sl], axis=1), f32)
    for nm, key in (("bre", "ssm_b_re"), ("bim", "ssm_b_im")):
        a = np.asarray(inp[key][0], f32).reshape(32, 2, 64, 16).transpose(1, 2, 0, 3).reshape(128, 512)
        sh[nm] = np.ascontiguousarray(a)
    for nm, key in (("cre", "ssm_c_re"), ("cim", "ssm_c_im")):
        a = np.asarray(inp[key][0], f32).reshape(32, 2, 16, 64).transpose(1, 3, 0, 2).reshape(128, 512)
        sh[nm] = np.ascontiguousarray(a)
    for nm, key in (("w_in", "w_in"), ("w_glu", "w_glu"), ("w_a_out", "w_a_out"), ("w_b_out", "w_b_out"),
                    ("w_o", "w_o"), ("w_ff1", "w_ff1"), ("w_ff2", "w_ff2")):
        sh[nm] = np.ascontiguousarray(inp[key][0], f32)
    per = []
    for b in range(8):
        d = dict(sh)
        d["xp"] = np.ascontiguousarray(inp["x_prompt"][b], f32)
        d["xs"] = np.ascontiguousarray(inp["x_sample"][16 * b:16 * b + 16], f32).reshape(128, 1024)
        for nm, key in (("h0re", "state_ssm_re"), ("h0im", "state_ssm_im")):
            a = np.asarray(inp[key][0, 16 * b:16 * b + 16], f32)
            d[nm] = np.ascontiguousarray(a.reshape(16, 32, 2, 64).transpose(2, 3, 1, 0).reshape(128, 512))
        sc = np.asarray(inp["state_conv"][0, 16 * b:16 * b + 16], f32)
        d["sconv_nat"] = np.ascontiguousarray(sc.reshape(16 * 30, 1024))
        d["sconv_fm"] = np.ascontiguousarray(sc.reshape(16, 30, 8, 128).transpose(3, 2, 0, 1).reshape(128, 8 * 16 * 30))
        per.append(d)
    return per


_CACHE = {}


def kernel(**inputs):
    if "nc" not in _CACHE:
        K = build()
        main_loop(K)
        _CACHE["nc"] = finish(K)
    nc = _CACHE["nc"]
    per = host_prep(inputs)
    res = run_bass_kernel_spmd(nc, per, core_ids=list(range(8)))
    R = res.results
    f32 = np.float32
    y_prompt = np.stack([np.asarray(R[b]["yp"], f32) for b in range(8)], 0)
    y_sample = np.concatenate([np.asarray(R[b]["ys"], f32).reshape(16, 8, 1024) for b in range(8)], 0)
    nrp = np.stack([np.asarray(R[b]["nre_p"], f32).reshape(64, 64) for b in range(8)], 0)[None]
    nip = np.stack([np.asarray(R[b]["nim_p"], f32).reshape(64, 64) for b in range(8)], 0)[None]
    ncp = np.stack([np.asarray(R[b]["ncv_p"], f32) for b in range(8)], 0)[None]
    nrs = np.concatenate([np.asarray(R[b]["nre_s"], f32).reshape(16, 64, 64) for b in range(8)], 0)[None]
    nis = np.concatenate([np.asarray(R[b]["nim_s"], f32).reshape(16, 64, 64) for b in range(8)], 0)[None]
    ncs = np.concatenate([np.asarray(R[b]["ncv_s"], f32).reshape(16, 30, 1024) for b in range(8)], 0)[None]
    return (y_prompt, y_sample, nrp, nip, ncp, nrs, nis, ncs)
```
